# Optimizing a Trainium2 kernel written in Bass

```python
import jax, jax.numpy as jnp
from jax import lax
import numpy as np

D_MODEL = 1024
BATCH = 1
SEQ = 16384
DEPTH = 4

GRID_W = 64
CTX_LEN = 256
N_MIXERS = 4
CTX_READERS = (0, 2)
N_PER_MIXER = tuple((DEPTH - k + N_MIXERS - 1) // N_MIXERS for k in range(N_MIXERS))
N_MOD = 9
D_FF = 2816
EPS = 1e-6
Q_BLOCK = 128
MLA_HEADS = 8
MLA_Q_LORA = 384
MLA_KV_LORA = 256
MLA_NOPE = 128
MLA_ROPE = 64
MLA_V = 128
MLA_SCALE = (MLA_NOPE + MLA_ROPE) ** -0.5
ROPE_BASE = 10000.0
ROPE_PAIRS_PER_AXIS = MLA_ROPE // 4
POOL_WINDOWS = (2, 4, 8, 16)
POOL_GROUPS = 4
POOL_GROUP = D_MODEL // POOL_GROUPS
NA_HEADS = 16
NA_HEAD_DIM = D_MODEL // NA_HEADS
NA_ROWS = 8
NA_COLS = 16
NA_SCALE = NA_HEAD_DIM ** -0.5
CONV_WIDTH = 3

kernel_name = "hybrid_interleaved_diffusion_trunk"


def rms_norm(x):
    xf = x.astype(jnp.float32)
    return (xf * lax.rsqrt(jnp.mean(xf * xf, axis=-1, keepdims=True) + EPS)).astype(x.dtype)


def modulate(x, shift, scale):
    return rms_norm(x) * (1 + scale) + shift


def ada_params(cond, w, b):
    m = jax.nn.silu(cond) @ w + b
    return jnp.split(m[:, None, :], N_MOD, axis=-1)


def swiglu(h, w_in, w_out):
    g, u = jnp.split(h @ w_in, 2, axis=-1)
    return (jax.nn.silu(g) * u) @ w_out


def axial_rope_angles(T):
    t = jnp.arange(T)
    row = (t // GRID_W).astype(jnp.float32)
    col = (t % GRID_W).astype(jnp.float32)
    freqs = ROPE_BASE ** (-jnp.arange(ROPE_PAIRS_PER_AXIS, dtype=jnp.float32) / ROPE_PAIRS_PER_AXIS)
    return jnp.concatenate([row[:, None] * freqs, col[:, None] * freqs], axis=-1)


def rope_2d(x, ang):
    shape = (1, ang.shape[0]) + (1,) * (x.ndim - 3) + (ang.shape[1],)
    cos = jnp.cos(ang).reshape(shape)
    sin = jnp.sin(ang).reshape(shape)
    xp = x.astype(jnp.float32).reshape(x.shape[:-1] + (-1, 2))
    x0, x1 = xp[..., 0], xp[..., 1]
    out = jnp.stack([x0 * cos - x1 * sin, x0 * sin + x1 * cos], axis=-1)
    return out.reshape(x.shape).astype(x.dtype)


def mla_attention(qn, qr, kn, kr, v):
    B, T, H, _ = qn.shape
    blk = min(Q_BLOCK, T)
    nb = T // blk

    def to_blocks(a):
        return jnp.moveaxis(a.reshape((B, nb, blk) + a.shape[2:]), 1, 0)

    def one_block(qs):
        qn_b, qr_b = qs
        s = (jnp.einsum('bqhd,bkhd->bhqk', qn_b, kn) + jnp.einsum('bqhr,bkr->bhqk', qr_b, kr)) * MLA_SCALE
        p = jax.nn.softmax(s.astype(jnp.float32), axis=-1).astype(v.dtype)
        return jnp.einsum('bhqk,bkhd->bqhd', p, v)

    o = lax.map(one_block, (to_blocks(qn), to_blocks(qr)))
    return jnp.moveaxis(o, 0, 1).reshape(B, T, H * MLA_V)


def mla_mixer(hx, hc, ang, ctx_out, w_dq, g_dq, w_uq, w_dkv, g_dkv, w_uk, w_uv, g_qn, g_qr, g_kn, g_kr, w_o):
    def queries(h):
        cq = rms_norm(h @ w_dq) * g_dq
        q = jnp.einsum('btr,rhd->bthd', cq, w_uq)
        return rms_norm(q[..., :MLA_NOPE]) * g_qn, rms_norm(q[..., MLA_NOPE:]) * g_qr

    def keys_values(h):
        kv = h @ w_dkv
        ckv = rms_norm(kv[..., :MLA_KV_LORA]) * g_dkv
        kn = rms_norm(jnp.einsum('btr,rhd->bthd', ckv, w_uk)) * g_kn
        kr = rms_norm(kv[..., MLA_KV_LORA:]) * g_kr
        v = jnp.einsum('btr,rhd->bthd', ckv, w_uv)
        return kn, kr, v

    qn, qr = queries(hx)
    qr = rope_2d(qr, ang)
    kn, kr, v = keys_values(hx)
    kr = rope_2d(kr, ang)
    kn_c, kr_c, v_c = keys_values(hc)
    o = mla_attention(qn, qr,
                      jnp.concatenate([kn_c, kn], axis=1),
                      jnp.concatenate([kr_c, kr], axis=1),
                      jnp.concatenate([v_c, v], axis=1))
    yx = o @ w_o
    yc = None
    if ctx_out:
        qn_c, qr_c = queries(hc)
        yc = mla_attention(qn_c, qr_c, kn_c, kr_c, v_c) @ w_o
    return yx, yc


def pool_mixer(h, w_pool, scale):
    B, T, D = h.shape
    hg = h.reshape(B, T, POOL_GROUPS, POOL_GROUP)
    hf = hg.astype(jnp.float32)
    cs = jnp.concatenate([jnp.zeros((B, 1, POOL_GROUPS, POOL_GROUP), jnp.float32), jnp.cumsum(hf, axis=1)], axis=1)
    t = jnp.arange(T)[:, None]
    half = jnp.array(POOL_WINDOWS, dtype=jnp.int32)[None, :] // 2
    lo = jnp.clip(t - half, 0, T)
    hi = jnp.clip(t + half, 0, T)
    g_idx = jnp.arange(POOL_GROUPS)
    win_sum = cs[:, hi, g_idx] - cs[:, lo, g_idx]
    mean = win_sum / (hi - lo).astype(jnp.float32)[None, :, :, None]
    y = (mean - hf).astype(h.dtype)
    return jnp.einsum('btgc,gcd->btgd', y, w_pool).reshape(B, T, D) * scale


def dense_attention(q, k, v, scale):
    s = jnp.einsum('bqhd,bkhd->bhqk', q, k) * scale
    p = jax.nn.softmax(s.astype(jnp.float32), axis=-1).astype(v.dtype)
    return jnp.einsum('bhqk,bkhd->bqhd', p, v)


def na_mixer(hx, hc, ctx_out, w_qkv, g_q, g_k, rpb, w_o):
    B, S, D = hx.shape
    rows = S // GRID_W
    kr_win = min(NA_ROWS, rows)
    w = w_qkv.reshape(D, 3, NA_HEADS, NA_HEAD_DIM)
    pl = jnp.einsum('btd,dnhe->btnhe', hx, w)
    q = rms_norm(pl[:, :, 0]) * g_q
    k = rms_norm(pl[:, :, 1]) * g_k
    v = pl[:, :, 2]
    pc = jnp.einsum('btd,dnhe->btnhe', hc, w[:, 1:])
    k_c = rms_norm(pc[:, :, 0]) * g_k
    v_c = pc[:, :, 1]

    qg = q.reshape(B, rows, GRID_W, NA_HEADS, NA_HEAD_DIM)
    kg = k.reshape(B, rows, GRID_W, NA_HEADS, NA_HEAD_DIM)
    vg = v.reshape(B, rows, GRID_W, NA_HEADS, NA_HEAD_DIM)
    cols = jnp.arange(GRID_W)
    col_start = jnp.clip(cols - NA_COLS // 2, 0, GRID_W - NA_COLS)
    col_idx = col_start[:, None] + jnp.arange(NA_COLS)[None, :]
    dc_idx = col_idx - cols[:, None] + (NA_COLS - 1)
    n_loc = kr_win * NA_COLS

    def one_row(r):
        rs = jnp.clip(r - kr_win // 2, 0, rows - kr_win)
        q_r = lax.dynamic_index_in_dim(qg, r, axis=1, keepdims=False)
        k_sel = lax.dynamic_slice_in_dim(kg, rs, kr_win, axis=1)[:, :, col_idx]
        v_sel = lax.dynamic_slice_in_dim(vg, rs, kr_win, axis=1)[:, :, col_idx]
        dr_idx = rs + jnp.arange(kr_win) - r + (NA_ROWS - 1)
        bias = rpb[:, dr_idx[:, None, None], dc_idx[None, :, :]].transpose(0, 2, 1, 3)
        s_loc = jnp.einsum('bqhd,biqjhd->bhqij', q_r, k_sel) * NA_SCALE + bias
        s_ctx = jnp.einsum('bqhd,bkhd->bhqk', q_r, k_c) * NA_SCALE
        s = jnp.concatenate([s_loc.reshape(B, NA_HEADS, GRID_W, n_loc), s_ctx], axis=-1)
        p = jax.nn.softmax(s.astype(jnp.float32), axis=-1).astype(v.dtype)
        p_loc = p[..., :n_loc].reshape(B, NA_HEADS, GRID_W, kr_win, NA_COLS)
        return (jnp.einsum('bhqij,biqjhd->bqhd', p_loc, v_sel)
                + jnp.einsum('bhqk,bkhd->bqhd', p[..., n_loc:], v_c))

    o = lax.map(one_row, jnp.arange(rows))
    yx = jnp.moveaxis(o, 0, 1).reshape(B, S, D) @ w_o
    yc = None
    if ctx_out:
        q_c = rms_norm(jnp.einsum('btd,dhe->bthe', hc, w[:, 0])) * g_q
        yc = dense_attention(q_c, k_c, v_c, NA_SCALE).reshape(hc.shape) @ w_o
    return yx, yc


def conv_mixer(h, w_in, w_conv, w_out):
    b_gate, c_gate, u = jnp.split(h @ w_in, 3, axis=-1)
    z = lax.conv_general_dilated(c_gate * u, w_conv[:, None, :], window_strides=(1,),
                                 padding=((CONV_WIDTH // 2, CONV_WIDTH // 2),),
                                 dimension_numbers=('NWC', 'WIO', 'NWC'),
                                 feature_group_count=h.shape[-1])
    return (b_gate * z) @ w_out


def setup_inputs(seed: int = 0) -> dict:
    key = jax.random.key(seed)
    ks = iter(jax.random.split(key, 40))
    D = D_MODEL
    nA, nB, nC, nD = N_PER_MIXER

    def nrm(shape, scale):
        return scale * jax.random.normal(next(ks), shape, jnp.float32)

    def gain(shape):
        return 1.0 + 0.02 * jax.random.normal(next(ks), shape, jnp.float32)

    return {
        "x": nrm((BATCH, SEQ, D), 1.0),
        "c": nrm((BATCH, D), 1.0),
        "ctx": nrm((BATCH, CTX_LEN, D), 1.0),
        "c_ctx": nrm((D,), 1.0),
        "mod_w": nrm((DEPTH, D, N_MOD * D), 0.5 * D ** -0.5),
        "mod_b": nrm((DEPTH, N_MOD * D), 0.01),
        "ffn_w_in": nrm((DEPTH, 2, D, 2 * D_FF), D ** -0.5),
        "ffn_w_out": nrm((DEPTH, 2, D_FF, D), D_FF ** -0.5),
        "mla_w_dq": nrm((nA, D, MLA_Q_LORA), D ** -0.5),
        "mla_g_dq": gain((nA, MLA_Q_LORA)),
        "mla_w_uq": nrm((nA, MLA_Q_LORA, MLA_HEADS, MLA_NOPE + MLA_ROPE), MLA_Q_LORA ** -0.5),
        "mla_w_dkv": nrm((nA, D, MLA_KV_LORA + MLA_ROPE), D ** -0.5),
        "mla_g_dkv": gain((nA, MLA_KV_LORA)),
        "mla_w_uk": nrm((nA, MLA_KV_LORA, MLA_HEADS, MLA_NOPE), MLA_KV_LORA ** -0.5),
        "mla_w_uv": nrm((nA, MLA_KV_LORA, MLA_HEADS, MLA_V), MLA_KV_LORA ** -0.5),
        "mla_g_qn": gain((nA, MLA_NOPE)),
        "mla_g_qr": gain((nA, MLA_ROPE)),
        "mla_g_kn": gain((nA, MLA_NOPE)),
        "mla_g_kr": gain((nA, MLA_ROPE)),
        "mla_w_o": nrm((nA, MLA_HEADS * MLA_V, D), (MLA_HEADS * MLA_V) ** -0.5),
        "pool_w": nrm((nB, POOL_GROUPS, POOL_GROUP, POOL_GROUP), POOL_GROUP ** -0.5),
        "pool_scale": 1.0 + 0.1 * jax.random.normal(next(ks), (nB, D), jnp.float32),
        "na_w_qkv": nrm((nC, D, 3 * NA_HEADS * NA_HEAD_DIM), D ** -0.5),
        "na_g_q": gain((nC, NA_HEAD_DIM)),
        "na_g_k": gain((nC, NA_HEAD_DIM)),
        "na_rpb": nrm((nC, NA_HEADS, 2 * NA_ROWS - 1, 2 * NA_COLS - 1), 0.1),
        "na_w_o": nrm((nC, NA_HEADS * NA_HEAD_DIM, D), (NA_HEADS * NA_HEAD_DIM) ** -0.5),
        "conv_w_in": nrm((nD, D, 3 * D), D ** -0.5),
        "conv_w": nrm((nD, CONV_WIDTH, D), CONV_WIDTH ** -0.5),
        "conv_w_out": nrm((nD, D, D), D ** -0.5),
    }


def reference(x, c, ctx, c_ctx, mod_w, mod_b, ffn_w_in, ffn_w_out,
              mla_w_dq, mla_g_dq, mla_w_uq, mla_w_dkv, mla_g_dkv, mla_w_uk, mla_w_uv,
              mla_g_qn, mla_g_qr, mla_g_kn, mla_g_kr, mla_w_o,
              pool_w, pool_scale,
              na_w_qkv, na_g_q, na_g_k, na_rpb, na_w_o,
              conv_w_in, conv_w, conv_w_out):
    B, S, _ = x.shape
    ang = axial_rope_angles(S)
    h_ctx = ctx
    for i in range(DEPTH):
        kind = i % N_MIXERS
        j = i // N_MIXERS
        reads_ctx = kind in CTX_READERS
        ctx_after = any((l % N_MIXERS) in CTX_READERS for l in range(i + 1, DEPTH))
        ctx_live = reads_ctx or ctx_after
        mx = ada_params(c, mod_w[i], mod_b[i])
        x = x + 0.5 * mx[2] * swiglu(modulate(x, mx[0], mx[1]), ffn_w_in[i, 0], ffn_w_out[i, 0])
        hx = modulate(x, mx[3], mx[4])
        hc = None
        if ctx_live:
            mc = ada_params(c_ctx[None, :], mod_w[i], mod_b[i])
            h_ctx = h_ctx + 0.5 * mc[2] * swiglu(modulate(h_ctx, mc[0], mc[1]), ffn_w_in[i, 0], ffn_w_out[i, 0])
            hc = modulate(h_ctx, mc[3], mc[4])
        if kind == 0:
            yx, yc = mla_mixer(hx, hc, ang, ctx_after, mla_w_dq[j], mla_g_dq[j], mla_w_uq[j], mla_w_dkv[j],
                               mla_g_dkv[j], mla_w_uk[j], mla_w_uv[j], mla_g_qn[j], mla_g_qr[j],
                               mla_g_kn[j], mla_g_kr[j], mla_w_o[j])
        elif kind == 1:
            yx = pool_mixer(hx, pool_w[j], pool_scale[j])
            yc = pool_mixer(hc, pool_w[j], pool_scale[j]) if ctx_after else None
        elif kind == 2:
            yx, yc = na_mixer(hx, hc, ctx_after, na_w_qkv[j], na_g_q[j], na_g_k[j], na_rpb[j], na_w_o[j])
        else:
            yx = conv_mixer(hx, conv_w_in[j], conv_w[j], conv_w_out[j])
            yc = conv_mixer(hc, conv_w_in[j], conv_w[j], conv_w_out[j]) if ctx_after else None
        x = x + mx[5] * yx
        x = x + 0.5 * mx[8] * swiglu(modulate(x, mx[6], mx[7]), ffn_w_in[i, 1], ffn_w_out[i, 1])
        if ctx_after:
            h_ctx = h_ctx + mc[5] * yc
            h_ctx = h_ctx + 0.5 * mc[8] * swiglu(modulate(h_ctx, mc[6], mc[7]), ffn_w_in[i, 1], ffn_w_out[i, 1])
    return x
```

```python
import numpy as np
from contextlib import ExitStack
import concourse.bass as bass
import concourse.mybir as mybir

F32 = mybir.dt.float32
BF16 = mybir.dt.bfloat16
I32 = mybir.dt.int32
AF = mybir.ActivationFunctionType
ALU = mybir.AluOpType


class Tile:
    __slots__ = ("t", "name", "acc")

    def __init__(self, t, name):
        self.t = t
        self.name = name
        self.acc = {}

    def __getitem__(self, idx):
        return self.t[idx]


class Op:
    __slots__ = ("eng", "fn", "waits", "sem", "val", "isdma")


class Prog:
    ENGS = ("pe", "act", "dve", "pool", "sp")

    def __init__(self):
        self.nc = bass.Bass("TRN2", target_bir_lowering=False)
        self.es = ExitStack()
        self.ops = {e: [] for e in self.ENGS}
        self.esem = {}
        self.ecount = {e: 0 for e in self.ENGS}
        self.waited = {e: {} for e in self.ENGS}
        self.dsem = {}
        self.dcount = {}
        self.nsem = 0
        for e in self.ENGS:
            self.esem[e] = self.es.enter_context(self.nc.semaphore("se_" + e))
            self.nsem += 1
        self.final = []

    def sb(self, name, shape, dt):
        return Tile(self.es.enter_context(self.nc.sbuf_tensor(name, list(shape), dt)), name)

    def ps(self, name, shape, dt=F32):
        return Tile(self.es.enter_context(self.nc.psum_tensor(name, list(shape), dt)), name)

    def dram(self, name, shape, dt, kind):
        return self.nc.dram_tensor(name, list(shape), dt, kind=kind).ap()

    def _deps(self, reads, writes):
        deps = []
        for (tl, rg) in reads:
            for k, ent in tl.acc.items():
                if rg is None or k is None or k == rg:
                    if ent[0] is not None:
                        deps.append(ent[0])
        for (tl, rg) in writes:
            for k, ent in tl.acc.items():
                if rg is None or k is None or k == rg:
                    if ent[0] is not None:
                        deps.append(ent[0])
                    deps.extend(ent[1])
        return deps

    def _commit(self, op, reads, writes):
        for (tl, rg) in reads:
            ent = tl.acc.setdefault(rg, [None, []])
            ent[1].append(op)
        for (tl, rg) in writes:
            if rg is None:
                tl.acc.clear()
            tl.acc[rg] = [op, []]

    def add(self, eng, fn, r=(), w=(), dma_key=None):
        op = Op()
        op.eng = eng
        op.fn = fn
        op.isdma = dma_key is not None
        deps = self._deps(r, w)
        waits = {}
        wd = self.waited[eng]
        for d in deps:
            if d.eng == "pe" and eng == "pe" and not d.isdma:
                continue
            if wd.get(d.sem, 0) >= d.val:
                continue
            if waits.get(d.sem, (None, 0))[1] < d.val:
                waits[d.sem] = (d.sem, d.val)
        for s, v in waits.values():
            wd[s] = v
        op.waits = list(waits.values())
        if op.isdma:
            if dma_key not in self.dsem:
                self.dsem[dma_key] = self.es.enter_context(self.nc.semaphore("sd%d" % len(self.dsem)))
                self.dcount[dma_key] = 0
                self.nsem += 1
            self.dcount[dma_key] += 16
            op.sem = self.dsem[dma_key]
            op.val = self.dcount[dma_key]
        else:
            self.ecount[eng] += 1
            op.sem = self.esem[eng]
            op.val = self.ecount[eng]
        self._commit(op, r, w)
        self.ops[eng].append(op)
        return op

    def dma(self, eng, out, in_, r=(), w=(), key=None, final=False, **kw):
        op = self.add(eng, lambda e: e.dma_start(out=out, in_=in_, **kw), r=r, w=w, dma_key=key)
        if final:
            self.final.append(op)
        return op

    def emit(self):
        nc = self.nc
        with nc.Block() as block:
            def run(e, name):
                for op in self.ops[name]:
                    for s, v in op.waits:
                        e.wait_ge(s, v)
                    ins = op.fn(e)
                    ins.then_inc(op.sem, 16 if op.isdma else 1)
                if name == "sp":
                    for op in self.final:
                        e.wait_ge(op.sem, op.val)

            @block.tensor
            def _(e):
                run(e, "pe")

            @block.scalar
            def _(e):
                run(e, "act")

            @block.vector
            def _(e):
                run(e, "dve")

            @block.gpsimd
            def _(e):
                run(e, "pool")

            @block.sync
            def _(e):
                run(e, "sp")
        self.es.close()
        return nc


D = 1024
DFF = 2816
NJ = DFF // 128
EPS = 1e-6

_cache = {}
_DBG = None


def _dbg(name, val):
    if _DBG is not None:
        _DBG[name] = val


def _mk_consts(P):
    c = {}
    c["onesD"] = P.sb("onesD", [128, 128], BF16)
    c["eps"] = P.sb("epsc", [128, 1], F32)
    P.add("dve", lambda e: e.memset(c["onesD"][:], 1.0 / D), w=[(c["onesD"], None)])
    P.add("dve", lambda e: e.memset(c["eps"][:], EPS), w=[(c["eps"], None)])
    return c


def build_mod():
    P = Prog()
    cT = P.dram("cT", [128, 8, 2], F32, "ExternalInput")
    mw = P.dram("mw", [4, D, 1152], F32, "ExternalInput")
    mb = P.dram("mb", [128, 4, 9], F32, "ExternalInput")
    out = P.dram("out", [128, 4, 9, 2], F32, "ExternalOutput")
    C = P.sb("C", [128, 8, 2], F32)
    SC = P.sb("SCs", [128, 8, 2], F32)
    MB = P.sb("MB", [128, 4, 9], F32)
    W = [P.sb("W%d" % i, [128, 8, 1152], F32) for i in range(2)]
    O = P.sb("O", [128, 4, 9, 2], F32)
    PS = [P.ps("ps%d" % i, [128, 2]) for i in range(4)]
    P.dma("sp", C[:], cT, w=[(C, None)], key="c")
    P.dma("sp", MB[:], mb, w=[(MB, None)], key="mb")
    P.add("act", lambda e: e.activation(out=SC[:], in_=C[:], func=AF.Silu), r=[(C, None)], w=[(SC, None)])
    n = 0
    for l in range(4):
        Wl = W[l % 2]
        P.dma("sp", Wl[:], mw[l].rearrange("(k p) c -> p k c", p=128), w=[(Wl, None)], key=("w", l % 2))
        for j in range(9):
            pt = PS[n % 4]
            n += 1
            for k in range(8):
                P.add("pe", lambda e, pt=pt, Wl=Wl, k=k, j=j: e.matmul(
                    pt[:], lhsT=Wl[:, k, j * 128:(j + 1) * 128], rhs=SC[:, k, :], start=(k == 0), stop=(k == 7)),
                    r=[(Wl, None), (SC, None)], w=[(pt, None)])
            P.add("dve", lambda e, pt=pt, l=l, j=j: e.tensor_scalar(
                out=O[:, l, j, :], in0=pt[:], scalar1=MB[:, l, j:j + 1], scalar2=None, op0=ALU.add),
                r=[(pt, None), (MB, None)], w=[(O, (l, j))])
    P.dma("sp", out, O[:], r=[(O, None)], key="o", final=True)
    return P.emit()


def build_tl(has_pre, has_post, halves):
    P = Prog()
    N = max(o + w for hv in halves for (o, w, _) in hv)
    HW = max(sum(w for (_, w, _) in hv) for hv in halves)
    xT = P.dram("xT", [D, N], F32, "ExternalInput")
    mod = P.dram("mod", [128, 2, 18, 8], F32, "ExternalInput")
    xo = P.dram("xo", [D, N], F32, "ExternalOutput")
    if has_pre:
        mT = P.dram("mT", [D, N], F32, "ExternalInput")
        wo = P.dram("wo", [D, D], F32, "ExternalInput")
        psd = P.dram("ps", [128, 8], F32, "ExternalInput")
        w_in2 = P.dram("w_in2", [D, 2 * DFF], F32, "ExternalInput")
        w_out2 = P.dram("w_out2", [DFF, D], F32, "ExternalInput")
    if has_post:
        w_in1 = P.dram("w_in1", [D, 2 * DFF], F32, "ExternalInput")
        w_out1 = P.dram("w_out1", [DFF, D], F32, "ExternalInput")
        ho = P.dram("ho", [D, N], F32, "ExternalOutput")

    cst = _mk_consts(P)
    X = P.sb("X", [128, 8, HW], F32)
    H = P.sb("H", [128, 8, HW], BF16)
    A = P.sb("A", [128, NJ, HW], BF16)
    WST = P.sb("WST", [128, 2, 8, 256], F32)
    WIN = P.sb("WIN", [128, 2, 8, 256], BF16)
    WOST = P.sb("WOST", [128, 2, 1024], F32)
    WOUT = P.sb("WOUT", [128, NJ, 1024], BF16)
    SQ = P.sb("SQ", [128, 8, 512], BF16)
    RS = P.sb("RS", [128, 2, 512], F32)
    TMP = P.sb("TMP", [128, 2, 512], F32)
    SG = P.sb("SG", [128, 2, 512], F32)
    HOF = P.sb("HOF", [128, 2, 512], F32)
    MOD = P.sb("MOD", [128, 2, 18, 8], F32)
    OPS = P.sb("OPS", [128, 2, 18, 8], F32)
    HG = P.sb("HG", [128, 2, 18, 8], F32)
    GEFF = P.sb("GEFF", [128, 2, 8], F32)
    PSV = P.sb("PSV", [128, 8], F32)
    PG = [P.ps("pg%d" % i, [128, 512]) for i in range(2)]
    PU = [P.ps("pu%d" % i, [128, 512]) for i in range(2)]
    PY = [P.ps("py%d" % i, [128, 512]) for i in range(2)]
    PM = [P.ps("pm%d" % i, [128, 512]) for i in range(2)]

    P.dma("sp", MOD[:], mod, w=[(MOD, None)], key="mod")
    P.add("dve", lambda e: e.tensor_scalar(out=OPS[:], in0=MOD[:], scalar1=1.0, scalar2=None, op0=ALU.add),
          r=[(MOD, None)], w=[(OPS, None)])
    P.add("dve", lambda e: e.tensor_scalar(out=HG[:], in0=MOD[:], scalar1=0.5, scalar2=None, op0=ALU.mult),
          r=[(MOD, None)], w=[(HG, None)])
    if has_pre:
        P.dma("sp", PSV[:], psd, w=[(PSV, None)], key="psv")
        for ci in range(2):
            P.add("dve", lambda e, ci=ci: e.tensor_tensor(out=GEFF[:, ci, :], in0=MOD[:, ci, 5, :], in1=PSV[:], op=ALU.mult),
                  r=[(MOD, None), (PSV, None)], w=[(GEFF, ci)])

    cnt = {"n": 0, "pm": 0, "pg": 0, "py": 0}

    def modulate(hv, v_shift, v_scale, to_out):
        lo = 0
        for (off, wd, ci) in hv:
            sl = slice(lo, lo + wd)
            pm = PM[cnt["pm"] % 2]
            rs = cnt["pm"] % 2
            cnt["pm"] += 1
            for k in range(8):
                P.add("act", lambda e, k=k, sl=sl, wd=wd: e.activation(out=SQ[:, k, 0:wd], in_=X[:, k, sl], func=AF.Square),
                      r=[(X, (k, lo))], w=[(SQ, k)])
                P.add("pe", lambda e, k=k, wd=wd, pm=pm: e.matmul(pm[:, 0:wd], lhsT=cst["onesD"][:], rhs=SQ[:, k, 0:wd],
                                                                 start=(k == 0), stop=(k == 7)),
                      r=[(SQ, k), (cst["onesD"], None)], w=[(pm, None)])
            P.add("act", lambda e, wd=wd, pm=pm, rs=rs: e.activation(out=RS[:, rs, 0:wd], in_=pm[:, 0:wd], func=AF.Sqrt,
                                                                     bias=cst["eps"][:], scale=1.0),
                  r=[(pm, None), (cst["eps"], None)], w=[(RS, rs)])
            P.add("dve", lambda e, wd=wd, rs=rs: e.reciprocal(out=RS[:, rs, 0:wd], in_=RS[:, rs, 0:wd]),
                  r=[(RS, rs)], w=[(RS, rs)])
            for k in range(8):
                t = cnt["n"] % 2
                cnt["n"] += 1
                P.add("dve", lambda e, k=k, sl=sl, wd=wd, t=t, ci=ci, rs=rs: e.scalar_tensor_tensor(
                    out=TMP[:, t, 0:wd], in0=X[:, k, sl], scalar=OPS[:, ci, v_scale, k:k + 1], in1=RS[:, rs, 0:wd],
                    op0=ALU.mult, op1=ALU.mult),
                    r=[(X, (k, lo)), (OPS, None), (RS, rs)], w=[(TMP, t)])
                P.add("act", lambda e, k=k, sl=sl, wd=wd, t=t, ci=ci: e.activation(
                    out=H[:, k, sl], in_=TMP[:, t, 0:wd], func=AF.Identity, bias=MOD[:, ci, v_shift, k:k + 1], scale=1.0),
                    r=[(TMP, t), (MOD, None)], w=[(H, (k, lo))])
                if to_out:
                    P.add("dve", lambda e, k=k, wd=wd, t=t, ci=ci: e.tensor_scalar(
                        out=HOF[:, t, 0:wd], in0=TMP[:, t, 0:wd], scalar1=MOD[:, ci, v_shift, k:k + 1], scalar2=None, op0=ALU.add),
                        r=[(TMP, t), (MOD, None)], w=[(HOF, t)])
                    P.dma("sp", ho[k * 128:(k + 1) * 128, off:off + wd], HOF[:, t, 0:wd], r=[(HOF, t)],
                          key=("hof", t), final=True)
            lo += wd

    def ffn(hv, w_in, w_out, v_shift, v_scale, v_gate):
        modulate(hv, v_shift, v_scale, False)

        def load_in(j):
            s = j % 2
            o1 = P.dma("sp", WST[:, s, :, 0:128], w_in[:, j * 128:(j + 1) * 128].rearrange("(k p) c -> p k c", p=128),
                       w=[(WST, s)], key=("wst", s))
            P.dma("sp", WST[:, s, :, 128:256],
                  w_in[:, DFF + j * 128:DFF + (j + 1) * 128].rearrange("(k p) c -> p k c", p=128),
                  w=[(WST, (s, 1))], key=("wst", s))
            P.add("pool", lambda e, s=s: e.tensor_copy(out=WIN[:, s], in_=WST[:, s]), r=[(WST, None)], w=[(WIN, s)])
            s2 = j % 2
            P.dma("sp", WOST[:, s2, :], w_out[j * 128:(j + 1) * 128, :], w=[(WOST, s2)], key=("wost", s2))
            P.add("pool", lambda e, s2=s2, j=j: e.tensor_copy(out=WOUT[:, j, :], in_=WOST[:, s2, :]),
                  r=[(WOST, s2)], w=[(WOUT, j)])

        load_in(0)
        for j in range(NJ):
            if j + 1 < NJ:
                load_in(j + 1)
            s = j % 2
            lo = 0
            for (off, wd, ci) in hv:
                sl = slice(lo, lo + wd)
                b = cnt["pg"] % 2
                cnt["pg"] += 1
                for k in range(8):
                    P.add("pe", lambda e, k=k, s=s, sl=sl, wd=wd, b=b: e.matmul(
                        PG[b][:, 0:wd], lhsT=WIN[:, s, k, 0:128], rhs=H[:, k, sl], start=(k == 0), stop=(k == 7)),
                        r=[(WIN, s), (H, (k, lo))], w=[(PG[b], None)])
                for k in range(8):
                    P.add("pe", lambda e, k=k, s=s, sl=sl, wd=wd, b=b: e.matmul(
                        PU[b][:, 0:wd], lhsT=WIN[:, s, k, 128:256], rhs=H[:, k, sl], start=(k == 0), stop=(k == 7)),
                        r=[(WIN, s), (H, (k, lo))], w=[(PU[b], None)])
                P.add("act", lambda e, wd=wd, b=b: e.activation(out=SG[:, b, 0:wd], in_=PG[b][:, 0:wd], func=AF.Silu),
                      r=[(PG[b], None)], w=[(SG, b)])
                P.add("dve", lambda e, wd=wd, b=b, j=j, sl=sl: e.tensor_tensor(
                    out=A[:, j, sl], in0=SG[:, b, 0:wd], in1=PU[b][:, 0:wd], op=ALU.mult),
                    r=[(SG, b), (PU[b], None)], w=[(A, (j, lo))])
                lo += wd
        for mc in range(8):
            lo = 0
            for (off, wd, ci) in hv:
                sl = slice(lo, lo + wd)
                b = cnt["py"] % 2
                cnt["py"] += 1
                for j in range(NJ):
                    P.add("pe", lambda e, j=j, mc=mc, sl=sl, wd=wd, b=b: e.matmul(
                        PY[b][:, 0:wd], lhsT=WOUT[:, j, mc * 128:(mc + 1) * 128], rhs=A[:, j, sl],
                        start=(j == 0), stop=(j == NJ - 1)),
                        r=[(WOUT, j), (A, (j, lo))], w=[(PY[b], None)])
                P.add("dve", lambda e, mc=mc, sl=sl, wd=wd, b=b, ci=ci: e.scalar_tensor_tensor(
                    out=X[:, mc, sl], in0=PY[b][:, 0:wd], scalar=HG[:, ci, v_gate, mc:mc + 1], in1=X[:, mc, sl],
                    op0=ALU.mult, op1=ALU.add),
                    r=[(PY[b], None), (HG, None), (X, (mc, lo))], w=[(X, (mc, lo))])
                lo += wd

    for hi, hv in enumerate(halves):
        h0 = hv[0][0]
        hw = sum(w for (_, w, _) in hv)
        P.dma("sp", X[:, :, 0:hw], xT[:, h0:h0 + hw].rearrange("(k p) n -> p k n", p=128), w=[(X, None)], key="x")
        if has_pre:
            lo = 0
            for (off, wd, ci) in hv:
                for q in range(0, wd, 256):
                    qw = min(256, wd - q)
                    s = (q // 256) % 2
                    P.dma("sp", WST[:, s, :, 0:qw], mT[:, off + q:off + q + qw].rearrange("(k p) n -> p k n", p=128),
                          w=[(WST, None)], key=("wst", s))
                    P.add("pool", lambda e, s=s, qw=qw, a=lo + q: e.tensor_copy(out=H[:, :, a:a + qw], in_=WST[:, s, :, 0:qw]),
                          r=[(WST, None)], w=[(H, None)])
                lo += wd
            for k in range(8):
                s = k % 2
                P.dma("sp", WOST[:, s, :], wo[k * 128:(k + 1) * 128, :], w=[(WOST, s)], key=("wost", s))
                P.add("pool", lambda e, s=s, k=k: e.tensor_copy(out=WOUT[:, k, :], in_=WOST[:, s, :]),
                      r=[(WOST, s)], w=[(WOUT, k)])
            for mc in range(8):
                lo = 0
                for (off, wd, ci) in hv:
                    sl = slice(lo, lo + wd)
                    b = cnt["py"] % 2
                    cnt["py"] += 1
                    for k in range(8):
                        P.add("pe", lambda e, k=k, mc=mc, sl=sl, wd=wd, b=b: e.matmul(
                            PY[b][:, 0:wd], lhsT=WOUT[:, k, mc * 128:(mc + 1) * 128], rhs=H[:, k, sl],
                            start=(k == 0), stop=(k == 7)),
                            r=[(WOUT, k), (H, None)], w=[(PY[b], None)])
                    P.add("dve", lambda e, mc=mc, sl=sl, wd=wd, b=b, ci=ci: e.scalar_tensor_tensor(
                        out=X[:, mc, sl], in0=PY[b][:, 0:wd], scalar=GEFF[:, ci, mc:mc + 1], in1=X[:, mc, sl],
                        op0=ALU.mult, op1=ALU.add),
                        r=[(PY[b], None), (GEFF, None), (X, (mc, lo))], w=[(X, (mc, lo))])
                    lo += wd
            ffn(hv, w_in2, w_out2, 6, 7, 8)
        if has_post:
            ffn(hv, w_in1, w_out1, 9 + 0, 9 + 1, 9 + 2)
            modulate(hv, 9 + 3, 9 + 4, True)
        P.dma("sp", xo[:, h0:h0 + hw].rearrange("(k p) n -> p k n", p=128), X[:, :, 0:hw], r=[(X, None)],
              key="xo", final=True)
    return P.emit()


NCORES = 8
SEQ = 16384
TPC = SEQ // NCORES
CTX = 256
CPC = CTX // NCORES
NTL = TPC + CPC
HALVES = [[(0, 512, 0), (512, 512, 0)], [(1024, 512, 0), (1536, 512, 0), (2048, CPC, 1)]]


def _run(key, builder, in_maps):
    from concourse.bass_utils import run_bass_kernel_spmd
    if key not in _cache:
        _cache[key] = builder()
    res = run_bass_kernel_spmd(_cache[key], in_maps, core_ids=list(range(NCORES)))
    return res.results


def _f32(a):
    return np.ascontiguousarray(a, dtype=np.float32)


def host_mod(c, c_ctx, mod_w, mod_b):
    cc = np.stack([c[0], c_ctx], axis=-1)
    cT = _f32(cc.reshape(8, 128, 2).transpose(1, 0, 2))
    maps = []
    for r in range(NCORES):
        mw = _f32(mod_w[:, :, 1152 * r:1152 * (r + 1)])
        mb = _f32(mod_b[:, 1152 * r:1152 * (r + 1)].reshape(4, 9, 128).transpose(2, 0, 1))
        maps.append({"cT": cT, "mw": mw, "mb": mb})
    outs = _run("mod", build_mod, maps)
    full = np.zeros((4, 2, 72, 128), np.float32)
    for r in range(NCORES):
        o = outs[r]["out"]
        full[:, :, 9 * r:9 * r + 9, :] = o.transpose(1, 3, 2, 0)
    return full.reshape(4, 2, 9, 8, 128)


def mod_for_tl(modf, l_pre, l_post):
    m = np.zeros((128, 2, 18, 8), np.float32)
    if l_pre is not None:
        m[:, :, 0:9, :] = modf[l_pre].transpose(3, 0, 1, 2)
    if l_post is not None:
        m[:, :, 9:18, :] = modf[l_post].transpose(3, 0, 1, 2)
    return m


def run_tl(xT_cores, mod_in, pre=None, post=None):
    maps = []
    for r in range(NCORES):
        d = {"xT": _f32(xT_cores[r]), "mod": mod_in}
        if pre is not None:
            d.update({"mT": _f32(pre["mT"][r]), "wo": pre["wo"], "ps": pre["ps"],
                      "w_in2": pre["w_in"], "w_out2": pre["w_out"]})
        if post is not None:
            d.update({"w_in1": post["w_in"], "w_out1": post["w_out"]})
        maps.append(d)
    hp, hq = pre is not None, post is not None
    outs = _run(("tl", hp, hq), lambda: build_tl(hp, hq, HALVES), maps)
    xs = [o["xo"] for o in outs]
    hs = [o["ho"] for o in outs] if hq else None
    return xs, hs


def build_proj(KC, chunks, groups, M, use_rope):
    P = Prog()
    g2 = []
    for (c0, nck, part, norm, rope) in groups:
        if norm is None:
            g2 += [(c0 + i * part, 1, part, None, rope) for i in range(nck)]
        else:
            assert nck <= 3
            g2.append((c0, nck, part, norm, rope))
    groups = g2
    N = max(o + w for (o, w) in chunks)
    nch = sum(g[1] for g in groups)
    hT = P.dram("hT", [KC * 128, N], F32, "ExternalInput")
    Wd = P.dram("W", [KC * 128, M], F32, "ExternalInput")
    Gd = P.dram("G", [128, nch], F32, "ExternalInput")
    oT = P.dram("oT", [M, N], F32, "ExternalOutput")
    if use_rope:
        cosd = P.dram("cos", [64, N], F32, "ExternalInput")
        sind = P.dram("sin", [64, N], F32, "ExternalInput")
        pswd = P.dram("psw", [64, 64], F32, "ExternalInput")
    WB = P.sb("WB", [128, KC, M], BF16)
    WS = P.sb("WS", [128, 2, KC, 512], F32)
    HS = P.sb("HS", [128, KC, 512], F32)
    HB = P.sb("HB", [128, 2, KC, 512], BF16)
    G = P.sb("Gs", [128, nch], F32)
    Y = P.sb("Y", [128, 4, 512], F32)
    SQ = P.sb("SQ", [128, 2, 512], BF16)
    RS = P.sb("RS", [128, 2, 512], F32)
    OB = P.sb("OB", [128, 4, 512], F32)
    eps = P.sb("epsp", [128, 1], F32)
    P.add("dve", lambda e: e.memset(eps[:], EPS), w=[(eps, None)])
    ones = {}
    for (c0, nck, part, norm, rope) in groups:
        if norm == "full" and (nck * part) not in ones:
            t = P.sb("ones%d" % (nck * part), [128, 128], BF16)
            P.add("dve", lambda e, t=t, v=1.0 / (nck * part): e.memset(t[:], v), w=[(t, None)])
            ones[nck * part] = t
        if norm == "blk64" and "b" not in ones:
            t = P.sb("onesb", [128, 128], BF16)
            P.add("dve", lambda e, t=t: e.memset(t[:], 0.0), w=[(t, None)])
            P.add("dve", lambda e, t=t: e.memset(t[0:64, 0:64], 1.0 / 64), r=[(t, None)], w=[(t, None)])
            P.add("dve", lambda e, t=t: e.memset(t[64:128, 64:128], 1.0 / 64), r=[(t, None)], w=[(t, None)])
            ones["b"] = t
    if use_rope:
        COS = P.sb("COSs", [64, 512], F32)
        SIN = P.sb("SINs", [64, 512], F32)
        PSWf = P.sb("PSWf", [64, 64], F32)
        PSW = P.sb("PSW", [64, 64], BF16)
        KRB = P.sb("KRB", [64, 512], BF16)
        T1 = P.sb("T1", [64, 512], F32)
        P.dma("sp", PSWf[:], pswd, w=[(PSWf, None)], key="psw")
        P.add("dve", lambda e: e.tensor_copy(out=PSW[:], in_=PSWf[:]), r=[(PSWf, None)], w=[(PSW, None)])
        PR = P.ps("pr", [64, 512])
    PSA = [P.ps("psa%d" % i, [128, 512]) for i in range(4)]
    PM = [P.ps("pm%d" % i, [128, 512]) for i in range(2)]
    P.dma("sp", G[:], Gd, w=[(G, None)], key="g")
    for si, c0 in enumerate(range(0, M, 512)):
        cw = min(512, M - c0)
        s = si % 2
        P.dma("sp", WS[:, s, :, 0:cw], Wd[:, c0:c0 + cw].rearrange("(k p) c -> p k c", p=128), w=[(WS, s)], key=("ws", s))
        P.add("pool", lambda e, s=s, cw=cw, c0=c0: e.tensor_copy(out=WB[:, :, c0:c0 + cw], in_=WS[:, s, :, 0:cw]),
              r=[(WS, s)], w=[(WB, c0)])
    cnt = {"a": 0, "m": 0, "o": 0}
    for ti, (off, wd) in enumerate(chunks):
        hs = ti % 2
        P.dma("sp", HS[:, :, 0:wd], hT[:, off:off + wd].rearrange("(k p) n -> p k n", p=128), w=[(HS, None)], key="hs")
        P.add("pool", lambda e, hs=hs, wd=wd: e.tensor_copy(out=HB[:, hs, :, 0:wd], in_=HS[:, :, 0:wd]),
              r=[(HS, None)], w=[(HB, hs)])
        if use_rope:
            P.dma("sp", COS[:, 0:wd], cosd[:, off:off + wd], w=[(COS, None)], key="cos")
            P.dma("sp", SIN[:, 0:wd], sind[:, off:off + wd], w=[(SIN, None)], key="sin")
        gi = 0
        for (c0, nck, part, norm, rope) in groups:
            pm = PM[cnt["m"] % 2]
            rs = cnt["m"] % 2
            if norm is not None:
                cnt["m"] += 1
            ys = []
            for c in range(nck):
                col = c0 + c * part
                pa = PSA[cnt["a"] % 4]
                yi = cnt["a"] % 4
                cnt["a"] += 1
                ys.append(yi)
                for k in range(KC):
                    P.add("pe", lambda e, pa=pa, k=k, col=col, part=part, hs=hs, wd=wd: e.matmul(
                        pa[0:part, 0:wd], lhsT=WB[:, k, col:col + part], rhs=HB[:, hs, k, 0:wd], start=(k == 0), stop=(k == KC - 1)),
                        r=[(WB, None), (HB, hs)], w=[(pa, None)])
                P.add("act", lambda e, pa=pa, yi=yi, part=part, wd=wd: e.activation(
                    out=Y[0:part, yi, 0:wd], in_=pa[0:part, 0:wd], func=AF.Identity),
                    r=[(pa, None)], w=[(Y, yi)])
                if norm is not None:
                    sq = cnt["a"] % 2
                    P.add("act", lambda e, pa=pa, sq=sq, part=part, wd=wd: e.activation(
                        out=SQ[0:part, sq, 0:wd], in_=pa[0:part, 0:wd], func=AF.Square),
                        r=[(pa, None)], w=[(SQ, sq)])
                    om = ones["b"] if norm == "blk64" else ones[nck * part]
                    P.add("pe", lambda e, pm=pm, om=om, sq=sq, part=part, wd=wd, c=c, nck=nck: e.matmul(
                        pm[0:part, 0:wd], lhsT=om[0:part, 0:part], rhs=SQ[0:part, sq, 0:wd], start=(c == 0), stop=(c == nck - 1)),
                        r=[(om, None), (SQ, sq)], w=[(pm, None)])
            if norm is not None:
                P.add("act", lambda e, pm=pm, rs=rs, part=part, wd=wd: e.activation(
                    out=RS[0:part, rs, 0:wd], in_=pm[0:part, 0:wd], func=AF.Sqrt, bias=eps[0:part, :], scale=1.0),
                    r=[(pm, None), (eps, None)], w=[(RS, rs)])
                P.add("dve", lambda e, rs=rs, part=part, wd=wd: e.reciprocal(out=RS[0:part, rs, 0:wd], in_=RS[0:part, rs, 0:wd]),
                      r=[(RS, rs)], w=[(RS, rs)])
            for c in range(nck):
                col = c0 + c * part
                yi = ys[c]
                ob = cnt["o"] % 4
                cnt["o"] += 1
                if norm is not None:
                    P.add("dve", lambda e, yi=yi, ob=ob, rs=rs, part=part, wd=wd, gc=gi + c: e.scalar_tensor_tensor(
                        out=OB[0:part, ob, 0:wd], in0=Y[0:part, yi, 0:wd], scalar=G[0:part, gc:gc + 1], in1=RS[0:part, rs, 0:wd],
                        op0=ALU.mult, op1=ALU.mult),
                        r=[(Y, yi), (G, None), (RS, rs)], w=[(OB, ob)])
                    src = (OB, ob)
                else:
                    src = (Y, yi)
                if rope:
                    st, sidx = src
                    P.add("act", lambda e, st=st, sidx=sidx, wd=wd: e.activation(out=KRB[:, 0:wd], in_=st[0:64, sidx, 0:wd], func=AF.Identity),
                          r=[src], w=[(KRB, None)])
                    P.add("pe", lambda e, wd=wd: e.matmul(PR[:, 0:wd], lhsT=PSW[:], rhs=KRB[:, 0:wd], start=True, stop=True),
                          r=[(PSW, None), (KRB, None)], w=[(PR, None)])
                    P.add("dve", lambda e, st=st, sidx=sidx, wd=wd: e.tensor_tensor(out=T1[:, 0:wd], in0=st[0:64, sidx, 0:wd], in1=COS[:, 0:wd], op=ALU.mult),
                          r=[src, (COS, None)], w=[(T1, None)])
                    ob2 = cnt["o"] % 4
                    cnt["o"] += 1
                    P.add("dve", lambda e, ob2=ob2, wd=wd: e.tensor_tensor(out=OB[0:64, ob2, 0:wd], in0=PR[:, 0:wd], in1=SIN[:, 0:wd], op=ALU.mult),
                          r=[(PR, None), (SIN, None)], w=[(OB, ob2)])
                    P.add("dve", lambda e, ob2=ob2, wd=wd: e.tensor_tensor(out=OB[0:64, ob2, 0:wd], in0=OB[0:64, ob2, 0:wd], in1=T1[:, 0:wd], op=ALU.add),
                          r=[(OB, ob2), (T1, None)], w=[(OB, ob2)])
                    src = (OB, ob2)
                st, sidx = src
                P.dma("sp", oT[col:col + part, off:off + wd], st[0:part, sidx, 0:wd], r=[src], key=("o", st.name, sidx), final=True)
            gi += nck
    return P.emit()


def run_proj(key, hT_cores, W, G, groups, M, KC, chunks, rope=None):
    maps = []
    for r in range(NCORES):
        d = {"hT": _f32(hT_cores[r]), "W": _f32(W), "G": _f32(G)}
        if rope is not None:
            d.update({"cos": _f32(rope[0][r]), "sin": _f32(rope[1][r]), "psw": rope[2]})
        maps.append(d)
    outs = _run(("proj", key), lambda: build_proj(KC, chunks, groups, M, rope is not None), maps)
    return [o["oT"] for o in outs]


def build_att(NQ, NK, qchunks, scale):
    P = Prog()
    qn = P.dram("qn", [128, NQ], F32, "ExternalInput")
    qr = P.dram("qr", [64, NQ], F32, "ExternalInput")
    kn = P.dram("kn", [128, NK], F32, "ExternalInput")
    kr = P.dram("kr", [64, NK], F32, "ExternalInput")
    vd = P.dram("v", [NK, 128], F32, "ExternalInput")
    oT = P.dram("oT", [128, NQ], F32, "ExternalOutput")
    NT = NK // 128
    KN = P.sb("KN", [128, NK], BF16)
    KR = P.sb("KR", [64, NK], BF16)
    V = P.sb("V", [128, NT, 128], BF16)
    ST = P.sb("ST", [128, 2, 2048], F32)
    QS = P.sb("QS", [128, 2, 512], F32)
    QRS = P.sb("QRS", [64, 2, 512], F32)
    QN = P.sb("QN", [128, 2, 512], BF16)
    QR = P.sb("QR", [64, 2, 512], BF16)
    PT = P.sb("PT", [128, 3, 512], BF16)
    RC = P.sb("RC", [128, 2, 512], F32)
    OS = P.sb("OS", [128, 2, 512], F32)
    ONES = P.sb("ONES", [128, 128], BF16)
    P.add("dve", lambda e: e.memset(ONES[:], 1.0), w=[(ONES, None)])
    PS = [P.ps("ps%d" % i, [128, 512]) for i in range(3)]
    PO = [P.ps("po%d" % i, [128, 512]) for i in range(2)]
    PSM = [P.ps("psm%d" % i, [128, 512]) for i in range(2)]
    n = 0
    for c0 in range(0, NK, 2048):
        cw = min(2048, NK - c0)
        for (src, dst, pp) in ((kn, KN, 128), (kr, KR, 64)):
            s = n % 2
            n += 1
            P.dma("sp", ST[0:pp, s, 0:cw], src[:, c0:c0 + cw], w=[(ST, s)], key=("st", s))
            P.add("pool", lambda e, s=s, cw=cw, c0=c0, dst=dst, pp=pp: e.tensor_copy(out=dst[:, c0:c0 + cw], in_=ST[0:pp, s, 0:cw]),
                  r=[(ST, s)], w=[(dst, c0)])
        s = n % 2
        n += 1
        nt = cw // 128
        P.dma("sp", ST[:, s, 0:cw].rearrange("p (t d) -> p t d", d=128),
              vd[c0:c0 + cw, :].rearrange("(t p) d -> p t d", p=128), w=[(ST, s)], key=("st", s))
        P.add("pool", lambda e, s=s, cw=cw, c0=c0, nt=nt: e.tensor_copy(
            out=V[:, c0 // 128:c0 // 128 + nt, :], in_=ST[:, s, 0:cw].rearrange("p (t d) -> p t d", d=128)),
            r=[(ST, s)], w=[(V, c0)])
    pt_n = 0
    ps_n = 0
    for qi, (off, wd, nkt) in enumerate(qchunks):
        s = qi % 2
        P.dma("sp", QS[:, s, 0:wd], qn[:, off:off + wd], w=[(QS, s)], key=("qs", s))
        P.dma("sp", QRS[:, s, 0:wd], qr[:, off:off + wd], w=[(QRS, s)], key=("qrs", s))
        P.add("pool", lambda e, s=s, wd=wd: e.tensor_copy(out=QN[:, s, 0:wd], in_=QS[:, s, 0:wd]), r=[(QS, s)], w=[(QN, s)])
        P.add("pool", lambda e, s=s, wd=wd: e.tensor_copy(out=QR[:, s, 0:wd], in_=QRS[:, s, 0:wd]), r=[(QRS, s)], w=[(QR, s)])
        po = PO[qi % 2]
        psm = PSM[qi % 2]

        def qk(t, b, wd=wd, s=s):
            P.add("pe", lambda e, t=t, b=b, wd=wd, s=s: e.matmul(PS[b][:, 0:wd], lhsT=KN[:, t * 128:(t + 1) * 128], rhs=QN[:, s, 0:wd],
                                                    start=True, stop=False),
                  r=[(KN, None), (QN, s)], w=[(PS[b], None)])
            P.add("pe", lambda e, t=t, b=b, wd=wd, s=s: e.matmul(PS[b][:, 0:wd], lhsT=KR[:, t * 128:(t + 1) * 128], rhs=QR[:, s, 0:wd],
                                                    start=False, stop=True),
                  r=[(KR, None), (QR, s)], w=[(PS[b], None)])

        bs = []
        b0 = ps_n % 3
        ps_n += 1
        qk(0, b0)
        bs.append(b0)
        for t in range(nkt):
            if t + 1 < nkt:
                b1 = ps_n % 3
                ps_n += 1
                qk(t + 1, b1)
                bs.append(b1)
            b = bs[t]
            p = pt_n % 3
            pt_n += 1
            P.add("act", lambda e, b=b, p=p, wd=wd: e.activation(out=PT[:, p, 0:wd], in_=PS[b][:, 0:wd], func=AF.Exp, scale=scale),
                  r=[(PS[b], None)], w=[(PT, p)])
            P.add("pe", lambda e, t=t, p=p, wd=wd, po=po, nkt=nkt: e.matmul(po[:, 0:wd], lhsT=V[:, t, :], rhs=PT[:, p, 0:wd], start=(t == 0), stop=(t == nkt - 1)),
                  r=[(V, None), (PT, p)], w=[(po, None)])
            P.add("pe", lambda e, t=t, p=p, wd=wd, psm=psm, nkt=nkt: e.matmul(psm[:, 0:wd], lhsT=ONES[:], rhs=PT[:, p, 0:wd], start=(t == 0), stop=(t == nkt - 1)),
                  r=[(ONES, None), (PT, p)], w=[(psm, None)])
        P.add("dve", lambda e, s=s, wd=wd, psm=psm: e.reciprocal(out=RC[:, s, 0:wd], in_=psm[:, 0:wd]), r=[(psm, None)], w=[(RC, s)])
        P.add("dve", lambda e, s=s, wd=wd, po=po: e.tensor_tensor(out=OS[:, s, 0:wd], in0=po[:, 0:wd], in1=RC[:, s, 0:wd], op=ALU.mult),
              r=[(po, None), (RC, s)], w=[(OS, s)])
        P.dma("sp", oT[:, off:off + wd], OS[:, s, 0:wd], r=[(OS, s)], key=("os", s), final=True)
    return P.emit()


def build_pool(segs):
    P = Prog()
    L = max(b + n + 16 for (b, n, _) in segs)
    NO = max(o + n for (_, n, o) in segs)
    hp = P.dram("hp", [D, L], F32, "ExternalInput")
    rc = P.dram("rc", [D, NO], F32, "ExternalInput")
    mT = P.dram("mT", [D, NO], F32, "ExternalOutput")
    A = [P.sb("A%d" % i, [128, L], F32) for i in range(2)]
    B = [P.sb("B%d" % i, [128, L], F32) for i in range(2)]
    RCs = [P.sb("RC%d" % i, [128, NO], F32) for i in range(2)]
    O = [P.sb("O%d" % i, [128, NO], F32) for i in range(2)]
    for k in range(8):
        g = k // 2
        a = A[k % 2]
        r_ = RCs[k % 2]
        o_ = O[k % 2]
        P.dma("sp", a[:], hp[k * 128:(k + 1) * 128, :], w=[(a, None)], key=("a", k % 2))
        P.dma("sp", r_[:], rc[k * 128:(k + 1) * 128, :], w=[(r_, None)], key=("r", k % 2))
        cur = a
        lo, hi = 0, L
        for lev in range(g + 1):
            sh = 1 if lev == 0 else (1 << (lev - 1))
            dst = B[lev % 2]
            if lev == 0:
                nlo, nhi = lo + 1, hi
                P.add("dve", lambda e, cur=cur, dst=dst, nlo=nlo, nhi=nhi: e.tensor_tensor(
                    out=dst[:, nlo:nhi], in0=cur[:, nlo - 1:nhi - 1], in1=cur[:, nlo:nhi], op=ALU.add),
                    r=[(cur, None)], w=[(dst, None)])
            else:
                nlo, nhi = lo + sh, hi - sh
                P.add("dve", lambda e, cur=cur, dst=dst, nlo=nlo, nhi=nhi, sh=sh: e.tensor_tensor(
                    out=dst[:, nlo:nhi], in0=cur[:, nlo - sh:nhi - sh], in1=cur[:, nlo + sh:nhi + sh], op=ALU.add),
                    r=[(cur, None)], w=[(dst, None)])
            cur, lo, hi = dst, nlo, nhi
        for (b, n, oo) in segs:
            P.add("dve", lambda e, cur=cur, b=b, n=n, oo=oo, o_=o_, r_=r_: e.tensor_tensor(
                out=o_[:, oo:oo + n], in0=cur[:, b + 8:b + 8 + n], in1=r_[:, oo:oo + n], op=ALU.mult),
                r=[(cur, None), (r_, None)], w=[(o_, oo)])
            P.add("dve", lambda e, a=a, b=b, n=n, oo=oo, o_=o_: e.tensor_tensor(
                out=o_[:, oo:oo + n], in0=o_[:, oo:oo + n], in1=a[:, b + 8:b + 8 + n], op=ALU.subtract),
                r=[(o_, oo), (a, None)], w=[(o_, oo)])
        P.dma("sp", mT[k * 128:(k + 1) * 128, :], o_[:], r=[(o_, None)], key=("o", k % 2), final=True)
    return P.emit()


def build_conve(n):
    P = Prog()
    bT = P.dram("bT", [D, n], F32, "ExternalInput")
    cT = P.dram("cT", [D, n + 2], F32, "ExternalInput")
    uT = P.dram("uT", [D, n + 2], F32, "ExternalInput")
    cw = P.dram("cw", [128, 8, 3], F32, "ExternalInput")
    mT = P.dram("mT", [D, n], F32, "ExternalOutput")
    CW = P.sb("CW", [128, 8, 3], F32)
    Bt = [P.sb("B%d" % i, [128, n], F32) for i in range(2)]
    Ct = [P.sb("C%d" % i, [128, n + 2], F32) for i in range(2)]
    Ut = [P.sb("U%d" % i, [128, n + 2], F32) for i in range(2)]
    Z = [P.sb("Z%d" % i, [128, n], F32) for i in range(2)]
    P.dma("sp", CW[:], cw, w=[(CW, None)], key="cw")
    for k in range(8):
        s = k % 2
        b_, c_, u_, z_ = Bt[s], Ct[s], Ut[s], Z[s]
        P.dma("sp", b_[:], bT[k * 128:(k + 1) * 128, :], w=[(b_, None)], key=("b", s))
        P.dma("sp", c_[:], cT[k * 128:(k + 1) * 128, :], w=[(c_, None)], key=("c", s))
        P.dma("sp", u_[:], uT[k * 128:(k + 1) * 128, :], w=[(u_, None)], key=("u", s))
        P.add("dve", lambda e, c_=c_, u_=u_: e.tensor_tensor(out=c_[:], in0=c_[:], in1=u_[:], op=ALU.mult),
              r=[(c_, None), (u_, None)], w=[(c_, None)])
        P.add("dve", lambda e, c_=c_, z_=z_, k=k: e.tensor_scalar(out=z_[:], in0=c_[:, 0:n], scalar1=CW[:, k, 0:1], scalar2=None, op0=ALU.mult),
              r=[(c_, None), (CW, None)], w=[(z_, None)])
        for j in (1, 2):
            P.add("dve", lambda e, c_=c_, z_=z_, k=k, j=j: e.scalar_tensor_tensor(
                out=z_[:], in0=c_[:, j:j + n], scalar=CW[:, k, j:j + 1], in1=z_[:], op0=ALU.mult, op1=ALU.add),
                r=[(c_, None), (CW, None), (z_, None)], w=[(z_, None)])
        P.add("dve", lambda e, b_=b_, z_=z_: e.tensor_tensor(out=z_[:], in0=z_[:], in1=b_[:], op=ALU.mult),
              r=[(z_, None), (b_, None)], w=[(z_, None)])
        P.dma("sp", mT[k * 128:(k + 1) * 128, :], z_[:], r=[(z_, None)], key=("z", s), final=True)
    return P.emit()


NA_H = 16
NA_EXT = 2560
NA_NK = NA_EXT + CTX
NA_NT = NA_NK // 128


def build_natt():
    P = Prog()
    qT = P.dram("qT", [D, TPC], F32, "ExternalInput")
    kT = P.dram("kT", [D, NA_NK], F32, "ExternalInput")
    vd = P.dram("v", [NA_NK, D], F32, "ExternalInput")
    bias = P.dram("bias", [NA_H, 128, 3 * 6 * 256], F32, "ExternalInput")
    ident = P.dram("ident", [128, 128], F32, "ExternalInput")
    mT = P.dram("mT", [D, TPC], F32, "ExternalOutput")
    KT = P.sb("KT", [128, 8, NA_NK], BF16)
    QT = P.sb("QT", [128, 8, TPC], BF16)
    VB = P.sb("VB", [128, NA_NT, D], BF16)
    ST = P.sb("ST", [128, 2, 2048], F32)
    BS = P.sb("BSt", [128, 3 * 6 * 256], F32)
    BB = P.sb("BB", [128, 2, 3 * 6 * 256], BF16)
    IDf = P.sb("IDf", [128, 128], F32)
    ID = P.sb("ID", [128, 128], BF16)
    ONES = P.sb("ONES", [128, 64], BF16)
    PT = P.sb("PT", [128, 3, 256], BF16)
    RC = P.sb("RC", [64, 2, 256], F32)
    OS = P.sb("OS", [64, 2, 256], F32)
    PS = [P.ps("ps%d" % i, [128, 512]) for i in range(3)]
    PO = [P.ps("po%d" % i, [128, 512]) for i in range(2)]
    PSM = [P.ps("psm%d" % i, [128, 512]) for i in range(2)]
    P.add("dve", lambda e: e.memset(ONES[:], 1.0), w=[(ONES, None)])
    P.dma("sp", IDf[:], ident, w=[(IDf, None)], key="id")
    P.add("dve", lambda e: e.tensor_copy(out=ID[:], in_=IDf[:]), r=[(IDf, None)], w=[(ID, None)])
    n = 0
    for k in range(8):
        for c0 in range(0, NA_NK, 2048):
            cw = min(2048, NA_NK - c0)
            s = n % 2
            n += 1
            P.dma("sp", ST[:, s, 0:cw], kT[k * 128:(k + 1) * 128, c0:c0 + cw], w=[(ST, s)], key=("st", s))
            P.add("pool", lambda e, s=s, cw=cw, c0=c0, k=k: e.tensor_copy(out=KT[:, k, c0:c0 + cw], in_=ST[:, s, 0:cw]),
                  r=[(ST, s)], w=[(KT, (k, c0))])
        s = n % 2
        n += 1
        P.dma("sp", ST[:, s, 0:TPC], qT[k * 128:(k + 1) * 128, :], w=[(ST, s)], key=("st", s))
        P.add("pool", lambda e, s=s, k=k: e.tensor_scalar(out=QT[:, k, :], in0=ST[:, s, 0:TPC], scalar1=0.125, scalar2=None, op0=ALU.mult),
              r=[(ST, s)], w=[(QT, k)])
    for t in range(NA_NT):
        for hh in range(0, D, 512):
            s = n % 2
            n += 1
            P.dma("sp", ST[:, s, 0:512], vd[t * 128:(t + 1) * 128, hh:hh + 512], w=[(ST, s)], key=("st", s))
            P.add("pool", lambda e, s=s, t=t, hh=hh: e.tensor_copy(out=VB[:, t, hh:hh + 512], in_=ST[:, s, 0:512]),
                  r=[(ST, s)], w=[(VB, (t, hh))])
    pn = 0
    tn = 0
    on = 0
    for h in range(NA_H):
        kc = h // 2
        p0 = (h % 2) * 64
        bb = h % 2
        P.dma("sp", BS[:], bias[h], w=[(BS, None)], key="bs")
        P.add("pool", lambda e, bb=bb: e.tensor_copy(out=BB[:, bb, :], in_=BS[:]), r=[(BS, None)], w=[(BB, bb)])
        for b in range(8):
            var = 0 if b == 0 else (2 if b == 7 else 1)
            q0 = b * 256
            po = PO[on % 2]
            psm = PSM[on % 2]
            osl = on % 2
            on += 1
            tiles = [("l", t) for t in range(6)] + [("c", 0), ("c", 1)]
            for ti, (kind, t) in enumerate(tiles):
                ps = PS[pn % 3]
                pn += 1
                if kind == "l":
                    k0 = (4 * b + 2 * t) * 64
                else:
                    k0 = NA_EXT + t * 128
                vt = k0 // 128
                P.add("pe", lambda e, ps=ps, kc=kc, p0=p0, k0=k0, q0=q0, kind=kind: e.matmul(
                    ps[:, 0:256], lhsT=KT[p0:p0 + 64, kc, k0:k0 + 128], rhs=QT[p0:p0 + 64, kc, q0:q0 + 256],
                    start=True, stop=(kind == "c")),
                    r=[(KT, None), (QT, kc)], w=[(ps, None)])
                if kind == "l":
                    bo = (var * 6 + t) * 256
                    P.add("pe", lambda e, ps=ps, bb=bb, bo=bo: e.matmul(
                        ps[:, 0:256], lhsT=ID[:], rhs=BB[:, bb, bo:bo + 256], start=False, stop=True),
                        r=[(ID, None), (BB, bb)], w=[(ps, None)])
                pt = tn % 3
                tn += 1
                P.add("act", lambda e, ps=ps, pt=pt: e.activation(out=PT[:, pt, :], in_=ps[:, 0:256], func=AF.Exp),
                      r=[(ps, None)], w=[(PT, pt)])
                P.add("pe", lambda e, po=po, vt=vt, h=h, pt=pt, ti=ti: e.matmul(
                    po[0:64, 0:256], lhsT=VB[:, vt, h * 64:(h + 1) * 64], rhs=PT[:, pt, :], start=(ti == 0), stop=(ti == 7)),
                    r=[(VB, None), (PT, pt)], w=[(po, None)])
                P.add("pe", lambda e, psm=psm, pt=pt, ti=ti: e.matmul(
                    psm[0:64, 0:256], lhsT=ONES[:], rhs=PT[:, pt, :], start=(ti == 0), stop=(ti == 7)),
                    r=[(ONES, None), (PT, pt)], w=[(psm, None)])
            P.add("dve", lambda e, psm=psm, osl=osl: e.reciprocal(out=RC[:, osl, :], in_=psm[0:64, 0:256]),
                  r=[(psm, None)], w=[(RC, osl)])
            P.add("dve", lambda e, po=po, osl=osl: e.tensor_tensor(out=OS[:, osl, :], in0=po[0:64, 0:256], in1=RC[:, osl, :], op=ALU.mult),
                  r=[(po, None), (RC, osl)], w=[(OS, osl)])
            P.dma("sp", mT[h * 64:(h + 1) * 64, q0:q0 + 256], OS[:, osl, :], r=[(OS, osl)], key=("os", osl), final=True)
    return P.emit()


PCH = [(0, 512), (512, 512), (1024, 512), (1536, 512), (2048, CPC)]


def _rope_tables():
    freqs = (np.float32(10000.0) ** (-np.arange(16, dtype=np.float32) / np.float32(16))).astype(np.float32)
    t = np.arange(SEQ)
    row = (t // 64).astype(np.float32)
    col = (t % 64).astype(np.float32)
    ang = np.concatenate([row[:, None] * freqs, col[:, None] * freqs], axis=-1).astype(np.float32)
    cos = np.repeat(np.cos(ang), 2, axis=1).T.astype(np.float32)
    sin = np.repeat(np.sin(ang), 2, axis=1).T.astype(np.float32)
    coss, sins = [], []
    for r in range(NCORES):
        coss.append(np.concatenate([cos[:, r * TPC:(r + 1) * TPC], np.ones((64, CPC), np.float32)], axis=1))
        sins.append(np.concatenate([sin[:, r * TPC:(r + 1) * TPC], np.zeros((64, CPC), np.float32)], axis=1))
    psw = np.zeros((64, 64), np.float32)
    for i in range(32):
        psw[2 * i + 1, 2 * i] = -1.0
        psw[2 * i, 2 * i + 1] = 1.0
    return coss, sins, psw


def _col(v, n=128):
    o = np.ones((128,), np.float32)
    o[:v.shape[0]] = v
    return o


def _gather(outs, r0, r1):
    lat = np.concatenate([o[r0:r1, :TPC] for o in outs], axis=1)
    cx = np.concatenate([o[r0:r1, TPC:] for o in outs], axis=1)
    return lat, cx


def _na_bias(rpb, base):
    t = np.arange(6)[:, None, None]
    kk = np.arange(128)[None, :, None]
    qq = np.arange(256)[None, None, :]
    key_row = base - 4 + 2 * t + kk // 64
    kcol = kk % 64
    r = base + qq // 64
    qc = qq % 64
    rs = np.clip(r - 4, 0, 248)
    cs = np.clip(qc - 8, 0, 48)
    valid = (key_row >= rs) & (key_row < rs + 8) & (kcol >= cs) & (kcol < cs + 16) & (key_row >= 0) & (key_row < 256)
    dr = np.clip(key_row - r + 7, 0, 14)
    dc = np.clip(kcol - qc + 15, 0, 30)
    vals = rpb[:, dr, dc]
    return np.where(valid[None], vals, np.float32(-30000.0)).astype(np.float32)


def kernel(x, c, ctx, c_ctx, mod_w, mod_b, ffn_w_in, ffn_w_out,
           mla_w_dq, mla_g_dq, mla_w_uq, mla_w_dkv, mla_g_dkv, mla_w_uk, mla_w_uv,
           mla_g_qn, mla_g_qr, mla_g_kn, mla_g_kr, mla_w_o,
           pool_w, pool_scale,
           na_w_qkv, na_g_q, na_g_k, na_rpb, na_w_o,
           conv_w_in, conv_w, conv_w_out):
    x = np.asarray(x, np.float32)[0]
    ctx = np.asarray(ctx, np.float32)[0]
    R = range(NCORES)
    ones8 = np.ones((128, 8), np.float32)
    modf = host_mod(np.asarray(c, np.float32), np.asarray(c_ctx, np.float32), np.asarray(mod_w), np.asarray(mod_b))
    ffn = lambda i, j: dict(w_in=_f32(ffn_w_in[i, j]), w_out=_f32(ffn_w_out[i, j]))

    xT = [np.concatenate([x[r * TPC:(r + 1) * TPC].T, ctx[r * CPC:(r + 1) * CPC].T], axis=1) for r in R]
    xs, hs = run_tl(xT, mod_for_tl(modf, None, 0), post=ffn(0, 0))

    _dbg('tl0', (xs, hs))
    coss, sins, psw = _rope_tables()
    rope = (coss, sins, psw)
    Wlat = np.concatenate([mla_w_dq[0], mla_w_dkv[0]], axis=1)
    G = np.stack([_col(mla_g_dq[0][0:128]), _col(mla_g_dq[0][128:256]), _col(mla_g_dq[0][256:384]),
                  _col(mla_g_dkv[0][0:128]), _col(mla_g_dkv[0][128:256]), _col(mla_g_kr[0])], axis=1)
    lat = run_proj("lat", hs, Wlat, G, [(0, 3, 128, "full", False), (384, 2, 128, "full", False), (640, 1, 64, "full", True)],
                   704, 8, PCH, rope)
    cq = [l[0:384] for l in lat]
    ckv = [l[384:640] for l in lat]
    gq = []
    Gq = []
    for h in range(8):
        gq += [(h * 192, 1, 128, "full", False), (h * 192 + 128, 1, 64, "full", True)]
        Gq += [_col(mla_g_qn[0]), _col(mla_g_qr[0])]
    Q = run_proj("q", cq, mla_w_uq[0].reshape(384, 8 * 192), np.stack(Gq, axis=1), gq, 1536, 3, PCH, rope)
    KNo = run_proj("kn", ckv, mla_w_uk[0].reshape(256, 1024), np.stack([_col(mla_g_kn[0])] * 8, axis=1),
                   [(h * 128, 1, 128, "full", False) for h in range(8)], 1024, 2, PCH)
    Vo = run_proj("v", ckv, mla_w_uv[0].reshape(256, 1024), ones8, [(0, 8, 128, None, False)], 1024, 2, PCH)
    kr_l, kr_c = _gather(lat, 640, 704)
    kr_all = np.concatenate([kr_c, kr_l], axis=1)
    maps = []
    for h in range(8):
        ql, qc_ = _gather(Q, h * 192, h * 192 + 128)
        rl, rc_ = _gather(Q, h * 192 + 128, (h + 1) * 192)
        kl, kc_ = _gather(KNo, h * 128, (h + 1) * 128)
        vl, vc_ = _gather(Vo, h * 128, (h + 1) * 128)
        maps.append({"qn": _f32(np.concatenate([qc_, ql], axis=1)), "qr": _f32(np.concatenate([rc_, rl], axis=1)),
                     "kn": _f32(np.concatenate([kc_, kl], axis=1)), "kr": _f32(kr_all),
                     "v": _f32(np.concatenate([vc_, vl], axis=1).T)})
    NQ = SEQ + CTX
    qch = [(0, CTX, CTX // 128)] + [(CTX + 512 * i, 512, NQ // 128) for i in range(SEQ // 512)]
    oh = _run("att", lambda: build_att(NQ, NQ, qch, float(192 ** -0.5)), maps)
    oall = np.concatenate([o["oT"] for o in oh], axis=0)
    mT = [np.concatenate([oall[:, CTX + r * TPC:CTX + (r + 1) * TPC], oall[:, r * CPC:(r + 1) * CPC]], axis=1) for r in R]
    _dbg('lat', lat); _dbg('Q', Q); _dbg('KN', KNo); _dbg('V', Vo); _dbg('oall', oall)
    pre = dict(mT=mT, wo=_f32(mla_w_o[0]), ps=ones8, **ffn(0, 1))
    xs, hs = run_tl(xs, mod_for_tl(modf, 0, 1), pre=pre, post=ffn(1, 0))

    _dbg('tl1', (xs, hs))
    hx = np.concatenate([h_[:, :TPC] for h_ in hs], axis=1)
    hc = np.concatenate([h_[:, TPC:] for h_ in hs], axis=1)
    hxp = np.pad(hx, ((0, 0), (8, 8)))
    hcp = np.pad(hc, ((0, 0), (8, 8)))
    half = np.repeat(np.array([1, 2, 4, 8]), 256)[:, None]

    def rcount(T, t):
        lo = np.clip(t[None, :] - half, 0, T)
        hi = np.clip(t[None, :] + half, 0, T)
        return (np.float32(1.0) / (hi - lo).astype(np.float32)).astype(np.float32)

    rc_c = rcount(CTX, np.arange(CTX))
    maps = []
    for r in R:
        hp = np.concatenate([hxp[:, r * TPC:r * TPC + TPC + 16], hcp], axis=1)
        rc = np.concatenate([rcount(SEQ, np.arange(r * TPC, (r + 1) * TPC)), rc_c], axis=1)
        maps.append({"hp": _f32(hp), "rc": _f32(rc)})
    po = _run("pool", lambda: build_pool([(0, TPC, 0), (TPC + 16, CTX, TPC)]), maps)
    mT = [np.concatenate([po[r]["mT"][:, :TPC], po[r]["mT"][:, TPC + r * CPC:TPC + (r + 1) * CPC]], axis=1) for r in R]
    wbd = np.zeros((D, D), np.float32)
    for g in range(4):
        wbd[g * 256:(g + 1) * 256, g * 256:(g + 1) * 256] = pool_w[0, g]
    psv = _f32(np.asarray(pool_scale[0], np.float32).reshape(8, 128).T)
    pre = dict(mT=mT, wo=wbd, ps=psv, **ffn(1, 1))
    xs, hs = run_tl(xs, mod_for_tl(modf, 1, 2), pre=pre, post=ffn(2, 0))

    _dbg('pool', po); _dbg('tl2', (xs, hs))
    gna = [(cc * 128, 1, 128, "blk64", False) for cc in range(16)] + [(2048, 8, 128, None, False)]
    Gna = np.stack([np.tile(np.asarray(na_g_q[0], np.float32), 2)] * 8 + [np.tile(np.asarray(na_g_k[0], np.float32), 2)] * 8
                   + [np.ones(128, np.float32)] * 8, axis=1)
    qkv = run_proj("qkv", hs, na_w_qkv[0], Gna, gna, 3072, 8, PCH)
    k_l, k_c = _gather(qkv, 1024, 2048)
    v_l, v_c = _gather(qkv, 2048, 3072)
    k_lp = np.pad(k_l, ((0, 0), (256, 256)))
    v_lp = np.pad(v_l, ((0, 0), (256, 256)))
    ident = np.eye(128, dtype=np.float32)
    b_int = _na_bias(np.asarray(na_rpb[0], np.float32), 8)
    b_top = _na_bias(np.asarray(na_rpb[0], np.float32), 0)
    b_bot = _na_bias(np.asarray(na_rpb[0], np.float32), 252)
    maps = []
    for r in R:
        bv = np.stack([b_top if r == 0 else b_int, b_int, b_bot if r == NCORES - 1 else b_int], axis=1)
        bv = bv.transpose(0, 3, 1, 2, 4).reshape(16, 128, 3 * 6 * 256)
        kT_in = np.concatenate([k_lp[:, r * TPC:r * TPC + NA_EXT], k_c], axis=1)
        v_in = np.concatenate([v_lp[:, r * TPC:r * TPC + NA_EXT], v_c], axis=1).T
        maps.append({"qT": _f32(qkv[r][0:1024, :TPC]), "kT": _f32(kT_in), "v": _f32(v_in), "bias": _f32(bv), "ident": ident})
    no = _run("natt", build_natt, maps)
    zc = np.zeros((D, CPC), np.float32)
    mT = [np.concatenate([no[r]["mT"], zc], axis=1) for r in R]
    pre = dict(mT=mT, wo=_f32(na_w_o[0]), ps=ones8, **ffn(2, 1))
    xs, hs = run_tl(xs, mod_for_tl(modf, 2, 3), pre=pre, post=ffn(3, 0))

    _dbg('qkv', qkv); _dbg('natt', no); _dbg('tl3', (xs, hs))
    bcu = run_proj("bcu", hs, conv_w_in[0], np.ones((128, 24), np.float32), [(0, 24, 128, None, False)], 3072, 8, PCH)
    c_l, _ = _gather(bcu, 1024, 2048)
    u_l, _ = _gather(bcu, 2048, 3072)
    c_lp = np.pad(c_l, ((0, 0), (1, 1)))
    u_lp = np.pad(u_l, ((0, 0), (1, 1)))
    cw = _f32(np.asarray(conv_w[0], np.float32).T.reshape(8, 128, 3).transpose(1, 0, 2))
    maps = [{"bT": _f32(bcu[r][0:1024, :TPC]), "cT": _f32(c_lp[:, r * TPC:r * TPC + TPC + 2]),
             "uT": _f32(u_lp[:, r * TPC:r * TPC + TPC + 2]), "cw": cw} for r in R]
    co = _run("conve", lambda: build_conve(TPC), maps)
    mT = [np.concatenate([co[r]["mT"], zc], axis=1) for r in R]
    pre = dict(mT=mT, wo=_f32(conv_w_out[0]), ps=ones8, **ffn(3, 1))
    _dbg('bcu', bcu); _dbg('conve', co)
    xs, _ = run_tl(xs, mod_for_tl(modf, 3, None), pre=pre, post=None)
    out = np.concatenate([xs[r][:, :TPC].T for r in R], axis=0)
    return np.ascontiguousarray(out[None], dtype=np.float32)
```

```python
import numpy as np
from contextlib import ExitStack
import concourse.bass as bass
import concourse.mybir as mybir

F32 = mybir.dt.float32
BF16 = mybir.dt.bfloat16
AF = mybir.ActivationFunctionType
ALU = mybir.AluOpType

D = 1024
DFF = 2816
NJ = DFF // 128
EPS = 1e-6
NCORES = 8
SEQ = 16384
TPC = SEQ // NCORES
CTX = 256
CPC = CTX // NCORES
NTL = TPC + CPC
NKEY = NCORES * NTL
HALVES = [[(0, 512, 0), (512, 512, 0)], [(1024, 512, 0), (1536, 512, 0), (2048, CPC, 1)]]
PCH = [(0, 512), (512, 512), (1024, 512), (1536, 512), (2048, CPC)]


class Tile:
    __slots__ = ("t", "name", "acc")

    def __init__(self, t, name):
        self.t = t
        self.name = name
        self.acc = {}

    def __getitem__(self, idx):
        return self.t[idx]


class Op:
    __slots__ = ("eng", "fn", "waits", "sem", "val", "isdma")


class Ctx:
    ENGS = ("pe", "act", "dve", "pool", "sp")

    def __init__(self):
        self.nc = bass.Bass("TRN2", target_bir_lowering=False, num_devices=NCORES)
        self.es = ExitStack()
        self.esem = [{e: self.es.enter_context(self.nc.semaphore("se%d_%s" % (p, e))) for e in self.ENGS} for p in range(2)]
        self.dpool = [[], []]
        self.ccsem = self.es.enter_context(self.nc.semaphore("cc"))
        self.cccount = 0
        self.stage = 0
        self.names = 0

    def dram(self, name, shape, dt=F32, kind="Internal"):
        return self.nc.dram_tensor(name, list(shape), dt, kind=kind).ap()

    def dsem(self, parity, idx):
        pool = self.dpool[parity]
        while len(pool) <= idx:
            pool.append(self.es.enter_context(self.nc.semaphore("sd%d_%d" % (parity, len(pool)))))
        return pool[idx]

    def allgather(self, src, dst):
        nc = self.nc
        self.cccount += 1
        cnt = self.cccount
        with nc.Block() as block:
            @block.gpsimd
            def _(g):
                g.collective_compute("AllGather", ALU.bypass, replica_groups=[list(range(NCORES))],
                                     ins=[src], outs=[dst]).then_inc(self.ccsem)
                g.wait_ge(self.ccsem, cnt)

    def finish(self):
        self.es.close()
        return self.nc


class Prog:
    ENGS = Ctx.ENGS

    def __init__(self, ctx):
        self.ctx = ctx
        self.nc = ctx.nc
        self.par = ctx.stage % 2
        self.es = ExitStack()
        self.ops = {e: [] for e in self.ENGS}
        self.esem = ctx.esem[self.par]
        self.ecount = {e: 0 for e in self.ENGS}
        self.waited = {e: {} for e in self.ENGS}
        self.dkeys = {}
        self.dcount = {}
        self.final = []
        self.sid = ctx.stage

    def sb(self, name, shape, dt):
        nm = "s%d_%s" % (self.sid, name)
        return Tile(self.es.enter_context(self.nc.sbuf_tensor(nm, list(shape), dt)), nm)

    def ps(self, name, shape, dt=F32):
        nm = "s%d_%s" % (self.sid, name)
        return Tile(self.es.enter_context(self.nc.psum_tensor(nm, list(shape), dt)), nm)

    def _deps(self, reads, writes):
        deps = []
        for (tl, rg) in reads:
            for k, ent in tl.acc.items():
                if rg is None or k is None or k == rg:
                    if ent[0] is not None:
                        deps.append(ent[0])
        for (tl, rg) in writes:
            for k, ent in tl.acc.items():
                if rg is None or k is None or k == rg:
                    if ent[0] is not None:
                        deps.append(ent[0])
                    deps.extend(ent[1])
        return deps

    def _commit(self, op, reads, writes):
        for (tl, rg) in reads:
            ent = tl.acc.setdefault(rg, [None, []])
            ent[1].append(op)
        for (tl, rg) in writes:
            if rg is None:
                tl.acc.clear()
            tl.acc[rg] = [op, []]

    def add(self, eng, fn, r=(), w=(), dma_key=None):
        op = Op()
        op.eng = eng
        op.fn = fn
        op.isdma = dma_key is not None
        deps = self._deps(r, w)
        waits = {}
        wd = self.waited[eng]
        for d in deps:
            if d.eng == "pe" and eng == "pe" and not d.isdma:
                continue
            if wd.get(d.sem, 0) >= d.val:
                continue
            if waits.get(d.sem, (None, 0))[1] < d.val:
                waits[d.sem] = (d.sem, d.val)
        for s, v in waits.values():
            wd[s] = v
        op.waits = list(waits.values())
        if op.isdma:
            if dma_key not in self.dkeys:
                self.dkeys[dma_key] = self.ctx.dsem(self.par, len(self.dkeys))
                self.dcount[dma_key] = 0
            self.dcount[dma_key] += 16
            op.sem = self.dkeys[dma_key]
            op.val = self.dcount[dma_key]
        else:
            self.ecount[eng] += 1
            op.sem = self.esem[eng]
            op.val = self.ecount[eng]
        self._commit(op, r, w)
        self.ops[eng].append(op)
        return op

    def dma(self, eng, out, in_, r=(), w=(), key=None, final=False, **kw):
        op = self.add(eng, lambda e: e.dma_start(out=out, in_=in_, **kw), r=r, w=w, dma_key=key)
        if final:
            self.final.append(op)
        return op

    def emit(self):
        nc = self.nc
        ctx = self.ctx
        other = list(ctx.esem[1 - self.par].values()) + list(ctx.dpool[1 - self.par])
        with nc.Block() as block:
            def run(e, name):
                if name == "sp":
                    for s in other:
                        e.sem_clear(s)
                for op in self.ops[name]:
                    for s, v in op.waits:
                        e.wait_ge(s, v)
                    ins = op.fn(e)
                    ins.then_inc(op.sem, 16 if op.isdma else 1)
                if name == "sp":
                    for op in self.final:
                        e.wait_ge(op.sem, op.val)

            @block.tensor
            def _(e):
                run(e, "pe")

            @block.scalar
            def _(e):
                run(e, "act")

            @block.vector
            def _(e):
                run(e, "dve")

            @block.gpsimd
            def _(e):
                run(e, "pool")

            @block.sync
            def _(e):
                run(e, "sp")
        self.es.close()
        ctx.stage += 1


def mvec(T, l, v, c, ci):
    g = 8 * v + c
    r, j = divmod(g, 9)
    col = (l * 9 + j) * 2 + ci
    return T[:, r, col:col + 1]


def stage_mod(ctx, cT, mw, mb, modl):
    P = Prog(ctx)
    C = P.sb("C", [128, 8, 2], F32)
    SC = P.sb("SC", [128, 8, 2], F32)
    MB = P.sb("MB", [128, 4, 9], F32)
    W = [P.sb("W%d" % i, [128, 8, 1152], F32) for i in range(2)]
    O = P.sb("O", [128, 4, 9, 2], F32)
    PS = [P.ps("ps%d" % i, [128, 2]) for i in range(4)]
    P.dma("sp", C[:], cT, w=[(C, None)], key="c")
    P.dma("sp", MB[:], mb, w=[(MB, None)], key="mb")
    P.add("act", lambda e: e.activation(out=SC[:], in_=C[:], func=AF.Silu), r=[(C, None)], w=[(SC, None)])
    n = 0
    for l in range(4):
        Wl = W[l % 2]
        P.dma("sp", Wl[:], mw[l].rearrange("(k p) c -> p k c", p=128), w=[(Wl, None)], key=("w", l % 2))
        for j in range(9):
            pt = PS[n % 4]
            n += 1
            for k in range(8):
                P.add("pe", lambda e, pt=pt, Wl=Wl, k=k, j=j: e.matmul(
                    pt[:], lhsT=Wl[:, k, j * 128:(j + 1) * 128], rhs=SC[:, k, :], start=(k == 0), stop=(k == 7)),
                    r=[(Wl, None), (SC, None)], w=[(pt, None)])
            P.add("dve", lambda e, pt=pt, l=l, j=j: e.tensor_scalar(
                out=O[:, l, j, :], in0=pt[:], scalar1=MB[:, l, j:j + 1], scalar2=None, op0=ALU.add),
                r=[(pt, None), (MB, None)], w=[(O, (l, j))])
    P.dma("sp", modl, O[:].rearrange("p l j c -> p (l j c)"), r=[(O, None)], key="o", final=True)
    P.emit()


def stage_tl(ctx, l_pre, l_post, x_in, x_out, modg, pre=None, post=None, chunks=None):
    P = Prog(ctx)
    if chunks is None:
        chunks = [(0, 512, 0), (512, 512, 0), (1024, 512, 0), (1536, 512, 0), (TPC, CPC, 1)]
    has_pre, has_post = pre is not None, post is not None
    HW = NTL
    JG = 6
    onesD = P.sb("onesD", [128, 128], BF16)
    epsc = P.sb("epsc", [128, 1], F32)
    P.add("dve", lambda e: e.memset(onesD[:], 1.0 / D), w=[(onesD, None)])
    P.add("dve", lambda e: e.memset(epsc[:], EPS), w=[(epsc, None)])
    X = P.sb("X", [128, 8, HW], F32)
    H = P.sb("H", [128, 8, HW], BF16)
    A = P.sb("A", [128, JG, HW], BF16)
    WST = P.sb("WST", [128, 2, 8, 256], F32)
    WIN = P.sb("WIN", [128, 2, 8, 256], BF16)
    WOST = P.sb("WOST", [128, 2, 1024], F32)
    WOUT = P.sb("WOUT", [128, 8, 1024], BF16)
    SQ = P.sb("SQ", [128, 2, 512], BF16)
    RS = P.sb("RS", [128, 2, 512], F32)
    TMP = P.sb("TMP", [128, 2, 512], F32)
    SG = P.sb("SG", [128, 2, 512], F32)
    HOF = P.sb("HOF", [128, 2, 512], F32)
    MOD = P.sb("MOD", [128, 8, 72], F32)
    OPS = P.sb("OPS", [128, 8, 72], F32)
    HG = P.sb("HG", [128, 8, 72], F32)
    GEFF = P.sb("GEFF", [128, 2, 8], F32)
    PSV = P.sb("PSV", [128, 8], F32)
    PG = [P.ps("pg%d" % i, [128, 512]) for i in range(2)]
    PU = [P.ps("pu%d" % i, [128, 512]) for i in range(2)]
    PY = [P.ps("py%d" % i, [128, 512]) for i in range(2)]
    PM = [P.ps("pm%d" % i, [128, 512]) for i in range(2)]

    P.dma("sp", MOD[:], modg.rearrange("(r p) c -> p r c", p=128), w=[(MOD, None)], key="mod")
    P.add("dve", lambda e: e.tensor_scalar(out=OPS[:], in0=MOD[:], scalar1=1.0, scalar2=None, op0=ALU.add),
          r=[(MOD, None)], w=[(OPS, None)])
    P.add("dve", lambda e: e.tensor_scalar(out=HG[:], in0=MOD[:], scalar1=0.5, scalar2=None, op0=ALU.mult),
          r=[(MOD, None)], w=[(HG, None)])
    if has_pre:
        P.dma("sp", PSV[:], pre["ps"], w=[(PSV, None)], key="psv")
        for ci in range(2):
            for c in range(8):
                P.add("dve", lambda e, ci=ci, c=c: e.tensor_tensor(out=GEFF[:, ci, c:c + 1], in0=mvec(MOD, l_pre, 5, c, ci),
                                                                 in1=PSV[:, c:c + 1], op=ALU.mult),
                      r=[(MOD, None), (PSV, None)], w=[(GEFF, (ci, c))])
    cnt = {"n": 0, "pm": 0, "pg": 0, "py": 0, "sq": 0}
    for k in range(8):
        P.dma("sp", X[:, k, :], x_in[k * 128:(k + 1) * 128, :], w=[(X, (k, "all"))], key=("x", k))

    def xr(k, off):
        return (X, (k, off))

    def xreads(k, off):
        return [(X, (k, off)), (X, (k, "all"))]

    def modulate(l, v_shift, v_scale, h_out):
        for (off, wd, ci) in chunks:
            sl = slice(off, off + wd)
            pm = PM[cnt["pm"] % 2]
            rs = cnt["pm"] % 2
            cnt["pm"] += 1
            for k in range(8):
                sq = cnt["sq"] % 2
                cnt["sq"] += 1
                P.add("act", lambda e, k=k, sl=sl, wd=wd, sq=sq: e.activation(out=SQ[:, sq, 0:wd], in_=X[:, k, sl], func=AF.Square),
                      r=xreads(k, off), w=[(SQ, sq)])
                P.add("pe", lambda e, k=k, wd=wd, pm=pm, sq=sq: e.matmul(pm[:, 0:wd], lhsT=onesD[:], rhs=SQ[:, sq, 0:wd],
                                                                        start=(k == 0), stop=(k == 7)),
                      r=[(SQ, sq), (onesD, None)], w=[(pm, None)])
            P.add("act", lambda e, wd=wd, pm=pm, rs=rs: e.activation(out=RS[:, rs, 0:wd], in_=pm[:, 0:wd], func=AF.Sqrt,
                                                                     bias=epsc[:], scale=1.0),
                  r=[(pm, None), (epsc, None)], w=[(RS, rs)])
            P.add("dve", lambda e, wd=wd, rs=rs: e.reciprocal(out=RS[:, rs, 0:wd], in_=RS[:, rs, 0:wd]),
                  r=[(RS, rs)], w=[(RS, rs)])
            for k in range(8):
                t = cnt["n"] % 2
                cnt["n"] += 1
                P.add("dve", lambda e, k=k, sl=sl, wd=wd, t=t, ci=ci, rs=rs: e.scalar_tensor_tensor(
                    out=TMP[:, t, 0:wd], in0=X[:, k, sl], scalar=mvec(OPS, l, v_scale, k, ci), in1=RS[:, rs, 0:wd],
                    op0=ALU.mult, op1=ALU.mult),
                    r=xreads(k, off) + [(OPS, None), (RS, rs)], w=[(TMP, t)])
                P.add("act", lambda e, k=k, sl=sl, wd=wd, t=t, ci=ci: e.activation(
                    out=H[:, k, sl], in_=TMP[:, t, 0:wd], func=AF.Identity, bias=mvec(MOD, l, v_shift, k, ci), scale=1.0),
                    r=[(TMP, t), (MOD, None)], w=[(H, (k, off))])
                if h_out is not None:
                    P.add("dve", lambda e, k=k, wd=wd, t=t, ci=ci: e.tensor_scalar(
                        out=HOF[:, t, 0:wd], in0=TMP[:, t, 0:wd], scalar1=mvec(MOD, l, v_shift, k, ci), scalar2=None, op0=ALU.add),
                        r=[(TMP, t), (MOD, None)], w=[(HOF, t)])
                    P.dma("sp", h_out[k * 128:(k + 1) * 128, off:off + wd], HOF[:, t, 0:wd], r=[(HOF, t)],
                          key=("hof", t), final=True)

    def ffn(l, w_in, w_out, v_shift, v_scale, v_gate):
        modulate(l, v_shift, v_scale, None)

        def load_in(j, do_in=True, do_out=True):
            s = j % 2
            jl = j % JG
            if do_in:
                P.dma("sp", WST[:, s, :, 0:128], w_in[:, j * 128:(j + 1) * 128].rearrange("(k p) c -> p k c", p=128),
                      w=[(WST, s)], key=("wst", s))
                P.dma("sp", WST[:, s, :, 128:256],
                      w_in[:, DFF + j * 128:DFF + (j + 1) * 128].rearrange("(k p) c -> p k c", p=128),
                      w=[(WST, (s, 1))], key=("wst", s))
                P.add("pool", lambda e, s=s: e.tensor_copy(out=WIN[:, s], in_=WST[:, s]), r=[(WST, None)], w=[(WIN, s)])
            if do_out:
                P.dma("sp", WOST[:, s, :], w_out[j * 128:(j + 1) * 128, :], w=[(WOST, s)], key=("wost", s))
                P.add("pool", lambda e, s=s, jl=jl: e.tensor_copy(out=WOUT[:, jl, :], in_=WOST[:, s, :]),
                      r=[(WOST, s)], w=[(WOUT, jl)])

        load_in(0)
        for g0 in range(0, NJ, JG):
            gn = min(JG, NJ - g0)
            if g0 > 0:
                load_in(g0, do_in=False, do_out=True)
            for j in range(g0, g0 + gn):
                if j + 1 < NJ:
                    load_in(j + 1, do_in=True, do_out=(j + 1 < g0 + gn))
                s = j % 2
                jl = j % JG
                for (off, wd, ci) in chunks:
                    sl = slice(off, off + wd)
                    b = cnt["pg"] % 2
                    cnt["pg"] += 1
                    for k in range(8):
                        P.add("pe", lambda e, k=k, s=s, sl=sl, wd=wd, b=b: e.matmul(
                            PG[b][:, 0:wd], lhsT=WIN[:, s, k, 0:128], rhs=H[:, k, sl], start=(k == 0), stop=(k == 7)),
                            r=[(WIN, s), (H, (k, off))], w=[(PG[b], None)])
                    for k in range(8):
                        P.add("pe", lambda e, k=k, s=s, sl=sl, wd=wd, b=b: e.matmul(
                            PU[b][:, 0:wd], lhsT=WIN[:, s, k, 128:256], rhs=H[:, k, sl], start=(k == 0), stop=(k == 7)),
                            r=[(WIN, s), (H, (k, off))], w=[(PU[b], None)])
                    P.add("act", lambda e, wd=wd, b=b: e.activation(out=SG[:, b, 0:wd], in_=PG[b][:, 0:wd], func=AF.Silu),
                          r=[(PG[b], None)], w=[(SG, b)])
                    P.add("dve", lambda e, wd=wd, b=b, jl=jl, sl=sl: e.tensor_tensor(
                        out=A[:, jl, sl], in0=SG[:, b, 0:wd], in1=PU[b][:, 0:wd], op=ALU.mult),
                        r=[(SG, b), (PU[b], None)], w=[(A, (jl, off))])
            for mc in range(8):
                for (off, wd, ci) in chunks:
                    sl = slice(off, off + wd)
                    b = cnt["py"] % 2
                    cnt["py"] += 1
                    for jl in range(gn):
                        P.add("pe", lambda e, jl=jl, mc=mc, sl=sl, wd=wd, b=b, gn=gn: e.matmul(
                            PY[b][:, 0:wd], lhsT=WOUT[:, jl, mc * 128:(mc + 1) * 128], rhs=A[:, jl, sl],
                            start=(jl == 0), stop=(jl == gn - 1)),
                            r=[(WOUT, jl), (A, (jl, off))], w=[(PY[b], None)])
                    P.add("dve", lambda e, mc=mc, sl=sl, wd=wd, b=b, ci=ci: e.scalar_tensor_tensor(
                        out=X[:, mc, sl], in0=PY[b][:, 0:wd], scalar=mvec(HG, l, v_gate, mc, ci), in1=X[:, mc, sl],
                        op0=ALU.mult, op1=ALU.add),
                        r=[(PY[b], None), (HG, None)] + xreads(mc, off), w=[(X, (mc, off))])

    if has_pre:
        mT, wo = pre["mT"], pre["wo"]
        for (off, wd, ci) in chunks:
            for q in range(0, wd, 256):
                qw = min(256, wd - q)
                s = (q // 256) % 2
                P.dma("sp", WST[:, s, :, 0:qw], mT[:, off + q:off + q + qw].rearrange("(k p) n -> p k n", p=128),
                      w=[(WST, None)], key=("wst", s))
                P.add("pool", lambda e, s=s, qw=qw, a=off + q: e.tensor_copy(out=H[:, :, a:a + qw], in_=WST[:, s, :, 0:qw]),
                      r=[(WST, None)], w=[(H, None)])
        for k in range(8):
            s = k % 2
            P.dma("sp", WOST[:, s, :], wo[k * 128:(k + 1) * 128, :], w=[(WOST, s)], key=("wost", s))
            P.add("pool", lambda e, s=s, k=k: e.tensor_copy(out=WOUT[:, k, :], in_=WOST[:, s, :]),
                  r=[(WOST, s)], w=[(WOUT, k)])
        for mc in range(8):
            for (off, wd, ci) in chunks:
                sl = slice(off, off + wd)
                b = cnt["py"] % 2
                cnt["py"] += 1
                for k in range(8):
                    P.add("pe", lambda e, k=k, mc=mc, sl=sl, wd=wd, b=b: e.matmul(
                        PY[b][:, 0:wd], lhsT=WOUT[:, k, mc * 128:(mc + 1) * 128], rhs=H[:, k, sl],
                        start=(k == 0), stop=(k == 7)),
                        r=[(WOUT, k), (H, None)], w=[(PY[b], None)])
                P.add("dve", lambda e, mc=mc, sl=sl, wd=wd, b=b, ci=ci: e.scalar_tensor_tensor(
                    out=X[:, mc, sl], in0=PY[b][:, 0:wd], scalar=GEFF[:, ci, mc:mc + 1], in1=X[:, mc, sl],
                    op0=ALU.mult, op1=ALU.add),
                    r=[(PY[b], None), (GEFF, None)] + xreads(mc, off), w=[(X, (mc, off))])
        ffn(l_pre, pre["w_in"], pre["w_out"], 6, 7, 8)
    if has_post:
        ffn(l_post, post["w_in"], post["w_out"], 0, 1, 2)
        modulate(l_post, 3, 4, post["h_out"])
    for k in range(8):
        P.dma("sp", x_out[k * 128:(k + 1) * 128, :], X[:, k, :], r=[(X, None)], key=("xo", k % 2), final=True)
    P.emit()


def stage_proj(ctx, KC, hT, Wd, Gd, groups, M, out_fn, rope=None, chunks=PCH, obf=None):
    P = Prog(ctx)
    g2 = []
    for (c0, nck, part, norm, rp) in groups:
        if norm is None:
            g2 += [(c0 + i * part, 1, part, None, rp) for i in range(nck)]
        else:
            assert nck <= 3
            g2.append((c0, nck, part, norm, rp))
    groups = g2
    nch = sum(g[1] for g in groups)
    WB = P.sb("WB", [128, KC, M], BF16)
    WS = P.sb("WS", [128, 2, KC, 512], F32)
    HS = P.sb("HS", [128, KC, 512], F32)
    HB = P.sb("HB", [128, 2, KC, 512], BF16)
    G = P.sb("G", [128, nch], F32)
    Y = P.sb("Y", [128, 4, 512], F32)
    SQ = P.sb("SQ", [128, 2, 512], BF16)
    RS = P.sb("RS", [128, 2, 512], F32)
    OB = P.sb("OB", [128, 4, 512], F32)
    OBH = P.sb("OBH", [128, 4, 512], BF16)
    if obf is None:
        obf = lambda col: False
    eps = P.sb("eps", [128, 1], F32)
    P.add("dve", lambda e: e.memset(eps[:], EPS), w=[(eps, None)])
    ones = {}
    for (c0, nck, part, norm, rp) in groups:
        if norm == "full" and (nck * part) not in ones:
            t = P.sb("ones%d" % (nck * part), [128, 128], BF16)
            P.add("dve", lambda e, t=t, v=1.0 / (nck * part): e.memset(t[:], v), w=[(t, None)])
            ones[nck * part] = t
        if norm == "blk64" and "b" not in ones:
            t = P.sb("onesb", [128, 128], BF16)
            P.add("dve", lambda e, t=t: e.memset(t[:], 0.0), w=[(t, None)])
            P.add("dve", lambda e, t=t: e.memset(t[0:64, 0:64], 1.0 / 64), r=[(t, None)], w=[(t, None)])
            P.add("dve", lambda e, t=t: e.memset(t[64:128, 64:128], 1.0 / 64), r=[(t, None)], w=[(t, None)])
            ones["b"] = t
    if rope is not None:
        cosd, sind, pswd = rope
        COS = P.sb("COS", [64, 512], F32)
        SIN = P.sb("SIN", [64, 512], F32)
        PSWf = P.sb("PSWf", [64, 64], F32)
        PSW = P.sb("PSW", [64, 64], BF16)
        KRB = P.sb("KRB", [64, 512], BF16)
        T1 = P.sb("T1", [64, 512], F32)
        P.dma("sp", PSWf[:], pswd, w=[(PSWf, None)], key="psw")
        P.add("dve", lambda e: e.tensor_copy(out=PSW[:], in_=PSWf[:]), r=[(PSWf, None)], w=[(PSW, None)])
        PR = P.ps("pr", [64, 512])
    PSA = [P.ps("psa%d" % i, [128, 512]) for i in range(4)]
    PM = [P.ps("pm%d" % i, [128, 512]) for i in range(2)]
    P.dma("sp", G[:], Gd, w=[(G, None)], key="g")
    for si, c0 in enumerate(range(0, M, 512)):
        cw = min(512, M - c0)
        s = si % 2
        P.dma("sp", WS[:, s, :, 0:cw], Wd[:, c0:c0 + cw].rearrange("(k p) c -> p k c", p=128), w=[(WS, s)], key=("ws", s))
        P.add("pool", lambda e, s=s, cw=cw, c0=c0: e.tensor_copy(out=WB[:, :, c0:c0 + cw], in_=WS[:, s, :, 0:cw]),
              r=[(WS, s)], w=[(WB, c0)])
    cnt = {"a": 0, "m": 0, "o": 0}
    for ti, (off, wd) in enumerate(chunks):
        hs = ti % 2
        P.dma("sp", HS[:, :, 0:wd], hT[:, off:off + wd].rearrange("(k p) n -> p k n", p=128), w=[(HS, None)], key="hs")
        P.add("pool", lambda e, hs=hs, wd=wd: e.tensor_copy(out=HB[:, hs, :, 0:wd], in_=HS[:, :, 0:wd]),
              r=[(HS, None)], w=[(HB, hs)])
        if rope is not None:
            P.dma("sp", COS[:, 0:wd], cosd[:, off:off + wd], w=[(COS, None)], key="cos")
            P.dma("sp", SIN[:, 0:wd], sind[:, off:off + wd], w=[(SIN, None)], key="sin")
        gi = 0
        for (c0, nck, part, norm, rp) in groups:
            pm = PM[cnt["m"] % 2]
            rs = cnt["m"] % 2
            if norm is not None:
                cnt["m"] += 1
            ys = []
            for c in range(nck):
                col = c0 + c * part
                pa = PSA[cnt["a"] % 4]
                yi = cnt["a"] % 4
                cnt["a"] += 1
                ys.append(yi)
                for k in range(KC):
                    P.add("pe", lambda e, pa=pa, k=k, col=col, part=part, hs=hs, wd=wd: e.matmul(
                        pa[0:part, 0:wd], lhsT=WB[:, k, col:col + part], rhs=HB[:, hs, k, 0:wd], start=(k == 0), stop=(k == KC - 1)),
                        r=[(WB, None), (HB, hs)], w=[(pa, None)])
                P.add("act", lambda e, pa=pa, yi=yi, part=part, wd=wd: e.activation(
                    out=Y[0:part, yi, 0:wd], in_=pa[0:part, 0:wd], func=AF.Identity),
                    r=[(pa, None)], w=[(Y, yi)])
                if norm is not None:
                    sq = cnt["a"] % 2
                    P.add("act", lambda e, pa=pa, sq=sq, part=part, wd=wd: e.activation(
                        out=SQ[0:part, sq, 0:wd], in_=pa[0:part, 0:wd], func=AF.Square),
                        r=[(pa, None)], w=[(SQ, sq)])
                    om = ones["b"] if norm == "blk64" else ones[nck * part]
                    P.add("pe", lambda e, pm=pm, om=om, sq=sq, part=part, wd=wd, c=c, nck=nck: e.matmul(
                        pm[0:part, 0:wd], lhsT=om[0:part, 0:part], rhs=SQ[0:part, sq, 0:wd], start=(c == 0), stop=(c == nck - 1)),
                        r=[(om, None), (SQ, sq)], w=[(pm, None)])
            if norm is not None:
                P.add("act", lambda e, pm=pm, rs=rs, part=part, wd=wd: e.activation(
                    out=RS[0:part, rs, 0:wd], in_=pm[0:part, 0:wd], func=AF.Sqrt, bias=eps[0:part, :], scale=1.0),
                    r=[(pm, None), (eps, None)], w=[(RS, rs)])
                P.add("dve", lambda e, rs=rs, part=part, wd=wd: e.reciprocal(out=RS[0:part, rs, 0:wd], in_=RS[0:part, rs, 0:wd]),
                      r=[(RS, rs)], w=[(RS, rs)])
            for c in range(nck):
                col = c0 + c * part
                yi = ys[c]
                hb = obf(col)
                if norm is not None:
                    ob = cnt["o"] % 4
                    cnt["o"] += 1
                    ot = OBH if (hb and not rp) else OB
                    P.add("dve", lambda e, yi=yi, ob=ob, rs=rs, part=part, wd=wd, gc=gi + c, ot=ot: e.scalar_tensor_tensor(
                        out=ot[0:part, ob, 0:wd], in0=Y[0:part, yi, 0:wd], scalar=G[0:part, gc:gc + 1], in1=RS[0:part, rs, 0:wd],
                        op0=ALU.mult, op1=ALU.mult),
                        r=[(Y, yi), (G, None), (RS, rs)], w=[(ot, ob)])
                    src = (ot, ob)
                else:
                    assert not hb
                    src = (Y, yi)
                if rp:
                    st, sidx = src
                    P.add("act", lambda e, st=st, sidx=sidx, wd=wd: e.activation(out=KRB[:, 0:wd], in_=st[0:64, sidx, 0:wd], func=AF.Identity),
                          r=[src], w=[(KRB, None)])
                    P.add("pe", lambda e, wd=wd: e.matmul(PR[:, 0:wd], lhsT=PSW[:], rhs=KRB[:, 0:wd], start=True, stop=True),
                          r=[(PSW, None), (KRB, None)], w=[(PR, None)])
                    P.add("dve", lambda e, st=st, sidx=sidx, wd=wd: e.tensor_tensor(out=T1[:, 0:wd], in0=st[0:64, sidx, 0:wd], in1=COS[:, 0:wd], op=ALU.mult),
                          r=[src, (COS, None)], w=[(T1, None)])
                    ob2 = cnt["o"] % 4
                    cnt["o"] += 1
                    P.add("dve", lambda e, ob2=ob2, wd=wd: e.tensor_tensor(out=OB[0:64, ob2, 0:wd], in0=PR[:, 0:wd], in1=SIN[:, 0:wd], op=ALU.mult),
                          r=[(PR, None), (SIN, None)], w=[(OB, ob2)])
                    ot2 = OBH if hb else OB
                    P.add("dve", lambda e, ob2=ob2, wd=wd, ot2=ot2: e.tensor_tensor(out=ot2[0:64, ob2, 0:wd], in0=OB[0:64, ob2, 0:wd], in1=T1[:, 0:wd], op=ALU.add),
                          r=[(OB, ob2), (T1, None)], w=[(ot2, ob2)])
                    src = (ot2, ob2)
                st, sidx = src
                P.dma("sp", out_fn(col, part, off, wd), st[0:part, sidx, 0:wd], r=[src], key=("o", st.name, sidx), final=True)
            gi += nck
    P.emit()


def stage_projT(ctx, KC, hT, Wd, M, out_fn, chunks=PCH, bf16=False):
    P = Prog(ctx)
    WB = P.sb("WB", [128, KC, M], BF16)
    WS = P.sb("WS", [128, 2, KC, 512], F32)
    HS = P.sb("HS", [128, KC, 512], F32)
    HB = P.sb("HB", [128, 2, KC, 512], BF16)
    OB = P.sb("OB", [128, 4, 512], BF16 if bf16 else F32)
    PSA = [P.ps("psa%d" % i, [128, 512]) for i in range(4)]
    for si, c0 in enumerate(range(0, M, 512)):
        cw = min(512, M - c0)
        s = si % 2
        P.dma("sp", WS[:, s, :, 0:cw], Wd[:, c0:c0 + cw].rearrange("(k p) c -> p k c", p=128), w=[(WS, s)], key=("ws", s))
        P.add("pool", lambda e, s=s, cw=cw, c0=c0: e.tensor_copy(out=WB[:, :, c0:c0 + cw], in_=WS[:, s, :, 0:cw]),
              r=[(WS, s)], w=[(WB, c0)])
    n = 0
    for ti, (off, wd) in enumerate(chunks):
        hs = ti % 2
        P.dma("sp", HS[:, :, 0:wd], hT[:, off:off + wd].rearrange("(k p) n -> p k n", p=128), w=[(HS, None)], key="hs")
        P.add("pool", lambda e, hs=hs, wd=wd: e.tensor_copy(out=HB[:, hs, :, 0:wd], in_=HS[:, :, 0:wd]),
              r=[(HS, None)], w=[(HB, hs)])
        for t0 in range(0, wd, 128):
            nt = min(128, wd - t0)
            for c0 in range(0, M, 512):
                cw = min(512, M - c0)
                b = n % 4
                n += 1
                for k in range(KC):
                    P.add("pe", lambda e, b=b, k=k, hs=hs, t0=t0, nt=nt, c0=c0, cw=cw: e.matmul(
                        PSA[b][0:nt, 0:cw], lhsT=HB[:, hs, k, t0:t0 + nt], rhs=WB[:, k, c0:c0 + cw], start=(k == 0), stop=(k == KC - 1)),
                        r=[(HB, hs), (WB, None)], w=[(PSA[b], None)])
                P.add("act", lambda e, b=b, nt=nt, cw=cw: e.activation(out=OB[0:nt, b, 0:cw], in_=PSA[b][0:nt, 0:cw], func=AF.Identity),
                      r=[(PSA[b], None)], w=[(OB, b)])
                dst, src_view = out_fn(off + t0, nt, c0, cw, OB[0:nt, b, 0:cw])
                P.dma("sp", dst, src_view, r=[(OB, b)], key=("o", b), final=True)
    P.emit()


def stage_copy(ctx, moves, zero=()):
    P = Prog(ctx)
    B = [P.sb("B%d" % i, [128, 2048], F32) for i in range(2)]
    Z = P.sb("Z", [128, 2048], F32)
    P.add("dve", lambda e: e.memset(Z[:], 0.0), w=[(Z, None)])
    for i, (dst, src, pp, fr) in enumerate(moves):
        b = B[i % 2]
        P.dma("sp", b[0:pp, 0:fr], src, w=[(b, None)], key=("ld", i % 2))
        P.dma("sp", dst, b[0:pp, 0:fr], r=[(b, None)], key=("st", i % 2), final=True)
    for (dst, pp, fr) in zero:
        P.dma("sp", dst, Z[0:pp, 0:fr], r=[(Z, None)], key="z", final=True)
    P.emit()


KKR = D + 64


def stage_att(ctx, Qd, KKg, VLg, VCg, oT, scale):
    P = Prog(ctx)
    NTT = NCORES * 16
    KN = [P.sb("KN%d" % i, [128, NKEY], BF16) for i in range(2)]
    KR = P.sb("KR", [64, NKEY], BF16)
    V = [P.sb("V%d" % i, [128, NTT, 128], BF16) for i in range(2)]
    KNc = [P.sb("KNc%d" % i, [128, CTX], BF16) for i in range(2)]
    KRc = P.sb("KRc", [64, CTX], BF16)
    Vc = [P.sb("Vc%d" % i, [128, 2, 128], BF16) for i in range(2)]
    QN = P.sb("QN", [128, 2, 512], BF16)
    QR = P.sb("QR", [64, 2, 512], BF16)
    PT = P.sb("PT", [128, 4, 512], BF16)
    ACCd = P.sb("ACCd", [128, 2, 512], F32)
    ACCp = P.sb("ACCp", [128, 2, 512], F32)
    RC = P.sb("RC", [128, 2, 512], F32)
    OS = P.sb("OS", [128, 2, 512], F32)
    ONESf = P.sb("ONESf", [128, 128], F32)
    P.add("dve", lambda e: e.memset(ONESf[:], 1.0), w=[(ONESf, None)])
    PS = [P.ps("ps%d" % i, [128, 512]) for i in range(3)]
    PO = [P.ps("po%d" % i, [128, 512]) for i in range(2)]
    PSM = [P.ps("psm%d" % i, [128, 512]) for i in range(2)]
    cn = {"pt": 0, "ps": 0, "q": 0}

    for r in range(NCORES):
        P.dma("sp", KR[:, r * NTL:(r + 1) * NTL], KKg[r * KKR + D:r * KKR + D + 64, :], w=[(KR, r)], key=("kr", r))
    for r in range(NCORES):
        P.add("pool", lambda e, r=r: e.tensor_copy(out=KRc[:, r * CPC:(r + 1) * CPC], in_=KR[:, r * NTL + TPC:(r + 1) * NTL]),
              r=[(KR, r)], w=[(KRc, r)])

    def load_head(h):
        b = h % 2
        for r in range(NCORES):
            P.dma("sp", KN[b][:, r * NTL:(r + 1) * NTL], KKg[r * KKR + h * 128:r * KKR + (h + 1) * 128, :],
                  w=[(KN[b], r)], key=("kn", b, r))
            P.dma("sp", V[b][:, r * 16:(r + 1) * 16, :],
                  VLg[r * D + h * 128:r * D + (h + 1) * 128, :].rearrange("p (t d) -> p t d", d=128),
                  w=[(V[b], r)], key=("v", b, r))
            P.dma("sp", Vc[b][(r % 4) * 32:(r % 4) * 32 + 32, r // 4, :], VCg[r * CTX + h * CPC:r * CTX + (h + 1) * CPC, :],
                  w=[(Vc[b], r)], key=("vc", b, r))
        for r in range(NCORES):
            P.add("pool", lambda e, r=r, b=b: e.tensor_copy(out=KNc[b][:, r * CPC:(r + 1) * CPC],
                                                             in_=KN[b][:, r * NTL + TPC:(r + 1) * NTL]),
                  r=[(KN[b], r)], w=[(KNc[b], r)])

    load_head(0)
    for h in range(8):
        hb = h % 2
        if h + 1 < 8:
            load_head(h + 1)
        lat_tiles = [(KN[hb], r * NTL + t * 128, KR, r * NTL + t * 128, V[hb], r * 16 + t) for r in range(NCORES) for t in range(16)]
        ctx_tiles = [(KNc[hb], t * 128, KRc, t * 128, Vc[hb], t) for t in range(2)]
        for (off, wd) in PCH:
            is_ctx = off >= TPC
            tiles = ctx_tiles if is_ctx else lat_tiles + ctx_tiles
            nkt = len(tiles)
            s = cn["q"] % 2
            cn["q"] += 1
            P.dma("sp", QN[:, s, 0:wd], Qd[h * 192:h * 192 + 128, off:off + wd], w=[(QN, s)], key=("qn", s))
            P.dma("sp", QR[:, s, 0:wd], Qd[h * 192 + 128:(h + 1) * 192, off:off + wd], w=[(QR, s)], key=("qr", s))
            po = PO[s]
            psm = PSM[s]

            def qk(ti, b, wd=wd, s=s, tiles=tiles):
                kn_t, kc0, kr_t, rc0, _, _ = tiles[ti]
                P.add("pe", lambda e, b=b, wd=wd, s=s, kn_t=kn_t, kc0=kc0: e.matmul(
                    PS[b][:, 0:wd], lhsT=kn_t[:, kc0:kc0 + 128], rhs=QN[:, s, 0:wd], start=True, stop=False),
                    r=[(kn_t, None), (QN, s)], w=[(PS[b], None)])
                P.add("pe", lambda e, b=b, wd=wd, s=s, kr_t=kr_t, rc0=rc0: e.matmul(
                    PS[b][:, 0:wd], lhsT=kr_t[:, rc0:rc0 + 128], rhs=QR[:, s, 0:wd], start=False, stop=True),
                    r=[(kr_t, None), (QR, s)], w=[(PS[b], None)])

            bs = []
            b0 = cn["ps"] % 3
            cn["ps"] += 1
            qk(0, b0)
            bs.append(b0)
            for ti in range(nkt):
                if ti + 1 < nkt:
                    b1 = cn["ps"] % 3
                    cn["ps"] += 1
                    qk(ti + 1, b1)
                    bs.append(b1)
                b = bs[ti]
                p = cn["pt"] % 4
                cn["pt"] += 1
                v_t, vt = tiles[ti][4], tiles[ti][5]
                P.add("act", lambda e, b=b, p=p, wd=wd: e.activation(out=PT[:, p, 0:wd], in_=PS[b][:, 0:wd], func=AF.Exp, scale=scale),
                      r=[(PS[b], None)], w=[(PT, p)])
                P.add("pe", lambda e, ti=ti, p=p, wd=wd, po=po, nkt=nkt, v_t=v_t, vt=vt: e.matmul(
                    po[:, 0:wd], lhsT=v_t[:, vt, :], rhs=PT[:, p, 0:wd], start=(ti == 0), stop=(ti == nkt - 1)),
                    r=[(v_t, None), (PT, p)], w=[(po, None)])
                eng, acc = ("dve", ACCd) if ti % 2 == 0 else ("pool", ACCp)
                if ti < 2:
                    P.add(eng, lambda e, p=p, wd=wd, s=s, acc=acc: e.tensor_copy(out=acc[:, s, 0:wd], in_=PT[:, p, 0:wd]),
                          r=[(PT, p)], w=[(acc, s)])
                else:
                    P.add(eng, lambda e, p=p, wd=wd, s=s, acc=acc: e.tensor_tensor(out=acc[:, s, 0:wd], in0=PT[:, p, 0:wd],
                                                                                 in1=acc[:, s, 0:wd], op=ALU.add),
                          r=[(PT, p), (acc, s)], w=[(acc, s)])
            P.add("pe", lambda e, wd=wd, s=s, psm=psm: e.matmul(psm[:, 0:wd], lhsT=ONESf[:], rhs=ACCd[:, s, 0:wd], start=True, stop=False),
                  r=[(ONESf, None), (ACCd, s)], w=[(psm, None)])
            P.add("pe", lambda e, wd=wd, s=s, psm=psm: e.matmul(psm[:, 0:wd], lhsT=ONESf[:], rhs=ACCp[:, s, 0:wd], start=False, stop=True),
                  r=[(ONESf, None), (ACCp, s)], w=[(psm, None)])
            P.add("dve", lambda e, s=s, wd=wd, psm=psm: e.reciprocal(out=RC[:, s, 0:wd], in_=psm[:, 0:wd]), r=[(psm, None)], w=[(RC, s)])
            P.add("dve", lambda e, s=s, wd=wd, po=po: e.tensor_tensor(out=OS[:, s, 0:wd], in0=po[:, 0:wd], in1=RC[:, s, 0:wd], op=ALU.mult),
                  r=[(po, None), (RC, s)], w=[(OS, s)])
            P.dma("sp", oT[h * 128:(h + 1) * 128, off:off + wd], OS[:, s, 0:wd], r=[(OS, s)], key=("os", s), final=True)
    P.emit()


def _select(P, eng, out, srcs, oh, r_lists, w_list, first_is_write=True):
    for r, src in enumerate(srcs):
        if r == 0:
            P.add(eng, lambda e, src=src: e.tensor_scalar(out=out, in0=src, scalar1=oh[:, 0:1], scalar2=None, op0=ALU.mult),
                  r=r_lists, w=w_list)
        else:
            P.add(eng, lambda e, src=src, r=r: e.scalar_tensor_tensor(out=out, in0=src, scalar=oh[:, r:r + 1], in1=out,
                                                                      op0=ALU.mult, op1=ALU.add),
                  r=r_lists + w_list, w=w_list)


def stage_pool(ctx, hT, edgeg, rc, ohLd, ohRd, mT):
    P = Prog(ctx)
    L = TPC + 16 + CPC + 16
    segs = [(0, TPC, 0), (TPC + 16, CPC, TPC)]
    EG = P.sb("EG", [128, NCORES, 8, 32], F32)
    OHL = P.sb("OHL", [128, NCORES], F32)
    OHR = P.sb("OHR", [128, NCORES], F32)
    HAL = P.sb("HAL", [128, 4, 8, 8], F32)
    A = [P.sb("A%d" % i, [128, L], F32) for i in range(2)]
    B = [P.sb("Bw%d" % i, [128, L], F32) for i in range(2)]
    RCs = [P.sb("RC%d" % i, [128, NTL], F32) for i in range(2)]
    O = [P.sb("O%d" % i, [128, NTL], F32) for i in range(2)]
    P.dma("sp", EG[:], edgeg.rearrange("(r k p) c -> p r k c", r=NCORES, p=128), w=[(EG, None)], key="eg")
    P.dma("sp", OHL[:], ohLd, w=[(OHL, None)], key="ohl")
    P.dma("sp", OHR[:], ohRd, w=[(OHR, None)], key="ohr")
    for hi, (oh, c0) in enumerate(((OHL, 8), (OHR, 0), (OHL, 24), (OHR, 16))):
        _select(P, "dve", HAL[:, hi], [EG[:, r, :, c0:c0 + 8] for r in range(NCORES)], oh,
                [(EG, None), (oh, None)], [(HAL, hi)])
    for k in range(8):
        g = k // 2
        a = A[k % 2]
        r_ = RCs[k % 2]
        o_ = O[k % 2]
        P.dma("sp", a[:, 8:8 + TPC], hT[k * 128:(k + 1) * 128, 0:TPC], w=[(a, "m")], key=("a", k % 2))
        P.dma("sp", a[:, TPC + 24:TPC + 24 + CPC], hT[k * 128:(k + 1) * 128, TPC:NTL], w=[(a, "c")], key=("a", k % 2))
        P.dma("sp", r_[:], rc[k * 128:(k + 1) * 128, :], w=[(r_, None)], key=("r", k % 2))
        for hi, c0 in ((0, 0), (1, TPC + 8), (2, TPC + 16), (3, TPC + 24 + CPC)):
            P.add("pool", lambda e, a=a, hi=hi, c0=c0, k=k: e.tensor_copy(out=a[:, c0:c0 + 8], in_=HAL[:, hi, k, :]),
                  r=[(HAL, hi)], w=[(a, ("h", hi))])
        cur = a
        lo, hi_ = 0, L
        for lev in range(g + 1):
            sh = 1 if lev == 0 else (1 << (lev - 1))
            dst = B[lev % 2]
            if lev == 0:
                nlo, nhi = lo + 1, hi_
                P.add("dve", lambda e, cur=cur, dst=dst, nlo=nlo, nhi=nhi: e.tensor_tensor(
                    out=dst[:, nlo:nhi], in0=cur[:, nlo - 1:nhi - 1], in1=cur[:, nlo:nhi], op=ALU.add),
                    r=[(cur, None)], w=[(dst, None)])
            else:
                nlo, nhi = lo + sh, hi_ - sh
                P.add("dve", lambda e, cur=cur, dst=dst, nlo=nlo, nhi=nhi, sh=sh: e.tensor_tensor(
                    out=dst[:, nlo:nhi], in0=cur[:, nlo - sh:nhi - sh], in1=cur[:, nlo + sh:nhi + sh], op=ALU.add),
                    r=[(cur, None)], w=[(dst, None)])
            cur, lo, hi_ = dst, nlo, nhi
        for (b, n, oo) in segs:
            P.add("dve", lambda e, cur=cur, b=b, n=n, oo=oo, o_=o_, r_=r_: e.tensor_tensor(
                out=o_[:, oo:oo + n], in0=cur[:, b + 8:b + 8 + n], in1=r_[:, oo:oo + n], op=ALU.mult),
                r=[(cur, None), (r_, None)], w=[(o_, oo)])
            P.add("dve", lambda e, a=a, b=b, n=n, oo=oo, o_=o_: e.tensor_tensor(
                out=o_[:, oo:oo + n], in0=o_[:, oo:oo + n], in1=a[:, b + 8:b + 8 + n], op=ALU.subtract),
                r=[(o_, oo), (a, None)], w=[(o_, oo)])
        P.dma("sp", mT[k * 128:(k + 1) * 128, :], o_[:], r=[(o_, None)], key=("o", k % 2), final=True)
    P.emit()


def stage_conve(ctx, bcu, ecg, cw, ohLd, ohRd, mT):
    P = Prog(ctx)
    n = TPC
    ECS = P.sb("ECS", [128, NCORES, 16, 2], F32)
    OHL = P.sb("OHL", [128, NCORES], F32)
    OHR = P.sb("OHR", [128, NCORES], F32)
    HL = P.sb("HL", [128, 16, 1], F32)
    HR = P.sb("HR", [128, 16, 1], F32)
    CW = P.sb("CW", [128, 8, 3], F32)
    Bt = [P.sb("B%d" % i, [128, n], F32) for i in range(2)]
    Ct = [P.sb("C%d" % i, [128, n + 2], F32) for i in range(2)]
    Ut = [P.sb("U%d" % i, [128, n + 2], F32) for i in range(2)]
    Z = [P.sb("Z%d" % i, [128, n], F32) for i in range(2)]
    P.dma("sp", CW[:], cw, w=[(CW, None)], key="cw")
    P.dma("sp", ECS[:], ecg.rearrange("(r k p) c -> p r k c", r=NCORES, p=128), w=[(ECS, None)], key="ecs")
    P.dma("sp", OHL[:], ohLd, w=[(OHL, None)], key="ohl")
    P.dma("sp", OHR[:], ohRd, w=[(OHR, None)], key="ohr")
    _select(P, "dve", HL[:], [ECS[:, r, :, 1:2] for r in range(NCORES)], OHL, [(ECS, None), (OHL, None)], [(HL, None)])
    _select(P, "dve", HR[:], [ECS[:, r, :, 0:1] for r in range(NCORES)], OHR, [(ECS, None), (OHR, None)], [(HR, None)])
    for k in range(8):
        s = k % 2
        b_, c_, u_, z_ = Bt[s], Ct[s], Ut[s], Z[s]
        P.dma("sp", b_[:], bcu[k * 128:(k + 1) * 128, 0:n], w=[(b_, None)], key=("b", s))
        P.dma("sp", c_[:, 1:n + 1], bcu[D + k * 128:D + (k + 1) * 128, 0:n], w=[(c_, "m")], key=("c", s))
        P.dma("sp", u_[:, 1:n + 1], bcu[2 * D + k * 128:2 * D + (k + 1) * 128, 0:n], w=[(u_, "m")], key=("u", s))
        for (t_, kk) in ((c_, k), (u_, 8 + k)):
            P.add("pool", lambda e, t_=t_, kk=kk: e.tensor_copy(out=t_[:, 0:1], in_=HL[:, kk, :]), r=[(HL, None)], w=[(t_, "l")])
            P.add("pool", lambda e, t_=t_, kk=kk: e.tensor_copy(out=t_[:, n + 1:n + 2], in_=HR[:, kk, :]), r=[(HR, None)], w=[(t_, "r")])
        P.add("dve", lambda e, c_=c_, u_=u_: e.tensor_tensor(out=c_[:], in0=c_[:], in1=u_[:], op=ALU.mult),
              r=[(c_, None), (u_, None)], w=[(c_, None)])
        P.add("dve", lambda e, c_=c_, z_=z_, k=k: e.tensor_scalar(out=z_[:], in0=c_[:, 0:n], scalar1=CW[:, k, 0:1], scalar2=None, op0=ALU.mult),
              r=[(c_, None), (CW, None)], w=[(z_, None)])
        for j in (1, 2):
            P.add("dve", lambda e, c_=c_, z_=z_, k=k, j=j: e.scalar_tensor_tensor(
                out=z_[:], in0=c_[:, j:j + n], scalar=CW[:, k, j:j + 1], in1=z_[:], op0=ALU.mult, op1=ALU.add),
                r=[(c_, None), (CW, None), (z_, None)], w=[(z_, None)])
        P.add("dve", lambda e, b_=b_, z_=z_: e.tensor_tensor(out=z_[:], in0=z_[:], in1=b_[:], op=ALU.mult),
              r=[(z_, None), (b_, None)], w=[(z_, None)])
        P.dma("sp", mT[k * 128:(k + 1) * 128, 0:n], z_[:], r=[(z_, None)], key=("z", s), final=True)
    P.emit()


NA_H = 16
NA_EXT = 2560
NA_NK = NA_EXT + CTX
NA_NT = NA_NK // 128
NA_HW = 512 + CPC


def stage_natt(ctx, qk, Vna, KHg, VHg, bias, ident, ohLd, ohRd, mT):
    P = Prog(ctx)
    KT = P.sb("KT", [128, 8, NA_NK], BF16)
    QT = P.sb("QT", [128, 8, TPC], BF16)
    VB = P.sb("VB", [128, NA_NT, D], BF16)
    ST = P.sb("ST", [128, 2, 1024], F32)
    HS = P.sb("HS", [128, NCORES, NA_HW], F32)
    ACC = P.sb("ACC", [128, 2, 512], F32)
    VCS = P.sb("VCS", [128, 512], F32)
    BS = P.sb("BS", [128, 1536], F32)
    BB = P.sb("BB", [128, 2, 3 * 6 * 256], BF16)
    IDf = P.sb("IDf", [128, 128], F32)
    ID = P.sb("ID", [128, 128], BF16)
    ONES = P.sb("ONES", [128, 64], BF16)
    OHL = P.sb("OHL", [128, NCORES], F32)
    OHR = P.sb("OHR", [128, NCORES], F32)
    PT = P.sb("PT", [128, 3, 256], BF16)
    RC = P.sb("RC", [64, 2, 256], F32)
    OS = P.sb("OS", [64, 2, 256], F32)
    PS = [P.ps("ps%d" % i, [128, 512]) for i in range(3)]
    PO = [P.ps("po%d" % i, [128, 512]) for i in range(2)]
    PSM = [P.ps("psm%d" % i, [128, 512]) for i in range(2)]
    P.add("dve", lambda e: e.memset(ONES[:], 1.0), w=[(ONES, None)])
    P.dma("sp", IDf[:], ident, w=[(IDf, None)], key="id")
    P.add("dve", lambda e: e.tensor_copy(out=ID[:], in_=IDf[:]), r=[(IDf, None)], w=[(ID, None)])
    P.dma("sp", OHL[:], ohLd, w=[(OHL, None)], key="ohl")
    P.dma("sp", OHR[:], ohRd, w=[(OHR, None)], key="ohr")
    n = 0
    KHv = KHg.rearrange("(r k p) c -> p r k c", r=NCORES, p=128)
    for k in range(8):
        for c0 in range(0, TPC, 1024):
            s = n % 2
            n += 1
            P.dma("sp", ST[:, s, :], qk[D + k * 128:D + (k + 1) * 128, c0:c0 + 1024], w=[(ST, s)], key=("st", s))
            P.add("pool", lambda e, s=s, c0=c0, k=k: e.tensor_copy(out=KT[:, k, 256 + c0:256 + c0 + 1024], in_=ST[:, s, :]),
                  r=[(ST, s)], w=[(KT, (k, "o", c0))])
            s = n % 2
            n += 1
            P.dma("sp", ST[:, s, :], qk[k * 128:(k + 1) * 128, c0:c0 + 1024], w=[(ST, s)], key=("st", s))
            P.add("pool", lambda e, s=s, c0=c0, k=k: e.tensor_scalar(out=QT[:, k, c0:c0 + 1024], in0=ST[:, s, :], scalar1=0.125,
                                                                    scalar2=None, op0=ALU.mult),
                  r=[(ST, s)], w=[(QT, (k, c0))])
        P.dma("sp", HS[:], KHv[:, :, k, :], w=[(HS, None)], key="hs")
        _select(P, "dve", ACC[:, 0, 0:256], [HS[:, r, 256:512] for r in range(NCORES)], OHL, [(HS, None), (OHL, None)], [(ACC, 0)])
        _select(P, "dve", ACC[:, 1, 0:256], [HS[:, r, 0:256] for r in range(NCORES)], OHR, [(HS, None), (OHR, None)], [(ACC, 1)])
        P.add("act", lambda e, k=k: e.activation(out=KT[:, k, 0:256], in_=ACC[:, 0, 0:256], func=AF.Identity),
              r=[(ACC, 0)], w=[(KT, (k, "a"))])
        P.add("act", lambda e, k=k: e.activation(out=KT[:, k, 2304:2560], in_=ACC[:, 1, 0:256], func=AF.Identity),
              r=[(ACC, 1)], w=[(KT, (k, "b"))])
        P.add("act", lambda e, k=k: e.activation(out=KT[:, k, NA_EXT:NA_NK].rearrange("p (r c) -> p r c", c=CPC),
                                                 in_=HS[:, :, 512:NA_HW], func=AF.Identity),
              r=[(HS, None)], w=[(KT, (k, "c"))])
    for t in range(16):
        for hh in range(0, D, 512):
            s = n % 2
            n += 1
            P.dma("sp", ST[:, s, 0:512], Vna[t * 128:(t + 1) * 128, hh:hh + 512], w=[(ST, s)], key=("st", s))
            P.add("pool", lambda e, s=s, t=t, hh=hh: e.tensor_copy(out=VB[:, 2 + t, hh:hh + 512], in_=ST[:, s, 0:512]),
                  r=[(ST, s)], w=[(VB, (2 + t, hh))])
    for (tile0, row0, oh) in ((0, 256, OHL), (18, 0, OHR)):
        for tt in range(2):
            for hh in range(0, D, 512):
                for r in range(NCORES):
                    P.dma("sp", HS[:, r, 0:512], VHg[r * NA_HW + row0 + tt * 128:r * NA_HW + row0 + (tt + 1) * 128, hh:hh + 512],
                          w=[(HS, r)], key=("hsv", r))
                a = n % 2
                n += 1
                _select(P, "dve", ACC[:, a, :], [HS[:, r, 0:512] for r in range(NCORES)], oh, [(HS, None), (oh, None)], [(ACC, a)])
                P.add("act", lambda e, a=a, tile0=tile0, tt=tt, hh=hh: e.activation(
                    out=VB[:, tile0 + tt, hh:hh + 512], in_=ACC[:, a, :], func=AF.Identity),
                    r=[(ACC, a)], w=[(VB, (tile0 + tt, hh))])
    for hh in range(0, D, 512):
        for t2 in range(2):
            for r4 in range(4):
                r = t2 * 4 + r4
                P.dma("sp", VCS[r4 * 32:(r4 + 1) * 32, :], VHg[r * NA_HW + 512:r * NA_HW + 512 + CPC, hh:hh + 512],
                      w=[(VCS, r4)], key="vcs")
            P.add("pool", lambda e, t2=t2, hh=hh: e.tensor_copy(out=VB[:, 20 + t2, hh:hh + 512], in_=VCS[:]),
                  r=[(VCS, None)], w=[(VB, (20 + t2, hh))])
    pn = 0
    tn = 0
    on = 0
    for h in range(NA_H):
        kc = h // 2
        p0 = (h % 2) * 64
        bb = h % 2
        for var in range(3):
            P.dma("sp", BS[:], bias[h, :, var * 1536:(var + 1) * 1536], w=[(BS, None)], key="bs")
            P.add("pool", lambda e, bb=bb, var=var: e.tensor_copy(out=BB[:, bb, var * 1536:(var + 1) * 1536], in_=BS[:]),
                  r=[(BS, None)], w=[(BB, (bb, var))])
        for b in range(8):
            var = 0 if b == 0 else (2 if b == 7 else 1)
            q0 = b * 256
            po = PO[on % 2]
            psm = PSM[on % 2]
            osl = on % 2
            on += 1
            tiles = [("l", t) for t in range(6)] + [("c", 0), ("c", 1)]
            for ti, (kind, t) in enumerate(tiles):
                ps = PS[pn % 3]
                pn += 1
                if kind == "l":
                    k0 = (4 * b + 2 * t) * 64
                else:
                    k0 = NA_EXT + t * 128
                vt = k0 // 128
                P.add("pe", lambda e, ps=ps, kc=kc, p0=p0, k0=k0, q0=q0, kind=kind: e.matmul(
                    ps[:, 0:256], lhsT=KT[p0:p0 + 64, kc, k0:k0 + 128], rhs=QT[p0:p0 + 64, kc, q0:q0 + 256],
                    start=True, stop=(kind == "c")),
                    r=[(KT, None), (QT, None)], w=[(ps, None)])
                if kind == "l":
                    bo = (var * 6 + t) * 256
                    P.add("pe", lambda e, ps=ps, bb=bb, bo=bo: e.matmul(
                        ps[:, 0:256], lhsT=ID[:], rhs=BB[:, bb, bo:bo + 256], start=False, stop=True),
                        r=[(ID, None), (BB, (bb, var))], w=[(ps, None)])
                pt = tn % 3
                tn += 1
                P.add("act", lambda e, ps=ps, pt=pt: e.activation(out=PT[:, pt, :], in_=ps[:, 0:256], func=AF.Exp),
                      r=[(ps, None)], w=[(PT, pt)])
                P.add("pe", lambda e, po=po, vt=vt, h=h, pt=pt, ti=ti: e.matmul(
                    po[0:64, 0:256], lhsT=VB[:, vt, h * 64:(h + 1) * 64], rhs=PT[:, pt, :], start=(ti == 0), stop=(ti == 7)),
                    r=[(VB, None), (PT, pt)], w=[(po, None)])
                P.add("pe", lambda e, psm=psm, pt=pt, ti=ti: e.matmul(
                    psm[0:64, 0:256], lhsT=ONES[:], rhs=PT[:, pt, :], start=(ti == 0), stop=(ti == 7)),
                    r=[(ONES, None), (PT, pt)], w=[(psm, None)])
            P.add("dve", lambda e, psm=psm, osl=osl: e.reciprocal(out=RC[:, osl, :], in_=psm[0:64, 0:256]),
                  r=[(psm, None)], w=[(RC, osl)])
            P.add("dve", lambda e, po=po, osl=osl: e.tensor_tensor(out=OS[:, osl, :], in0=po[0:64, 0:256], in1=RC[:, osl, :], op=ALU.mult),
                  r=[(po, None), (RC, osl)], w=[(OS, osl)])
            P.dma("sp", mT[h * 64:(h + 1) * 64, q0:q0 + 256], OS[:, osl, :], r=[(OS, osl)], key=("os", osl), final=True)
    P.emit()


def stage_edges(ctx, src, nchunk, specs, dst, width):
    P = Prog(ctx)
    E = P.sb("E", [128, nchunk, width], F32)
    sv = src.rearrange("(k p) n -> p k n", p=128)
    for i, (c0, n, d0) in enumerate(specs):
        P.dma("sp", E[:, :, d0:d0 + n], sv[:, :, c0:c0 + n], w=[(E, i)], key="ld", allow_slow_non_contiguous=True)
    P.dma("sp", dst.rearrange("(k p) c -> p k c", p=128), E[:], r=[(E, None)], key="st", final=True,
          allow_slow_non_contiguous=True)
    P.emit()


def build_fused():
    ctx = Ctx()
    ext = lambda name, shape: ctx.dram(name, shape, F32, "ExternalInput")
    xT = ext("xT", [D, NTL])
    cT = ext("cT", [128, 8, 2]); mw = ext("mw", [4, D, 1152]); mb = ext("mb", [128, 4, 9])
    w_in = ext("ffn_w_in", [4, 2, D, 2 * DFF]); w_out = ext("ffn_w_out", [4, 2, DFF, D])
    Wlat = ext("Wlat", [D, 704]); Glat = ext("Glat", [128, 6])
    w_uq = ext("w_uq", [384, 1536]); Gq = ext("Gq", [128, 16])
    w_uk = ext("w_uk", [256, D]); Gk = ext("Gk", [128, 8]); w_uv = ext("w_uv", [256, D])
    cos = ext("cos", [64, NTL]); sin = ext("sin", [64, NTL]); psw = ext("psw", [64, 64])
    mla_w_o = ext("mla_w_o", [D, D]); ones8 = ext("ones8", [128, 8])
    wbd = ext("wbd", [D, D]); psv = ext("psv", [128, 8]); rc = ext("rc", [D, NTL])
    ohL = ext("ohL", [128, NCORES]); ohR = ext("ohR", [128, NCORES])
    w_qkv = ext("w_qkv", [D, 3 * D]); Gna = ext("Gna", [128, 16])
    bias = ext("bias", [NA_H, 128, 3 * 6 * 256]); ident = ext("ident", [128, 128]); na_w_o = ext("na_w_o", [D, D])
    conv_w_in = ext("conv_w_in", [D, 3 * D]); Gc = ext("Gc", [128, 24]); cw = ext("cw", [128, 8, 3]); conv_w_out = ext("conv_w_out", [D, D])
    xo = ctx.dram("xo", [D, NTL], F32, "ExternalOutput")

    modl = ctx.dram("modl", [128, 72]); modg = ctx.dram("modg", [NCORES * 128, 72])
    xa = ctx.dram("xa", [D, NTL]); xb = ctx.dram("xb", [D, NTL]); h = ctx.dram("h", [D, NTL])
    lat = ctx.dram("lat", [640, NTL]); Qd = ctx.dram("Qd", [1536, NTL], BF16)
    KK = ctx.dram("KK", [KKR, NTL], BF16)
    VL = ctx.dram("VL", [D, TPC], BF16)
    VC = ctx.dram("VC", [8 * CPC, 128], BF16)
    KKg = ctx.dram("KKg", [NCORES * KKR, NTL], BF16)
    VLg = ctx.dram("VLg", [NCORES * D, TPC], BF16); VCg = ctx.dram("VCg", [NCORES * 8 * CPC, 128], BF16)
    oT = ctx.dram("oT", [D, NTL])
    edgeP = ctx.dram("edgeP", [D, 32]); edgePg = ctx.dram("edgePg", [NCORES * D, 32])
    mTp = ctx.dram("mTp", [D, NTL])
    qk = ctx.dram("qk", [2 * D, NTL]); Vna = ctx.dram("Vna", [NTL, D])
    KH = ctx.dram("KH", [D, NA_HW]); VH = ctx.dram("VH", [NA_HW, D])
    KHg = ctx.dram("KHg", [NCORES * D, NA_HW]); VHg = ctx.dram("VHg", [NCORES * NA_HW, D])
    mTn = ctx.dram("mTn", [D, NTL])
    bcu = ctx.dram("bcu", [3 * D, NTL])
    EC = ctx.dram("EC", [2 * D, 2]); ECg = ctx.dram("ECg", [NCORES * 2 * D, 2])
    mTc = ctx.dram("mTc", [D, NTL])

    stage_mod(ctx, cT, mw, mb, modl)
    ctx.allgather(modl[:, :], modg[:, :])
    stage_tl(ctx, None, 0, xT, xa, modg, post=dict(w_in=w_in[0, 0], w_out=w_out[0, 0], h_out=h))

    def lat_out(col, part, off, wd):
        if col < 640:
            return lat[col:col + part, off:off + wd]
        return KK[D:D + 64, off:off + wd]
    rope = (cos, sin, psw)
    stage_proj(ctx, 8, h, Wlat, Glat, [(0, 3, 128, "full", False), (384, 2, 128, "full", False), (640, 1, 64, "full", True)],
               704, lat_out, rope, obf=lambda col: col >= 640)
    gq = []
    for hh in range(8):
        gq += [(hh * 192, 1, 128, "full", False), (hh * 192 + 128, 1, 64, "full", True)]
    stage_proj(ctx, 3, lat[0:384, :], w_uq, Gq, gq, 1536, lambda col, part, off, wd: Qd[col:col + part, off:off + wd], rope,
               obf=lambda col: True)
    stage_proj(ctx, 2, lat[384:640, :], w_uk, Gk, [(hh * 128, 1, 128, "full", False) for hh in range(8)], D,
               lambda col, part, off, wd: KK[col:col + part, off:off + wd], obf=lambda col: True)

    def v_out(tok0, nt, c0, cw, src):
        h0, nh = c0 // 128, cw // 128
        sv = src.rearrange("p (h d) -> p h d", d=128)
        if tok0 < TPC:
            t = tok0 // 128
            return VL[h0 * 128:(h0 + nh) * 128, t * 128:(t + 1) * 128].rearrange("(h p) d -> p h d", p=128), sv
        return VC[h0 * CPC:(h0 + nh) * CPC, :].rearrange("(h p) d -> p h d", p=CPC), sv
    stage_projT(ctx, 2, lat[384:640, :], w_uv, D, v_out, bf16=True)
    ctx.allgather(KK[:, :], KKg[:, :])
    ctx.allgather(VL[:, :], VLg[:, :])
    ctx.allgather(VC[:, :], VCg[:, :])
    stage_att(ctx, Qd, KKg, VLg, VCg, oT, float(192 ** -0.5))
    stage_tl(ctx, 0, 1, xa, xb, modg, pre=dict(mT=oT, wo=mla_w_o, ps=ones8, w_in=w_in[0, 1], w_out=w_out[0, 1]),
             post=dict(w_in=w_in[1, 0], w_out=w_out[1, 0], h_out=h))

    stage_edges(ctx, h, 8, [(0, 8, 0), (TPC - 8, 8, 8), (TPC, 8, 16), (NTL - 8, 8, 24)], edgeP, 32)
    ctx.allgather(edgeP[:, :], edgePg[:, :])
    stage_pool(ctx, h, edgePg, rc, ohL, ohR, mTp)
    stage_tl(ctx, 1, 2, xb, xa, modg, pre=dict(mT=mTp, wo=wbd, ps=psv, w_in=w_in[1, 1], w_out=w_out[1, 1]),
             post=dict(w_in=w_in[2, 0], w_out=w_out[2, 0], h_out=h))

    stage_proj(ctx, 8, h, w_qkv[:, 0:2 * D], Gna, [(cc * 128, 1, 128, "blk64", False) for cc in range(16)], 2 * D,
               lambda col, part, off, wd: qk[col:col + part, off:off + wd])
    stage_projT(ctx, 8, h, w_qkv[:, 2 * D:3 * D], D,
                lambda tok0, nt, c0, cw, src: (Vna[tok0:tok0 + nt, c0:c0 + cw], src))
    stage_edges(ctx, qk[D:2 * D, :], 8, [(0, 256, 0), (TPC - 256, 256, 256), (TPC, CPC, 512)], KH, NA_HW)
    stage_copy(ctx, [(VH[0:128, :], Vna[0:128, :], 128, D), (VH[128:256, :], Vna[128:256, :], 128, D),
                     (VH[256:384, :], Vna[TPC - 256:TPC - 128, :], 128, D), (VH[384:512, :], Vna[TPC - 128:TPC, :], 128, D),
                     (VH[512:NA_HW, :], Vna[TPC:NTL, :], CPC, D)],
               zero=[(mTn[k * 128:(k + 1) * 128, TPC:NTL], 128, CPC) for k in range(8)]
               + [(mTc[k * 128:(k + 1) * 128, TPC:NTL], 128, CPC) for k in range(8)])
    ctx.allgather(KH[:, :], KHg[:, :])
    ctx.allgather(VH[:, :], VHg[:, :])
    stage_natt(ctx, qk, Vna, KHg, VHg, bias, ident, ohL, ohR, mTn)
    stage_tl(ctx, 2, 3, xa, xb, modg, pre=dict(mT=mTn, wo=na_w_o, ps=ones8, w_in=w_in[2, 1], w_out=w_out[2, 1]),
             post=dict(w_in=w_in[3, 0], w_out=w_out[3, 0], h_out=h))

    stage_proj(ctx, 8, h, conv_w_in, Gc, [(0, 24, 128, None, False)], 3 * D,
               lambda col, part, off, wd: bcu[col:col + part, off:off + wd])
    stage_edges(ctx, bcu[D:3 * D, :], 16, [(0, 1, 0), (TPC - 1, 1, 1)], EC, 2)
    ctx.allgather(EC[:, :], ECg[:, :])
    stage_conve(ctx, bcu, ECg, cw, ohL, ohR, mTc)
    stage_tl(ctx, 3, None, xb, xo, modg, pre=dict(mT=mTc, wo=conv_w_out, ps=ones8, w_in=w_in[3, 1], w_out=w_out[3, 1]))
    return ctx.finish()


_cache = {}


def _f32(a):
    return np.ascontiguousarray(a, dtype=np.float32)


def _col(v):
    o = np.ones((128,), np.float32)
    o[:v.shape[0]] = v
    return o


def _rope_tables():
    freqs = (np.float32(10000.0) ** (-np.arange(16, dtype=np.float32) / np.float32(16))).astype(np.float32)
    t = np.arange(SEQ)
    row = (t // 64).astype(np.float32)
    col = (t % 64).astype(np.float32)
    ang = np.concatenate([row[:, None] * freqs, col[:, None] * freqs], axis=-1).astype(np.float32)
    cos = np.repeat(np.cos(ang), 2, axis=1).T.astype(np.float32)
    sin = np.repeat(np.sin(ang), 2, axis=1).T.astype(np.float32)
    coss, sins = [], []
    for r in range(NCORES):
        coss.append(np.concatenate([cos[:, r * TPC:(r + 1) * TPC], np.ones((64, CPC), np.float32)], axis=1))
        sins.append(np.concatenate([sin[:, r * TPC:(r + 1) * TPC], np.zeros((64, CPC), np.float32)], axis=1))
    psw = np.zeros((64, 64), np.float32)
    for i in range(32):
        psw[2 * i + 1, 2 * i] = -1.0
        psw[2 * i, 2 * i + 1] = 1.0
    return coss, sins, psw


def _na_bias(rpb, base):
    t = np.arange(6)[:, None, None]
    kk = np.arange(128)[None, :, None]
    qq = np.arange(256)[None, None, :]
    key_row = base - 4 + 2 * t + kk // 64
    kcol = kk % 64
    r = base + qq // 64
    qc = qq % 64
    rs = np.clip(r - 4, 0, 248)
    cs = np.clip(qc - 8, 0, 48)
    valid = (key_row >= rs) & (key_row < rs + 8) & (kcol >= cs) & (kcol < cs + 16) & (key_row >= 0) & (key_row < 256)
    dr = np.clip(key_row - r + 7, 0, 14)
    dc = np.clip(kcol - qc + 15, 0, 30)
    vals = rpb[:, dr, dc]
    return np.where(valid[None], vals, np.float32(-30000.0)).astype(np.float32)


def kernel(x, c, ctx, c_ctx, mod_w, mod_b, ffn_w_in, ffn_w_out,
           mla_w_dq, mla_g_dq, mla_w_uq, mla_w_dkv, mla_g_dkv, mla_w_uk, mla_w_uv,
           mla_g_qn, mla_g_qr, mla_g_kn, mla_g_kr, mla_w_o,
           pool_w, pool_scale,
           na_w_qkv, na_g_q, na_g_k, na_rpb, na_w_o,
           conv_w_in, conv_w, conv_w_out):
    from concourse.bass_utils import run_bass_kernel_spmd
    x = np.asarray(x, np.float32)[0]
    cx = np.asarray(ctx, np.float32)[0]
    R = range(NCORES)
    ones8 = np.ones((128, 8), np.float32)
    cc = np.stack([np.asarray(c, np.float32)[0], np.asarray(c_ctx, np.float32)], axis=-1)
    cT = _f32(cc.reshape(8, 128, 2).transpose(1, 0, 2))
    coss, sins, psw = _rope_tables()
    Wlat = _f32(np.concatenate([mla_w_dq[0], mla_w_dkv[0]], axis=1))
    Glat = _f32(np.stack([_col(mla_g_dq[0][0:128]), _col(mla_g_dq[0][128:256]), _col(mla_g_dq[0][256:384]),
                          _col(mla_g_dkv[0][0:128]), _col(mla_g_dkv[0][128:256]), _col(mla_g_kr[0])], axis=1))
    Gq = _f32(np.stack([_col(mla_g_qn[0]), _col(mla_g_qr[0])] * 8, axis=1))
    Gk = _f32(np.stack([_col(mla_g_kn[0])] * 8, axis=1))
    wbd = np.zeros((D, D), np.float32)
    for g in range(4):
        wbd[g * 256:(g + 1) * 256, g * 256:(g + 1) * 256] = pool_w[0, g]
    psv = _f32(np.asarray(pool_scale[0], np.float32).reshape(8, 128).T)
    half = np.repeat(np.array([1, 2, 4, 8]), 256)[:, None]

    def rcount(T, t):
        lo = np.clip(t[None, :] - half, 0, T)
        hi = np.clip(t[None, :] + half, 0, T)
        return (np.float32(1.0) / (hi - lo).astype(np.float32)).astype(np.float32)

    Gna = _f32(np.stack([np.tile(np.asarray(na_g_q[0], np.float32), 2)] * 8 + [np.tile(np.asarray(na_g_k[0], np.float32), 2)] * 8, axis=1))
    rpb = np.asarray(na_rpb[0], np.float32)
    b_int, b_top, b_bot = _na_bias(rpb, 8), _na_bias(rpb, 0), _na_bias(rpb, 252)
    ident = np.eye(128, dtype=np.float32)
    cw = _f32(np.asarray(conv_w[0], np.float32).T.reshape(8, 128, 3).transpose(1, 0, 2))
    shared = {
        "cT": cT, "ffn_w_in": _f32(ffn_w_in), "ffn_w_out": _f32(ffn_w_out), "Wlat": Wlat, "Glat": Glat,
        "w_uq": _f32(np.asarray(mla_w_uq[0]).reshape(384, 1536)), "Gq": Gq, "w_uk": _f32(np.asarray(mla_w_uk[0]).reshape(256, D)),
        "Gk": Gk, "w_uv": _f32(np.asarray(mla_w_uv[0]).reshape(256, D)), "psw": psw, "mla_w_o": _f32(mla_w_o[0]), "ones8": ones8,
        "wbd": wbd, "psv": psv, "w_qkv": _f32(na_w_qkv[0]), "Gna": Gna, "ident": ident, "na_w_o": _f32(na_w_o[0]),
        "conv_w_in": _f32(conv_w_in[0]), "Gc": np.ones((128, 24), np.float32), "cw": cw, "conv_w_out": _f32(conv_w_out[0]),
    }
    maps = []
    for r in R:
        d = dict(shared)
        d["xT"] = _f32(np.concatenate([x[r * TPC:(r + 1) * TPC].T, cx[r * CPC:(r + 1) * CPC].T], axis=1))
        d["mw"] = _f32(np.asarray(mod_w)[:, :, 1152 * r:1152 * (r + 1)])
        d["mb"] = _f32(np.asarray(mod_b)[:, 1152 * r:1152 * (r + 1)].reshape(4, 9, 128).transpose(2, 0, 1))
        d["cos"] = coss[r]
        d["sin"] = sins[r]
        d["rc"] = _f32(np.concatenate([rcount(SEQ, np.arange(r * TPC, (r + 1) * TPC)),
                                       rcount(CTX, np.arange(r * CPC, (r + 1) * CPC))], axis=1))
        ohL = np.zeros((128, NCORES), np.float32)
        ohR = np.zeros((128, NCORES), np.float32)
        if r > 0:
            ohL[:, r - 1] = 1.0
        if r < NCORES - 1:
            ohR[:, r + 1] = 1.0
        d["ohL"] = ohL
        d["ohR"] = ohR
        bv = np.stack([b_top if r == 0 else b_int, b_int, b_bot if r == NCORES - 1 else b_int], axis=1)
        d["bias"] = _f32(bv.transpose(0, 3, 1, 2, 4).reshape(NA_H, 128, 3 * 6 * 256))
        maps.append(d)
    if "nc" not in _cache:
        _cache["nc"] = build_fused()
    res = run_bass_kernel_spmd(_cache["nc"], maps, core_ids=list(R))
    out = np.concatenate([res.results[r]["xo"][:, :TPC].T for r in R], axis=0)
    return np.ascontiguousarray(out[None], dtype=np.float32)
```

```python
import numpy as np
from contextlib import ExitStack
import concourse.bass as bass
import concourse.mybir as mybir

F32 = mybir.dt.float32
BF16 = mybir.dt.bfloat16
AF = mybir.ActivationFunctionType
ALU = mybir.AluOpType

D = 1024
DFF = 2816
NJ = DFF // 128
EPS = 1e-6
NCORES = 8
SEQ = 16384
TPC = SEQ // NCORES
CTX = 256
CPC = CTX // NCORES
NTL = TPC + CPC
NKEY = NCORES * NTL
HALVES = [[(0, 512, 0), (512, 512, 0)], [(1024, 512, 0), (1536, 512, 0), (2048, CPC, 1)]]
PCH = [(0, 512), (512, 512), (1024, 512), (1536, 512), (2048, CPC)]


class Tile:
    __slots__ = ("t", "name", "acc")

    def __init__(self, t, name):
        self.t = t
        self.name = name
        self.acc = {}

    def __getitem__(self, idx):
        return self.t[idx]


class Op:
    __slots__ = ("eng", "fn", "waits", "sem", "val", "isdma")


class Ctx:
    ENGS = ("pe", "act", "dve", "pool", "sp")

    def __init__(self):
        self.nc = bass.Bass("TRN2", target_bir_lowering=False, num_devices=NCORES)
        self.es = ExitStack()
        self.esem = [{e: self.es.enter_context(self.nc.semaphore("se%d_%s" % (p, e))) for e in self.ENGS} for p in range(2)]
        self.dpool = [[], []]
        self.ccsem = self.es.enter_context(self.nc.semaphore("cc"))
        self.cccount = 0
        self.stage = 0
        self.names = 0

    def dram(self, name, shape, dt=F32, kind="Internal"):
        return self.nc.dram_tensor(name, list(shape), dt, kind=kind).ap()

    def dsem(self, parity, idx):
        pool = self.dpool[parity]
        while len(pool) <= idx:
            pool.append(self.es.enter_context(self.nc.semaphore("sd%d_%d" % (parity, len(pool)))))
        return pool[idx]

    def allgather(self, src, dst):
        nc = self.nc
        self.cccount += 1
        cnt = self.cccount
        with nc.Block() as block:
            @block.gpsimd
            def _(g):
                g.collective_compute("AllGather", ALU.bypass, replica_groups=[list(range(NCORES))],
                                     ins=[src], outs=[dst]).then_inc(self.ccsem)
                g.wait_ge(self.ccsem, cnt)

    def finish(self):
        self.es.close()
        return self.nc


class Prog:
    ENGS = Ctx.ENGS

    def __init__(self, ctx):
        self.ctx = ctx
        self.nc = ctx.nc
        self.par = ctx.stage % 2
        self.es = ExitStack()
        self.ops = {e: [] for e in self.ENGS}
        self.esem = ctx.esem[self.par]
        self.ecount = {e: 0 for e in self.ENGS}
        self.waited = {e: {} for e in self.ENGS}
        self.dkeys = {}
        self.dcount = {}
        self.final = []
        self.sid = ctx.stage

    def sb(self, name, shape, dt):
        nm = "s%d_%s" % (self.sid, name)
        return Tile(self.es.enter_context(self.nc.sbuf_tensor(nm, list(shape), dt)), nm)

    def ps(self, name, shape, dt=F32):
        nm = "s%d_%s" % (self.sid, name)
        return Tile(self.es.enter_context(self.nc.psum_tensor(nm, list(shape), dt)), nm)

    def _deps(self, reads, writes):
        deps = []
        for (tl, rg) in reads:
            for k, ent in tl.acc.items():
                if rg is None or k is None or k == rg:
                    if ent[0] is not None:
                        deps.append(ent[0])
        for (tl, rg) in writes:
            for k, ent in tl.acc.items():
                if rg is None or k is None or k == rg:
                    if ent[0] is not None:
                        deps.append(ent[0])
                    deps.extend(ent[1])
        return deps

    def _commit(self, op, reads, writes):
        for (tl, rg) in reads:
            ent = tl.acc.setdefault(rg, [None, []])
            ent[1].append(op)
        for (tl, rg) in writes:
            if rg is None:
                tl.acc.clear()
            tl.acc[rg] = [op, []]

    def add(self, eng, fn, r=(), w=(), dma_key=None):
        op = Op()
        op.eng = eng
        op.fn = fn
        op.isdma = dma_key is not None
        deps = self._deps(r, w)
        waits = {}
        wd = self.waited[eng]
        for d in deps:
            if d.eng == "pe" and eng == "pe" and not d.isdma:
                continue
            if wd.get(d.sem, 0) >= d.val:
                continue
            if waits.get(d.sem, (None, 0))[1] < d.val:
                waits[d.sem] = (d.sem, d.val)
        for s, v in waits.values():
            wd[s] = v
        op.waits = list(waits.values())
        if op.isdma:
            if dma_key not in self.dkeys:
                self.dkeys[dma_key] = self.ctx.dsem(self.par, len(self.dkeys))
                self.dcount[dma_key] = 0
            self.dcount[dma_key] += 16
            op.sem = self.dkeys[dma_key]
            op.val = self.dcount[dma_key]
        else:
            self.ecount[eng] += 1
            op.sem = self.esem[eng]
            op.val = self.ecount[eng]
        self._commit(op, r, w)
        self.ops[eng].append(op)
        return op

    def dma(self, eng, out, in_, r=(), w=(), key=None, final=False, **kw):
        op = self.add(eng, lambda e: e.dma_start(out=out, in_=in_, **kw), r=r, w=w, dma_key=key)
        if final:
            self.final.append(op)
        return op

    def emit(self):
        nc = self.nc
        ctx = self.ctx
        other = list(ctx.esem[1 - self.par].values()) + list(ctx.dpool[1 - self.par])
        with nc.Block() as block:
            def run(e, name):
                if name == "sp":
                    for s in other:
                        e.sem_clear(s)
                for op in self.ops[name]:
                    for s, v in op.waits:
                        e.wait_ge(s, v)
                    ins = op.fn(e)
                    ins.then_inc(op.sem, 16 if op.isdma else 1)
                if name == "sp":
                    for op in self.final:
                        e.wait_ge(op.sem, op.val)

            @block.tensor
            def _(e):
                run(e, "pe")

            @block.scalar
            def _(e):
                run(e, "act")

            @block.vector
            def _(e):
                run(e, "dve")

            @block.gpsimd
            def _(e):
                run(e, "pool")

            @block.sync
            def _(e):
                run(e, "sp")
        self.es.close()
        ctx.stage += 1


def mvec(T, l, v, c, ci):
    g = 8 * v + c
    r, j = divmod(g, 9)
    col = (l * 9 + j) * 2 + ci
    return T[:, r, col:col + 1]


def stage_mod(ctx, cT, mw, mb, modl):
    P = Prog(ctx)
    C = P.sb("C", [128, 8, 2], F32)
    SC = P.sb("SC", [128, 8, 2], F32)
    MB = P.sb("MB", [128, 4, 9], F32)
    W = [P.sb("W%d" % i, [128, 8, 1152], F32) for i in range(2)]
    O = P.sb("O", [128, 4, 9, 2], F32)
    PS = [P.ps("ps%d" % i, [128, 2]) for i in range(4)]
    P.dma("sp", C[:], cT, w=[(C, None)], key="c")
    P.dma("sp", MB[:], mb, w=[(MB, None)], key="mb")
    P.add("act", lambda e: e.activation(out=SC[:], in_=C[:], func=AF.Silu), r=[(C, None)], w=[(SC, None)])
    n = 0
    for l in range(4):
        Wl = W[l % 2]
        P.dma("sp", Wl[:], mw[l].rearrange("(k p) c -> p k c", p=128), w=[(Wl, None)], key=("w", l % 2))
        for j in range(9):
            pt = PS[n % 4]
            n += 1
            for k in range(8):
                P.add("pe", lambda e, pt=pt, Wl=Wl, k=k, j=j: e.matmul(
                    pt[:], lhsT=Wl[:, k, j * 128:(j + 1) * 128], rhs=SC[:, k, :], start=(k == 0), stop=(k == 7)),
                    r=[(Wl, None), (SC, None)], w=[(pt, None)])
            P.add("dve", lambda e, pt=pt, l=l, j=j: e.tensor_scalar(
                out=O[:, l, j, :], in0=pt[:], scalar1=MB[:, l, j:j + 1], scalar2=None, op0=ALU.add),
                r=[(pt, None), (MB, None)], w=[(O, (l, j))])
    P.dma("sp", modl, O[:].rearrange("p l j c -> p (l j c)"), r=[(O, None)], key="o", final=True)
    P.emit()


def stage_tl(ctx, l_pre, l_post, x_in, x_out, modg, pre=None, post=None, chunks=None):
    P = Prog(ctx)
    if chunks is None:
        chunks = [(0, 512, 0), (512, 512, 0), (1024, 512, 0), (1536, 512, 0), (TPC, CPC, 1)]
    has_pre, has_post = pre is not None, post is not None
    HW = NTL
    JG = 6
    onesD = P.sb("onesD", [128, 128], BF16)
    epsc = P.sb("epsc", [128, 1], F32)
    P.add("dve", lambda e: e.memset(onesD[:], 1.0 / D), w=[(onesD, None)])
    P.add("dve", lambda e: e.memset(epsc[:], EPS), w=[(epsc, None)])
    X = P.sb("X", [128, 8, HW], F32)
    H = P.sb("H", [128, 8, HW], BF16)
    A = P.sb("A", [128, JG, HW], BF16)
    WST = P.sb("WST", [128, 2, 8, 256], F32)
    WIN = P.sb("WIN", [128, 2, 8, 256], BF16)
    WOST = P.sb("WOST", [128, 2, 1024], F32)
    WOUT = P.sb("WOUT", [128, 8, 1024], BF16)
    SQ = P.sb("SQ", [128, 2, 512], BF16)
    RS = P.sb("RS", [128, 2, 512], F32)
    TMP = P.sb("TMP", [128, 2, 512], F32)
    SG = P.sb("SG", [128, 2, 512], F32)
    HOF = P.sb("HOF", [128, 2, 512], F32)
    MOD = P.sb("MOD", [128, 8, 72], F32)
    OPS = P.sb("OPS", [128, 8, 72], F32)
    HG = P.sb("HG", [128, 8, 72], F32)
    GEFF = P.sb("GEFF", [128, 2, 8], F32)
    PSV = P.sb("PSV", [128, 8], F32)
    PG = [P.ps("pg%d" % i, [128, 512]) for i in range(2)]
    PU = [P.ps("pu%d" % i, [128, 512]) for i in range(2)]
    PY = [P.ps("py%d" % i, [128, 512]) for i in range(2)]
    PM = [P.ps("pm%d" % i, [128, 512]) for i in range(2)]

    P.dma("sp", MOD[:], modg.rearrange("(r p) c -> p r c", p=128), w=[(MOD, None)], key="mod")
    P.add("dve", lambda e: e.tensor_scalar(out=OPS[:], in0=MOD[:], scalar1=1.0, scalar2=None, op0=ALU.add),
          r=[(MOD, None)], w=[(OPS, None)])
    P.add("dve", lambda e: e.tensor_scalar(out=HG[:], in0=MOD[:], scalar1=0.5, scalar2=None, op0=ALU.mult),
          r=[(MOD, None)], w=[(HG, None)])
    if has_pre:
        P.dma("sp", PSV[:], pre["ps"], w=[(PSV, None)], key="psv")
        for ci in range(2):
            for c in range(8):
                P.add("dve", lambda e, ci=ci, c=c: e.tensor_tensor(out=GEFF[:, ci, c:c + 1], in0=mvec(MOD, l_pre, 5, c, ci),
                                                                 in1=PSV[:, c:c + 1], op=ALU.mult),
                      r=[(MOD, None), (PSV, None)], w=[(GEFF, (ci, c))])
    cnt = {"n": 0, "pm": 0, "pg": 0, "py": 0, "sq": 0}
    for k in range(8):
        P.dma("sp", X[:, k, :], x_in[k * 128:(k + 1) * 128, :], w=[(X, (k, "all"))], key=("x", k))

    def xr(k, off):
        return (X, (k, off))

    def xreads(k, off):
        return [(X, (k, off)), (X, (k, "all"))]

    def modulate(l, v_shift, v_scale, h_out):
        for (off, wd, ci) in chunks:
            sl = slice(off, off + wd)
            pm = PM[cnt["pm"] % 2]
            rs = cnt["pm"] % 2
            cnt["pm"] += 1
            for k in range(8):
                sq = cnt["sq"] % 2
                cnt["sq"] += 1
                P.add("act", lambda e, k=k, sl=sl, wd=wd, sq=sq: e.activation(out=SQ[:, sq, 0:wd], in_=X[:, k, sl], func=AF.Square),
                      r=xreads(k, off), w=[(SQ, sq)])
                P.add("pe", lambda e, k=k, wd=wd, pm=pm, sq=sq: e.matmul(pm[:, 0:wd], lhsT=onesD[:], rhs=SQ[:, sq, 0:wd],
                                                                        start=(k == 0), stop=(k == 7)),
                      r=[(SQ, sq), (onesD, None)], w=[(pm, None)])
            P.add("act", lambda e, wd=wd, pm=pm, rs=rs: e.activation(out=RS[:, rs, 0:wd], in_=pm[:, 0:wd], func=AF.Sqrt,
                                                                     bias=epsc[:], scale=1.0),
                  r=[(pm, None), (epsc, None)], w=[(RS, rs)])
            P.add("dve", lambda e, wd=wd, rs=rs: e.reciprocal(out=RS[:, rs, 0:wd], in_=RS[:, rs, 0:wd]),
                  r=[(RS, rs)], w=[(RS, rs)])
            for k in range(8):
                t = cnt["n"] % 2
                cnt["n"] += 1
                P.add("dve", lambda e, k=k, sl=sl, wd=wd, t=t, ci=ci, rs=rs: e.scalar_tensor_tensor(
                    out=TMP[:, t, 0:wd], in0=X[:, k, sl], scalar=mvec(OPS, l, v_scale, k, ci), in1=RS[:, rs, 0:wd],
                    op0=ALU.mult, op1=ALU.mult),
                    r=xreads(k, off) + [(OPS, None), (RS, rs)], w=[(TMP, t)])
                P.add("act", lambda e, k=k, sl=sl, wd=wd, t=t, ci=ci: e.activation(
                    out=H[:, k, sl], in_=TMP[:, t, 0:wd], func=AF.Identity, bias=mvec(MOD, l, v_shift, k, ci), scale=1.0),
                    r=[(TMP, t), (MOD, None)], w=[(H, (k, off))])
                if h_out is not None:
                    P.add("dve", lambda e, k=k, wd=wd, t=t, ci=ci: e.tensor_scalar(
                        out=HOF[:, t, 0:wd], in0=TMP[:, t, 0:wd], scalar1=mvec(MOD, l, v_shift, k, ci), scalar2=None, op0=ALU.add),
                        r=[(TMP, t), (MOD, None)], w=[(HOF, t)])
                    P.dma("sp", h_out[k * 128:(k + 1) * 128, off:off + wd], HOF[:, t, 0:wd], r=[(HOF, t)],
                          key=("hof", t), final=True)

    def ffn(l, w_in, w_out, v_shift, v_scale, v_gate):
        modulate(l, v_shift, v_scale, None)

        def load_in(j, do_in=True, do_out=True):
            s = j % 2
            jl = j % JG
            if do_in:
                P.dma("sp", WST[:, s, :, 0:128], w_in[:, j * 128:(j + 1) * 128].rearrange("(k p) c -> p k c", p=128),
                      w=[(WST, s)], key=("wst", s))
                P.dma("sp", WST[:, s, :, 128:256],
                      w_in[:, DFF + j * 128:DFF + (j + 1) * 128].rearrange("(k p) c -> p k c", p=128),
                      w=[(WST, (s, 1))], key=("wst", s))
                P.add("pool", lambda e, s=s: e.tensor_copy(out=WIN[:, s], in_=WST[:, s]), r=[(WST, None)], w=[(WIN, s)])
            if do_out:
                P.dma("sp", WOST[:, s, :], w_out[j * 128:(j + 1) * 128, :], w=[(WOST, s)], key=("wost", s))
                P.add("pool", lambda e, s=s, jl=jl: e.tensor_copy(out=WOUT[:, jl, :], in_=WOST[:, s, :]),
                      r=[(WOST, s)], w=[(WOUT, jl)])

        load_in(0)
        for g0 in range(0, NJ, JG):
            gn = min(JG, NJ - g0)
            if g0 > 0:
                load_in(g0, do_in=False, do_out=True)
            for j in range(g0, g0 + gn):
                if j + 1 < NJ:
                    load_in(j + 1, do_in=True, do_out=(j + 1 < g0 + gn))
                s = j % 2
                jl = j % JG
                for (off, wd, ci) in chunks:
                    sl = slice(off, off + wd)
                    b = cnt["pg"] % 2
                    cnt["pg"] += 1
                    for k in range(8):
                        P.add("pe", lambda e, k=k, s=s, sl=sl, wd=wd, b=b: e.matmul(
                            PG[b][:, 0:wd], lhsT=WIN[:, s, k, 0:128], rhs=H[:, k, sl], start=(k == 0), stop=(k == 7)),
                            r=[(WIN, s), (H, (k, off))], w=[(PG[b], None)])
                    for k in range(8):
                        P.add("pe", lambda e, k=k, s=s, sl=sl, wd=wd, b=b: e.matmul(
                            PU[b][:, 0:wd], lhsT=WIN[:, s, k, 128:256], rhs=H[:, k, sl], start=(k == 0), stop=(k == 7)),
                            r=[(WIN, s), (H, (k, off))], w=[(PU[b], None)])
                    P.add("act", lambda e, wd=wd, b=b: e.activation(out=SG[:, b, 0:wd], in_=PG[b][:, 0:wd], func=AF.Silu),
                          r=[(PG[b], None)], w=[(SG, b)])
                    P.add("dve", lambda e, wd=wd, b=b, jl=jl, sl=sl: e.tensor_tensor(
                        out=A[:, jl, sl], in0=SG[:, b, 0:wd], in1=PU[b][:, 0:wd], op=ALU.mult),
                        r=[(SG, b), (PU[b], None)], w=[(A, (jl, off))])
            for mc in range(8):
                for (off, wd, ci) in chunks:
                    sl = slice(off, off + wd)
                    b = cnt["py"] % 2
                    cnt["py"] += 1
                    for jl in range(gn):
                        P.add("pe", lambda e, jl=jl, mc=mc, sl=sl, wd=wd, b=b, gn=gn: e.matmul(
                            PY[b][:, 0:wd], lhsT=WOUT[:, jl, mc * 128:(mc + 1) * 128], rhs=A[:, jl, sl],
                            start=(jl == 0), stop=(jl == gn - 1)),
                            r=[(WOUT, jl), (A, (jl, off))], w=[(PY[b], None)])
                    P.add("dve", lambda e, mc=mc, sl=sl, wd=wd, b=b, ci=ci: e.scalar_tensor_tensor(
                        out=X[:, mc, sl], in0=PY[b][:, 0:wd], scalar=mvec(HG, l, v_gate, mc, ci), in1=X[:, mc, sl],
                        op0=ALU.mult, op1=ALU.add),
                        r=[(PY[b], None), (HG, None)] + xreads(mc, off), w=[(X, (mc, off))])

    if has_pre:
        mT, wo = pre["mT"], pre["wo"]
        for (off, wd, ci) in chunks:
            for q in range(0, wd, 256):
                qw = min(256, wd - q)
                s = (q // 256) % 2
                P.dma("sp", WST[:, s, :, 0:qw], mT[:, off + q:off + q + qw].rearrange("(k p) n -> p k n", p=128),
                      w=[(WST, None)], key=("wst", s))
                P.add("pool", lambda e, s=s, qw=qw, a=off + q: e.tensor_copy(out=H[:, :, a:a + qw], in_=WST[:, s, :, 0:qw]),
                      r=[(WST, None)], w=[(H, None)])
        for k in range(8):
            s = k % 2
            P.dma("sp", WOST[:, s, :], wo[k * 128:(k + 1) * 128, :], w=[(WOST, s)], key=("wost", s))
            P.add("pool", lambda e, s=s, k=k: e.tensor_copy(out=WOUT[:, k, :], in_=WOST[:, s, :]),
                  r=[(WOST, s)], w=[(WOUT, k)])
        for mc in range(8):
            for (off, wd, ci) in chunks:
                sl = slice(off, off + wd)
                b = cnt["py"] % 2
                cnt["py"] += 1
                for k in range(8):
                    P.add("pe", lambda e, k=k, mc=mc, sl=sl, wd=wd, b=b: e.matmul(
                        PY[b][:, 0:wd], lhsT=WOUT[:, k, mc * 128:(mc + 1) * 128], rhs=H[:, k, sl],
                        start=(k == 0), stop=(k == 7)),
                        r=[(WOUT, k), (H, None)], w=[(PY[b], None)])
                P.add("dve", lambda e, mc=mc, sl=sl, wd=wd, b=b, ci=ci: e.scalar_tensor_tensor(
                    out=X[:, mc, sl], in0=PY[b][:, 0:wd], scalar=GEFF[:, ci, mc:mc + 1], in1=X[:, mc, sl],
                    op0=ALU.mult, op1=ALU.add),
                    r=[(PY[b], None), (GEFF, None)] + xreads(mc, off), w=[(X, (mc, off))])
        ffn(l_pre, pre["w_in"], pre["w_out"], 6, 7, 8)
    if has_post:
        ffn(l_post, post["w_in"], post["w_out"], 0, 1, 2)
        modulate(l_post, 3, 4, post["h_out"])
    for k in range(8):
        P.dma("sp", x_out[k * 128:(k + 1) * 128, :], X[:, k, :], r=[(X, None)], key=("xo", k % 2), final=True)
    P.emit()


def stage_proj(ctx, KC, hT, Wd, Gd, groups, M, out_fn, rope=None, chunks=PCH, obf=None):
    P = Prog(ctx)
    g2 = []
    for (c0, nck, part, norm, rp) in groups:
        if norm is None:
            g2 += [(c0 + i * part, 1, part, None, rp) for i in range(nck)]
        else:
            assert nck <= 3
            g2.append((c0, nck, part, norm, rp))
    groups = g2
    nch = sum(g[1] for g in groups)
    WB = P.sb("WB", [128, KC, M], BF16)
    WS = P.sb("WS", [128, 2, KC, 512], F32)
    HS = P.sb("HS", [128, KC, 512], F32)
    HB = P.sb("HB", [128, 2, KC, 512], BF16)
    G = P.sb("G", [128, nch], F32)
    Y = P.sb("Y", [128, 4, 512], F32)
    SQ = P.sb("SQ", [128, 2, 512], BF16)
    RS = P.sb("RS", [128, 2, 512], F32)
    OB = P.sb("OB", [128, 4, 512], F32)
    OBH = P.sb("OBH", [128, 4, 512], BF16)
    if obf is None:
        obf = lambda col: False
    eps = P.sb("eps", [128, 1], F32)
    P.add("dve", lambda e: e.memset(eps[:], EPS), w=[(eps, None)])
    ones = {}
    for (c0, nck, part, norm, rp) in groups:
        if norm == "full" and (nck * part) not in ones:
            t = P.sb("ones%d" % (nck * part), [128, 128], BF16)
            P.add("dve", lambda e, t=t, v=1.0 / (nck * part): e.memset(t[:], v), w=[(t, None)])
            ones[nck * part] = t
        if norm == "blk64" and "b" not in ones:
            t = P.sb("onesb", [128, 128], BF16)
            P.add("dve", lambda e, t=t: e.memset(t[:], 0.0), w=[(t, None)])
            P.add("dve", lambda e, t=t: e.memset(t[0:64, 0:64], 1.0 / 64), r=[(t, None)], w=[(t, None)])
            P.add("dve", lambda e, t=t: e.memset(t[64:128, 64:128], 1.0 / 64), r=[(t, None)], w=[(t, None)])
            ones["b"] = t
    if rope is not None:
        cosd, sind, pswd = rope
        COS = P.sb("COS", [64, 512], F32)
        SIN = P.sb("SIN", [64, 512], F32)
        PSWf = P.sb("PSWf", [64, 64], F32)
        PSW = P.sb("PSW", [64, 64], BF16)
        KRB = P.sb("KRB", [64, 512], BF16)
        T1 = P.sb("T1", [64, 512], F32)
        P.dma("sp", PSWf[:], pswd, w=[(PSWf, None)], key="psw")
        P.add("dve", lambda e: e.tensor_copy(out=PSW[:], in_=PSWf[:]), r=[(PSWf, None)], w=[(PSW, None)])
        PR = P.ps("pr", [64, 512])
    PSA = [P.ps("psa%d" % i, [128, 512]) for i in range(4)]
    PM = [P.ps("pm%d" % i, [128, 512]) for i in range(2)]
    P.dma("sp", G[:], Gd, w=[(G, None)], key="g")
    for si, c0 in enumerate(range(0, M, 512)):
        cw = min(512, M - c0)
        s = si % 2
        P.dma("sp", WS[:, s, :, 0:cw], Wd[:, c0:c0 + cw].rearrange("(k p) c -> p k c", p=128), w=[(WS, s)], key=("ws", s))
        P.add("pool", lambda e, s=s, cw=cw, c0=c0: e.tensor_copy(out=WB[:, :, c0:c0 + cw], in_=WS[:, s, :, 0:cw]),
              r=[(WS, s)], w=[(WB, c0)])
    cnt = {"a": 0, "m": 0, "o": 0}
    for ti, (off, wd) in enumerate(chunks):
        hs = ti % 2
        P.dma("sp", HS[:, :, 0:wd], hT[:, off:off + wd].rearrange("(k p) n -> p k n", p=128), w=[(HS, None)], key="hs")
        P.add("pool", lambda e, hs=hs, wd=wd: e.tensor_copy(out=HB[:, hs, :, 0:wd], in_=HS[:, :, 0:wd]),
              r=[(HS, None)], w=[(HB, hs)])
        if rope is not None:
            P.dma("sp", COS[:, 0:wd], cosd[:, off:off + wd], w=[(COS, None)], key="cos")
            P.dma("sp", SIN[:, 0:wd], sind[:, off:off + wd], w=[(SIN, None)], key="sin")
        gi = 0
        for (c0, nck, part, norm, rp) in groups:
            pm = PM[cnt["m"] % 2]
            rs = cnt["m"] % 2
            if norm is not None:
                cnt["m"] += 1
            ys = []
            for c in range(nck):
                col = c0 + c * part
                pa = PSA[cnt["a"] % 4]
                yi = cnt["a"] % 4
                cnt["a"] += 1
                ys.append(yi)
                for k in range(KC):
                    P.add("pe", lambda e, pa=pa, k=k, col=col, part=part, hs=hs, wd=wd: e.matmul(
                        pa[0:part, 0:wd], lhsT=WB[:, k, col:col + part], rhs=HB[:, hs, k, 0:wd], start=(k == 0), stop=(k == KC - 1)),
                        r=[(WB, None), (HB, hs)], w=[(pa, None)])
                P.add("act", lambda e, pa=pa, yi=yi, part=part, wd=wd: e.activation(
                    out=Y[0:part, yi, 0:wd], in_=pa[0:part, 0:wd], func=AF.Identity),
                    r=[(pa, None)], w=[(Y, yi)])
                if norm is not None:
                    sq = cnt["a"] % 2
                    P.add("act", lambda e, pa=pa, sq=sq, part=part, wd=wd: e.activation(
                        out=SQ[0:part, sq, 0:wd], in_=pa[0:part, 0:wd], func=AF.Square),
                        r=[(pa, None)], w=[(SQ, sq)])
                    om = ones["b"] if norm == "blk64" else ones[nck * part]
                    P.add("pe", lambda e, pm=pm, om=om, sq=sq, part=part, wd=wd, c=c, nck=nck: e.matmul(
                        pm[0:part, 0:wd], lhsT=om[0:part, 0:part], rhs=SQ[0:part, sq, 0:wd], start=(c == 0), stop=(c == nck - 1)),
                        r=[(om, None), (SQ, sq)], w=[(pm, None)])
            if norm is not None:
                P.add("act", lambda e, pm=pm, rs=rs, part=part, wd=wd: e.activation(
                    out=RS[0:part, rs, 0:wd], in_=pm[0:part, 0:wd], func=AF.Sqrt, bias=eps[0:part, :], scale=1.0),
                    r=[(pm, None), (eps, None)], w=[(RS, rs)])
                P.add("dve", lambda e, rs=rs, part=part, wd=wd: e.reciprocal(out=RS[0:part, rs, 0:wd], in_=RS[0:part, rs, 0:wd]),
                      r=[(RS, rs)], w=[(RS, rs)])
            for c in range(nck):
                col = c0 + c * part
                yi = ys[c]
                hb = obf(col)
                if norm is not None:
                    ob = cnt["o"] % 4
                    cnt["o"] += 1
                    ot = OBH if (hb and not rp) else OB
                    P.add("dve", lambda e, yi=yi, ob=ob, rs=rs, part=part, wd=wd, gc=gi + c, ot=ot: e.scalar_tensor_tensor(
                        out=ot[0:part, ob, 0:wd], in0=Y[0:part, yi, 0:wd], scalar=G[0:part, gc:gc + 1], in1=RS[0:part, rs, 0:wd],
                        op0=ALU.mult, op1=ALU.mult),
                        r=[(Y, yi), (G, None), (RS, rs)], w=[(ot, ob)])
                    src = (ot, ob)
                else:
                    assert not hb
                    src = (Y, yi)
                if rp:
                    st, sidx = src
                    P.add("act", lambda e, st=st, sidx=sidx, wd=wd: e.activation(out=KRB[:, 0:wd], in_=st[0:64, sidx, 0:wd], func=AF.Identity),
                          r=[src], w=[(KRB, None)])
                    P.add("pe", lambda e, wd=wd: e.matmul(PR[:, 0:wd], lhsT=PSW[:], rhs=KRB[:, 0:wd], start=True, stop=True),
                          r=[(PSW, None), (KRB, None)], w=[(PR, None)])
                    P.add("dve", lambda e, st=st, sidx=sidx, wd=wd: e.tensor_tensor(out=T1[:, 0:wd], in0=st[0:64, sidx, 0:wd], in1=COS[:, 0:wd], op=ALU.mult),
                          r=[src, (COS, None)], w=[(T1, None)])
                    ob2 = cnt["o"] % 4
                    cnt["o"] += 1
                    P.add("dve", lambda e, ob2=ob2, wd=wd: e.tensor_tensor(out=OB[0:64, ob2, 0:wd], in0=PR[:, 0:wd], in1=SIN[:, 0:wd], op=ALU.mult),
                          r=[(PR, None), (SIN, None)], w=[(OB, ob2)])
                    ot2 = OBH if hb else OB
                    P.add("dve", lambda e, ob2=ob2, wd=wd, ot2=ot2: e.tensor_tensor(out=ot2[0:64, ob2, 0:wd], in0=OB[0:64, ob2, 0:wd], in1=T1[:, 0:wd], op=ALU.add),
                          r=[(OB, ob2), (T1, None)], w=[(ot2, ob2)])
                    src = (ot2, ob2)
                st, sidx = src
                P.dma("sp", out_fn(col, part, off, wd), st[0:part, sidx, 0:wd], r=[src], key=("o", st.name, sidx), final=True)
            gi += nck
    P.emit()


def stage_projT(ctx, KC, hT, Wd, M, out_fn, chunks=PCH, bf16=False):
    P = Prog(ctx)
    WB = P.sb("WB", [128, KC, M], BF16)
    WS = P.sb("WS", [128, 2, KC, 512], F32)
    HS = P.sb("HS", [128, KC, 512], F32)
    HB = P.sb("HB", [128, 2, KC, 512], BF16)
    OB = P.sb("OB", [128, 4, 512], BF16 if bf16 else F32)
    PSA = [P.ps("psa%d" % i, [128, 512]) for i in range(4)]
    for si, c0 in enumerate(range(0, M, 512)):
        cw = min(512, M - c0)
        s = si % 2
        P.dma("sp", WS[:, s, :, 0:cw], Wd[:, c0:c0 + cw].rearrange("(k p) c -> p k c", p=128), w=[(WS, s)], key=("ws", s))
        P.add("pool", lambda e, s=s, cw=cw, c0=c0: e.tensor_copy(out=WB[:, :, c0:c0 + cw], in_=WS[:, s, :, 0:cw]),
              r=[(WS, s)], w=[(WB, c0)])
    n = 0
    for ti, (off, wd) in enumerate(chunks):
        hs = ti % 2
        P.dma("sp", HS[:, :, 0:wd], hT[:, off:off + wd].rearrange("(k p) n -> p k n", p=128), w=[(HS, None)], key="hs")
        P.add("pool", lambda e, hs=hs, wd=wd: e.tensor_copy(out=HB[:, hs, :, 0:wd], in_=HS[:, :, 0:wd]),
              r=[(HS, None)], w=[(HB, hs)])
        for t0 in range(0, wd, 128):
            nt = min(128, wd - t0)
            for c0 in range(0, M, 512):
                cw = min(512, M - c0)
                b = n % 4
                n += 1
                for k in range(KC):
                    P.add("pe", lambda e, b=b, k=k, hs=hs, t0=t0, nt=nt, c0=c0, cw=cw: e.matmul(
                        PSA[b][0:nt, 0:cw], lhsT=HB[:, hs, k, t0:t0 + nt], rhs=WB[:, k, c0:c0 + cw], start=(k == 0), stop=(k == KC - 1)),
                        r=[(HB, hs), (WB, None)], w=[(PSA[b], None)])
                P.add("act", lambda e, b=b, nt=nt, cw=cw: e.activation(out=OB[0:nt, b, 0:cw], in_=PSA[b][0:nt, 0:cw], func=AF.Identity),
                      r=[(PSA[b], None)], w=[(OB, b)])
                dst, src_view = out_fn(off + t0, nt, c0, cw, OB[0:nt, b, 0:cw])
                P.dma("sp", dst, src_view, r=[(OB, b)], key=("o", b), final=True)
    P.emit()


def stage_copy(ctx, moves, zero=()):
    P = Prog(ctx)
    B = [P.sb("B%d" % i, [128, 2048], F32) for i in range(2)]
    Z = P.sb("Z", [128, 2048], F32)
    P.add("dve", lambda e: e.memset(Z[:], 0.0), w=[(Z, None)])
    for i, (dst, src, pp, fr) in enumerate(moves):
        b = B[i % 2]
        P.dma("sp", b[0:pp, 0:fr], src, w=[(b, None)], key=("ld", i % 2))
        P.dma("sp", dst, b[0:pp, 0:fr], r=[(b, None)], key=("st", i % 2), final=True)
    for (dst, pp, fr) in zero:
        P.dma("sp", dst, Z[0:pp, 0:fr], r=[(Z, None)], key="z", final=True)
    P.emit()


KKR = D + 64


def stage_att(ctx, Qd, KKg, VLg, VCg, oT, scale):
    P = Prog(ctx)
    NTT = NCORES * 16
    KN = [P.sb("KN%d" % i, [128, NKEY], BF16) for i in range(2)]
    KR = P.sb("KR", [64, NKEY], BF16)
    V = [P.sb("V%d" % i, [128, NTT, 128], BF16) for i in range(2)]
    KNc = [P.sb("KNc%d" % i, [128, CTX], BF16) for i in range(2)]
    KRc = P.sb("KRc", [64, CTX], BF16)
    Vc = [P.sb("Vc%d" % i, [128, 2, 128], BF16) for i in range(2)]
    QN = P.sb("QN", [128, 2, 512], BF16)
    QR = P.sb("QR", [64, 2, 512], BF16)
    PT = P.sb("PT", [128, 4, 512], BF16)
    ACCd = P.sb("ACCd", [128, 2, 512], F32)
    ACCp = P.sb("ACCp", [128, 2, 512], F32)
    RC = P.sb("RC", [128, 2, 512], F32)
    OS = P.sb("OS", [128, 2, 512], F32)
    ONESf = P.sb("ONESf", [128, 128], F32)
    P.add("dve", lambda e: e.memset(ONESf[:], 1.0), w=[(ONESf, None)])
    PS = [P.ps("ps%d" % i, [128, 512]) for i in range(4)]
    PO = [P.ps("po%d" % i, [128, 512]) for i in range(2)]
    PSM = [P.ps("psm%d" % i, [128, 512]) for i in range(2)]
    cn = {"pt": 0, "ps": 0, "q": 0}

    for r in range(NCORES):
        P.dma("sp", KR[:, r * NTL:(r + 1) * NTL], KKg[r * KKR + D:r * KKR + D + 64, :], w=[(KR, r)], key=("kr", r))
    for r in range(NCORES):
        P.add("pool", lambda e, r=r: e.tensor_copy(out=KRc[:, r * CPC:(r + 1) * CPC], in_=KR[:, r * NTL + TPC:(r + 1) * NTL]),
              r=[(KR, r)], w=[(KRc, r)])

    def load_head(h):
        b = h % 2
        for r in range(NCORES):
            P.dma("sp", KN[b][:, r * NTL:(r + 1) * NTL], KKg[r * KKR + h * 128:r * KKR + (h + 1) * 128, :],
                  w=[(KN[b], r)], key=("kn", b, r))
            P.dma("sp", V[b][:, r * 16:(r + 1) * 16, :],
                  VLg[r * D + h * 128:r * D + (h + 1) * 128, :].rearrange("p (t d) -> p t d", d=128),
                  w=[(V[b], r)], key=("v", b, r))
            P.dma("sp", Vc[b][(r % 4) * 32:(r % 4) * 32 + 32, r // 4, :], VCg[r * CTX + h * CPC:r * CTX + (h + 1) * CPC, :],
                  w=[(Vc[b], r)], key=("vc", b, r))
        for r in range(NCORES):
            P.add("pool", lambda e, r=r, b=b: e.tensor_copy(out=KNc[b][:, r * CPC:(r + 1) * CPC],
                                                             in_=KN[b][:, r * NTL + TPC:(r + 1) * NTL]),
                  r=[(KN[b], r)], w=[(KNc[b], r)])

    load_head(0)
    for h in range(8):
        hb = h % 2
        if h + 1 < 8:
            load_head(h + 1)
        lat_tiles = [(KN[hb], r * NTL + t * 128, KR, r * NTL + t * 128, V[hb], r * 16 + t) for r in range(NCORES) for t in range(16)]
        ctx_tiles = [(KNc[hb], t * 128, KRc, t * 128, Vc[hb], t) for t in range(2)]
        for (off, wd) in PCH:
            is_ctx = off >= TPC
            tiles = ctx_tiles if is_ctx else lat_tiles + ctx_tiles
            nkt = len(tiles)
            s = cn["q"] % 2
            cn["q"] += 1
            P.dma("sp", QN[:, s, 0:wd], Qd[h * 192:h * 192 + 128, off:off + wd], w=[(QN, s)], key=("qn", s))
            P.dma("sp", QR[:, s, 0:wd], Qd[h * 192 + 128:(h + 1) * 192, off:off + wd], w=[(QR, s)], key=("qr", s))
            po = PO[s]
            psm = PSM[s]

            def qk(ti, b, wd=wd, s=s, tiles=tiles):
                kn_t, kc0, kr_t, rc0, _, _ = tiles[ti]
                P.add("pe", lambda e, b=b, wd=wd, s=s, kn_t=kn_t, kc0=kc0: e.matmul(
                    PS[b][:, 0:wd], lhsT=kn_t[:, kc0:kc0 + 128], rhs=QN[:, s, 0:wd], start=True, stop=False),
                    r=[(kn_t, None), (QN, s)], w=[(PS[b], None)])
                P.add("pe", lambda e, b=b, wd=wd, s=s, kr_t=kr_t, rc0=rc0: e.matmul(
                    PS[b][:, 0:wd], lhsT=kr_t[:, rc0:rc0 + 128], rhs=QR[:, s, 0:wd], start=False, stop=True),
                    r=[(kr_t, None), (QR, s)], w=[(PS[b], None)])

            bs = []
            for t0_ in range(min(2, nkt)):
                b0 = cn["ps"] % 4
                cn["ps"] += 1
                qk(t0_, b0)
                bs.append(b0)
            for ti in range(nkt):
                if ti + 2 < nkt:
                    b1 = cn["ps"] % 4
                    cn["ps"] += 1
                    qk(ti + 2, b1)
                    bs.append(b1)
                b = bs[ti]
                p = cn["pt"] % 4
                cn["pt"] += 1
                v_t, vt = tiles[ti][4], tiles[ti][5]
                P.add("act", lambda e, b=b, p=p, wd=wd: e.activation(out=PT[:, p, 0:wd], in_=PS[b][:, 0:wd], func=AF.Exp, scale=scale),
                      r=[(PS[b], None)], w=[(PT, p)])
                P.add("pe", lambda e, ti=ti, p=p, wd=wd, po=po, nkt=nkt, v_t=v_t, vt=vt: e.matmul(
                    po[:, 0:wd], lhsT=v_t[:, vt, :], rhs=PT[:, p, 0:wd], start=(ti == 0), stop=(ti == nkt - 1)),
                    r=[(v_t, None), (PT, p)], w=[(po, None)])
                eng, acc = ("dve", ACCd) if ti % 2 == 0 else ("pool", ACCp)
                if ti < 2:
                    P.add(eng, lambda e, p=p, wd=wd, s=s, acc=acc: e.tensor_copy(out=acc[:, s, 0:wd], in_=PT[:, p, 0:wd]),
                          r=[(PT, p)], w=[(acc, s)])
                else:
                    P.add(eng, lambda e, p=p, wd=wd, s=s, acc=acc: e.tensor_tensor(out=acc[:, s, 0:wd], in0=PT[:, p, 0:wd],
                                                                                 in1=acc[:, s, 0:wd], op=ALU.add),
                          r=[(PT, p), (acc, s)], w=[(acc, s)])
            P.add("pe", lambda e, wd=wd, s=s, psm=psm: e.matmul(psm[:, 0:wd], lhsT=ONESf[:], rhs=ACCd[:, s, 0:wd], start=True, stop=False),
                  r=[(ONESf, None), (ACCd, s)], w=[(psm, None)])
            P.add("pe", lambda e, wd=wd, s=s, psm=psm: e.matmul(psm[:, 0:wd], lhsT=ONESf[:], rhs=ACCp[:, s, 0:wd], start=False, stop=True),
                  r=[(ONESf, None), (ACCp, s)], w=[(psm, None)])
            P.add("dve", lambda e, s=s, wd=wd, psm=psm: e.reciprocal(out=RC[:, s, 0:wd], in_=psm[:, 0:wd]), r=[(psm, None)], w=[(RC, s)])
            P.add("dve", lambda e, s=s, wd=wd, po=po: e.tensor_tensor(out=OS[:, s, 0:wd], in0=po[:, 0:wd], in1=RC[:, s, 0:wd], op=ALU.mult),
                  r=[(po, None), (RC, s)], w=[(OS, s)])
            P.dma("sp", oT[h * 128:(h + 1) * 128, off:off + wd], OS[:, s, 0:wd], r=[(OS, s)], key=("os", s), final=True)
    P.emit()


def _select(P, eng, out, srcs, oh, r_lists, w_list, first_is_write=True):
    for r, src in enumerate(srcs):
        if r == 0:
            P.add(eng, lambda e, src=src: e.tensor_scalar(out=out, in0=src, scalar1=oh[:, 0:1], scalar2=None, op0=ALU.mult),
                  r=r_lists, w=w_list)
        else:
            P.add(eng, lambda e, src=src, r=r: e.scalar_tensor_tensor(out=out, in0=src, scalar=oh[:, r:r + 1], in1=out,
                                                                      op0=ALU.mult, op1=ALU.add),
                  r=r_lists + w_list, w=w_list)


def stage_pool(ctx, hT, edgeg, rc, ohLd, ohRd, mT):
    P = Prog(ctx)
    L = TPC + 16 + CPC + 16
    segs = [(0, TPC, 0), (TPC + 16, CPC, TPC)]
    EG = P.sb("EG", [128, NCORES, 8, 32], F32)
    OHL = P.sb("OHL", [128, NCORES], F32)
    OHR = P.sb("OHR", [128, NCORES], F32)
    HAL = P.sb("HAL", [128, 4, 8, 8], F32)
    A = [P.sb("A%d" % i, [128, L], F32) for i in range(2)]
    B = [P.sb("Bw%d" % i, [128, L], F32) for i in range(2)]
    RCs = [P.sb("RC%d" % i, [128, NTL], F32) for i in range(2)]
    O = [P.sb("O%d" % i, [128, NTL], F32) for i in range(2)]
    P.dma("sp", EG[:], edgeg.rearrange("(r k p) c -> p r k c", r=NCORES, p=128), w=[(EG, None)], key="eg")
    P.dma("sp", OHL[:], ohLd, w=[(OHL, None)], key="ohl")
    P.dma("sp", OHR[:], ohRd, w=[(OHR, None)], key="ohr")
    for hi, (oh, c0) in enumerate(((OHL, 8), (OHR, 0), (OHL, 24), (OHR, 16))):
        _select(P, "dve", HAL[:, hi], [EG[:, r, :, c0:c0 + 8] for r in range(NCORES)], oh,
                [(EG, None), (oh, None)], [(HAL, hi)])
    for k in range(8):
        g = k // 2
        a = A[k % 2]
        r_ = RCs[k % 2]
        o_ = O[k % 2]
        P.dma("sp", a[:, 8:8 + TPC], hT[k * 128:(k + 1) * 128, 0:TPC], w=[(a, "m")], key=("a", k % 2))
        P.dma("sp", a[:, TPC + 24:TPC + 24 + CPC], hT[k * 128:(k + 1) * 128, TPC:NTL], w=[(a, "c")], key=("a", k % 2))
        P.dma("sp", r_[:], rc[k * 128:(k + 1) * 128, :], w=[(r_, None)], key=("r", k % 2))
        for hi, c0 in ((0, 0), (1, TPC + 8), (2, TPC + 16), (3, TPC + 24 + CPC)):
            P.add("pool", lambda e, a=a, hi=hi, c0=c0, k=k: e.tensor_copy(out=a[:, c0:c0 + 8], in_=HAL[:, hi, k, :]),
                  r=[(HAL, hi)], w=[(a, ("h", hi))])
        cur = a
        lo, hi_ = 0, L
        for lev in range(g + 1):
            sh = 1 if lev == 0 else (1 << (lev - 1))
            dst = B[lev % 2]
            if lev == 0:
                nlo, nhi = lo + 1, hi_
                P.add("dve", lambda e, cur=cur, dst=dst, nlo=nlo, nhi=nhi: e.tensor_tensor(
                    out=dst[:, nlo:nhi], in0=cur[:, nlo - 1:nhi - 1], in1=cur[:, nlo:nhi], op=ALU.add),
                    r=[(cur, None)], w=[(dst, None)])
            else:
                nlo, nhi = lo + sh, hi_ - sh
                P.add("dve", lambda e, cur=cur, dst=dst, nlo=nlo, nhi=nhi, sh=sh: e.tensor_tensor(
                    out=dst[:, nlo:nhi], in0=cur[:, nlo - sh:nhi - sh], in1=cur[:, nlo + sh:nhi + sh], op=ALU.add),
                    r=[(cur, None)], w=[(dst, None)])
            cur, lo, hi_ = dst, nlo, nhi
        for (b, n, oo) in segs:
            P.add("dve", lambda e, cur=cur, b=b, n=n, oo=oo, o_=o_, r_=r_: e.tensor_tensor(
                out=o_[:, oo:oo + n], in0=cur[:, b + 8:b + 8 + n], in1=r_[:, oo:oo + n], op=ALU.mult),
                r=[(cur, None), (r_, None)], w=[(o_, oo)])
            P.add("dve", lambda e, a=a, b=b, n=n, oo=oo, o_=o_: e.tensor_tensor(
                out=o_[:, oo:oo + n], in0=o_[:, oo:oo + n], in1=a[:, b + 8:b + 8 + n], op=ALU.subtract),
                r=[(o_, oo), (a, None)], w=[(o_, oo)])
        P.dma("sp", mT[k * 128:(k + 1) * 128, :], o_[:], r=[(o_, None)], key=("o", k % 2), final=True)
    P.emit()


def stage_conve(ctx, bcu, ecg, cw, ohLd, ohRd, mT):
    P = Prog(ctx)
    n = TPC
    ECS = P.sb("ECS", [128, NCORES, 16, 2], F32)
    OHL = P.sb("OHL", [128, NCORES], F32)
    OHR = P.sb("OHR", [128, NCORES], F32)
    HL = P.sb("HL", [128, 16, 1], F32)
    HR = P.sb("HR", [128, 16, 1], F32)
    CW = P.sb("CW", [128, 8, 3], F32)
    Bt = [P.sb("B%d" % i, [128, n], F32) for i in range(2)]
    Ct = [P.sb("C%d" % i, [128, n + 2], F32) for i in range(2)]
    Ut = [P.sb("U%d" % i, [128, n + 2], F32) for i in range(2)]
    Z = [P.sb("Z%d" % i, [128, n], F32) for i in range(2)]
    P.dma("sp", CW[:], cw, w=[(CW, None)], key="cw")
    P.dma("sp", ECS[:], ecg.rearrange("(r k p) c -> p r k c", r=NCORES, p=128), w=[(ECS, None)], key="ecs")
    P.dma("sp", OHL[:], ohLd, w=[(OHL, None)], key="ohl")
    P.dma("sp", OHR[:], ohRd, w=[(OHR, None)], key="ohr")
    _select(P, "dve", HL[:], [ECS[:, r, :, 1:2] for r in range(NCORES)], OHL, [(ECS, None), (OHL, None)], [(HL, None)])
    _select(P, "dve", HR[:], [ECS[:, r, :, 0:1] for r in range(NCORES)], OHR, [(ECS, None), (OHR, None)], [(HR, None)])
    for k in range(8):
        s = k % 2
        b_, c_, u_, z_ = Bt[s], Ct[s], Ut[s], Z[s]
        P.dma("sp", b_[:], bcu[k * 128:(k + 1) * 128, 0:n], w=[(b_, None)], key=("b", s))
        P.dma("sp", c_[:, 1:n + 1], bcu[D + k * 128:D + (k + 1) * 128, 0:n], w=[(c_, "m")], key=("c", s))
        P.dma("sp", u_[:, 1:n + 1], bcu[2 * D + k * 128:2 * D + (k + 1) * 128, 0:n], w=[(u_, "m")], key=("u", s))
        for (t_, kk) in ((c_, k), (u_, 8 + k)):
            P.add("pool", lambda e, t_=t_, kk=kk: e.tensor_copy(out=t_[:, 0:1], in_=HL[:, kk, :]), r=[(HL, None)], w=[(t_, "l")])
            P.add("pool", lambda e, t_=t_, kk=kk: e.tensor_copy(out=t_[:, n + 1:n + 2], in_=HR[:, kk, :]), r=[(HR, None)], w=[(t_, "r")])
        P.add("dve", lambda e, c_=c_, u_=u_: e.tensor_tensor(out=c_[:], in0=c_[:], in1=u_[:], op=ALU.mult),
              r=[(c_, None), (u_, None)], w=[(c_, None)])
        P.add("dve", lambda e, c_=c_, z_=z_, k=k: e.tensor_scalar(out=z_[:], in0=c_[:, 0:n], scalar1=CW[:, k, 0:1], scalar2=None, op0=ALU.mult),
              r=[(c_, None), (CW, None)], w=[(z_, None)])
        for j in (1, 2):
            P.add("dve", lambda e, c_=c_, z_=z_, k=k, j=j: e.scalar_tensor_tensor(
                out=z_[:], in0=c_[:, j:j + n], scalar=CW[:, k, j:j + 1], in1=z_[:], op0=ALU.mult, op1=ALU.add),
                r=[(c_, None), (CW, None), (z_, None)], w=[(z_, None)])
        P.add("dve", lambda e, b_=b_, z_=z_: e.tensor_tensor(out=z_[:], in0=z_[:], in1=b_[:], op=ALU.mult),
              r=[(z_, None), (b_, None)], w=[(z_, None)])
        P.dma("sp", mT[k * 128:(k + 1) * 128, 0:n], z_[:], r=[(z_, None)], key=("z", s), final=True)
    P.emit()


NA_H = 16
NA_EXT = 2560
NA_NK = NA_EXT + CTX
NA_NT = NA_NK // 128
NA_HW = 512 + CPC


def stage_natt(ctx, qk, Vna, KHg, VHg, bias, ident, ohLd, ohRd, mT):
    P = Prog(ctx)
    KT = P.sb("KT", [128, 8, NA_NK], BF16)
    QT = P.sb("QT", [128, 8, TPC], BF16)
    VB = P.sb("VB", [128, NA_NT, D], BF16)
    ST = P.sb("ST", [128, 2, 1024], F32)
    HS = P.sb("HS", [128, NCORES, NA_HW], F32)
    ACC = P.sb("ACC", [128, 2, 512], F32)
    VCS = P.sb("VCS", [128, 512], F32)
    BS = P.sb("BS", [128, 1536], F32)
    BB = P.sb("BB", [128, 2, 3 * 6 * 256], BF16)
    IDf = P.sb("IDf", [128, 128], F32)
    ID = P.sb("ID", [128, 128], BF16)
    ONES = P.sb("ONES", [128, 64], BF16)
    OHL = P.sb("OHL", [128, NCORES], F32)
    OHR = P.sb("OHR", [128, NCORES], F32)
    PT = P.sb("PT", [128, 3, 256], BF16)
    RC = P.sb("RC", [64, 2, 256], F32)
    OS = P.sb("OS", [64, 2, 256], F32)
    PS = [P.ps("ps%d" % i, [128, 512]) for i in range(3)]
    PO = [P.ps("po%d" % i, [128, 512]) for i in range(2)]
    PSM = [P.ps("psm%d" % i, [128, 512]) for i in range(2)]
    P.add("dve", lambda e: e.memset(ONES[:], 1.0), w=[(ONES, None)])
    P.dma("sp", IDf[:], ident, w=[(IDf, None)], key="id")
    P.add("dve", lambda e: e.tensor_copy(out=ID[:], in_=IDf[:]), r=[(IDf, None)], w=[(ID, None)])
    P.dma("sp", OHL[:], ohLd, w=[(OHL, None)], key="ohl")
    P.dma("sp", OHR[:], ohRd, w=[(OHR, None)], key="ohr")
    n = 0
    KHv = KHg.rearrange("(r k p) c -> p r k c", r=NCORES, p=128)
    for k in range(8):
        for c0 in range(0, TPC, 1024):
            s = n % 2
            n += 1
            P.dma("sp", ST[:, s, :], qk[D + k * 128:D + (k + 1) * 128, c0:c0 + 1024], w=[(ST, s)], key=("st", s))
            P.add("pool", lambda e, s=s, c0=c0, k=k: e.tensor_copy(out=KT[:, k, 256 + c0:256 + c0 + 1024], in_=ST[:, s, :]),
                  r=[(ST, s)], w=[(KT, (k, "o", c0))])
            s = n % 2
            n += 1
            P.dma("sp", ST[:, s, :], qk[k * 128:(k + 1) * 128, c0:c0 + 1024], w=[(ST, s)], key=("st", s))
            P.add("pool", lambda e, s=s, c0=c0, k=k: e.tensor_scalar(out=QT[:, k, c0:c0 + 1024], in0=ST[:, s, :], scalar1=0.125,
                                                                    scalar2=None, op0=ALU.mult),
                  r=[(ST, s)], w=[(QT, (k, c0))])
        P.dma("sp", HS[:], KHv[:, :, k, :], w=[(HS, None)], key="hs")
        _select(P, "dve", ACC[:, 0, 0:256], [HS[:, r, 256:512] for r in range(NCORES)], OHL, [(HS, None), (OHL, None)], [(ACC, 0)])
        _select(P, "dve", ACC[:, 1, 0:256], [HS[:, r, 0:256] for r in range(NCORES)], OHR, [(HS, None), (OHR, None)], [(ACC, 1)])
        P.add("act", lambda e, k=k: e.activation(out=KT[:, k, 0:256], in_=ACC[:, 0, 0:256], func=AF.Identity),
              r=[(ACC, 0)], w=[(KT, (k, "a"))])
        P.add("act", lambda e, k=k: e.activation(out=KT[:, k, 2304:2560], in_=ACC[:, 1, 0:256], func=AF.Identity),
              r=[(ACC, 1)], w=[(KT, (k, "b"))])
        P.add("act", lambda e, k=k: e.activation(out=KT[:, k, NA_EXT:NA_NK].rearrange("p (r c) -> p r c", c=CPC),
                                                 in_=HS[:, :, 512:NA_HW], func=AF.Identity),
              r=[(HS, None)], w=[(KT, (k, "c"))])
    for t in range(16):
        for hh in range(0, D, 512):
            s = n % 2
            n += 1
            P.dma("sp", ST[:, s, 0:512], Vna[t * 128:(t + 1) * 128, hh:hh + 512], w=[(ST, s)], key=("st", s))
            P.add("pool", lambda e, s=s, t=t, hh=hh: e.tensor_copy(out=VB[:, 2 + t, hh:hh + 512], in_=ST[:, s, 0:512]),
                  r=[(ST, s)], w=[(VB, (2 + t, hh))])
    for (tile0, row0, oh) in ((0, 256, OHL), (18, 0, OHR)):
        for tt in range(2):
            for hh in range(0, D, 512):
                for r in range(NCORES):
                    P.dma("sp", HS[:, r, 0:512], VHg[r * NA_HW + row0 + tt * 128:r * NA_HW + row0 + (tt + 1) * 128, hh:hh + 512],
                          w=[(HS, r)], key=("hsv", r))
                a = n % 2
                n += 1
                _select(P, "dve", ACC[:, a, :], [HS[:, r, 0:512] for r in range(NCORES)], oh, [(HS, None), (oh, None)], [(ACC, a)])
                P.add("act", lambda e, a=a, tile0=tile0, tt=tt, hh=hh: e.activation(
                    out=VB[:, tile0 + tt, hh:hh + 512], in_=ACC[:, a, :], func=AF.Identity),
                    r=[(ACC, a)], w=[(VB, (tile0 + tt, hh))])
    for hh in range(0, D, 512):
        for t2 in range(2):
            for r4 in range(4):
                r = t2 * 4 + r4
                P.dma("sp", VCS[r4 * 32:(r4 + 1) * 32, :], VHg[r * NA_HW + 512:r * NA_HW + 512 + CPC, hh:hh + 512],
                      w=[(VCS, r4)], key="vcs")
            P.add("pool", lambda e, t2=t2, hh=hh: e.tensor_copy(out=VB[:, 20 + t2, hh:hh + 512], in_=VCS[:]),
                  r=[(VCS, None)], w=[(VB, (20 + t2, hh))])
    jobs = []
    for h in range(NA_H):
        for b in range(8):
            for ti, (kind, t) in enumerate([("l", t) for t in range(6)] + [("c", 0), ("c", 1)]):
                jobs.append((h, b, ti, kind, t))
    PSn = PS + [P.ps("ps3", [128, 512])]
    psb = {}

    def qk(j):
        h, b, ti, kind, t = jobs[j]
        kc, p0, bb = h // 2, (h % 2) * 64, h % 2
        if b == 0 and ti == 0:
            for var in range(3):
                P.dma("sp", BS[:], bias[h, :, var * 1536:(var + 1) * 1536], w=[(BS, None)], key="bs")
                P.add("pool", lambda e, bb=bb, var=var: e.tensor_copy(out=BB[:, bb, var * 1536:(var + 1) * 1536], in_=BS[:]),
                      r=[(BS, None)], w=[(BB, (bb, var))])
        var = 0 if b == 0 else (2 if b == 7 else 1)
        q0 = b * 256
        ps = PSn[j % 4]
        psb[j] = ps
        k0 = (4 * b + 2 * t) * 64 if kind == "l" else NA_EXT + t * 128
        P.add("pe", lambda e, ps=ps, kc=kc, p0=p0, k0=k0, q0=q0, kind=kind: e.matmul(
            ps[:, 0:256], lhsT=KT[p0:p0 + 64, kc, k0:k0 + 128], rhs=QT[p0:p0 + 64, kc, q0:q0 + 256],
            start=True, stop=(kind == "c")),
            r=[(KT, None), (QT, None)], w=[(ps, None)])
        if kind == "l":
            bo = (var * 6 + t) * 256
            P.add("pe", lambda e, ps=ps, bb=bb, bo=bo: e.matmul(
                ps[:, 0:256], lhsT=ID[:], rhs=BB[:, bb, bo:bo + 256], start=False, stop=True),
                r=[(ID, None), (BB, (bb, var))], w=[(ps, None)])

    for j in range(min(2, len(jobs))):
        qk(j)
    for j, (h, b, ti, kind, t) in enumerate(jobs):
        if j + 2 < len(jobs):
            qk(j + 2)
        g = j // 8
        po = PO[g % 2]
        psm = PSM[g % 2]
        osl = g % 2
        q0 = b * 256
        k0 = (4 * b + 2 * t) * 64 if kind == "l" else NA_EXT + t * 128
        vt = k0 // 128
        ps = psb[j]
        pt = j % 3
        P.add("act", lambda e, ps=ps, pt=pt: e.activation(out=PT[:, pt, :], in_=ps[:, 0:256], func=AF.Exp),
              r=[(ps, None)], w=[(PT, pt)])
        P.add("pe", lambda e, po=po, vt=vt, h=h, pt=pt, ti=ti: e.matmul(
            po[0:64, 0:256], lhsT=VB[:, vt, h * 64:(h + 1) * 64], rhs=PT[:, pt, :], start=(ti == 0), stop=(ti == 7)),
            r=[(VB, None), (PT, pt)], w=[(po, None)])
        P.add("pe", lambda e, psm=psm, pt=pt, ti=ti: e.matmul(
            psm[0:64, 0:256], lhsT=ONES[:], rhs=PT[:, pt, :], start=(ti == 0), stop=(ti == 7)),
            r=[(ONES, None), (PT, pt)], w=[(psm, None)])
        if ti == 7:
            P.add("dve", lambda e, psm=psm, osl=osl: e.reciprocal(out=RC[:, osl, :], in_=psm[0:64, 0:256]),
                  r=[(psm, None)], w=[(RC, osl)])
            P.add("dve", lambda e, po=po, osl=osl: e.tensor_tensor(out=OS[:, osl, :], in0=po[0:64, 0:256], in1=RC[:, osl, :], op=ALU.mult),
                  r=[(po, None), (RC, osl)], w=[(OS, osl)])
            P.dma("sp", mT[h * 64:(h + 1) * 64, q0:q0 + 256], OS[:, osl, :], r=[(OS, osl)], key=("os", osl), final=True)
    P.emit()


def stage_edges(ctx, src, nchunk, specs, dst, width):
    P = Prog(ctx)
    E = P.sb("E", [128, nchunk, width], F32)
    sv = src.rearrange("(k p) n -> p k n", p=128)
    for i, (c0, n, d0) in enumerate(specs):
        P.dma("sp", E[:, :, d0:d0 + n], sv[:, :, c0:c0 + n], w=[(E, i)], key="ld", allow_slow_non_contiguous=True)
    P.dma("sp", dst.rearrange("(k p) c -> p k c", p=128), E[:], r=[(E, None)], key="st", final=True,
          allow_slow_non_contiguous=True)
    P.emit()


def build_fused():
    ctx = Ctx()
    ext = lambda name, shape: ctx.dram(name, shape, F32, "ExternalInput")
    xT = ext("xT", [D, NTL])
    cT = ext("cT", [128, 8, 2]); mw = ext("mw", [4, D, 1152]); mb = ext("mb", [128, 4, 9])
    w_in = ext("ffn_w_in", [4, 2, D, 2 * DFF]); w_out = ext("ffn_w_out", [4, 2, DFF, D])
    Wlat = ext("Wlat", [D, 704]); Glat = ext("Glat", [128, 6])
    w_uq = ext("w_uq", [384, 1536]); Gq = ext("Gq", [128, 16])
    w_uk = ext("w_uk", [256, D]); Gk = ext("Gk", [128, 8]); w_uv = ext("w_uv", [256, D])
    cos = ext("cos", [64, NTL]); sin = ext("sin", [64, NTL]); psw = ext("psw", [64, 64])
    mla_w_o = ext("mla_w_o", [D, D]); ones8 = ext("ones8", [128, 8])
    wbd = ext("wbd", [D, D]); psv = ext("psv", [128, 8]); rc = ext("rc", [D, NTL])
    ohL = ext("ohL", [128, NCORES]); ohR = ext("ohR", [128, NCORES])
    w_qkv = ext("w_qkv", [D, 3 * D]); Gna = ext("Gna", [128, 16])
    bias = ext("bias", [NA_H, 128, 3 * 6 * 256]); ident = ext("ident", [128, 128]); na_w_o = ext("na_w_o", [D, D])
    conv_w_in = ext("conv_w_in", [D, 3 * D]); Gc = ext("Gc", [128, 24]); cw = ext("cw", [128, 8, 3]); conv_w_out = ext("conv_w_out", [D, D])
    xo = ctx.dram("xo", [D, NTL], F32, "ExternalOutput")

    modl = ctx.dram("modl", [128, 72]); modg = ctx.dram("modg", [NCORES * 128, 72])
    xa = ctx.dram("xa", [D, NTL]); xb = ctx.dram("xb", [D, NTL]); h = ctx.dram("h", [D, NTL])
    lat = ctx.dram("lat", [640, NTL]); Qd = ctx.dram("Qd", [1536, NTL], BF16)
    KK = ctx.dram("KK", [KKR, NTL], BF16)
    VL = ctx.dram("VL", [D, TPC], BF16)
    VC = ctx.dram("VC", [8 * CPC, 128], BF16)
    KKg = ctx.dram("KKg", [NCORES * KKR, NTL], BF16)
    VLg = ctx.dram("VLg", [NCORES * D, TPC], BF16); VCg = ctx.dram("VCg", [NCORES * 8 * CPC, 128], BF16)
    oT = ctx.dram("oT", [D, NTL])
    edgeP = ctx.dram("edgeP", [D, 32]); edgePg = ctx.dram("edgePg", [NCORES * D, 32])
    mTp = ctx.dram("mTp", [D, NTL])
    qk = ctx.dram("qk", [2 * D, NTL]); Vna = ctx.dram("Vna", [NTL, D])
    KH = ctx.dram("KH", [D, NA_HW]); VH = ctx.dram("VH", [NA_HW, D])
    KHg = ctx.dram("KHg", [NCORES * D, NA_HW]); VHg = ctx.dram("VHg", [NCORES * NA_HW, D])
    mTn = ctx.dram("mTn", [D, NTL])
    bcu = ctx.dram("bcu", [3 * D, NTL])
    EC = ctx.dram("EC", [2 * D, 2]); ECg = ctx.dram("ECg", [NCORES * 2 * D, 2])
    mTc = ctx.dram("mTc", [D, NTL])

    stage_mod(ctx, cT, mw, mb, modl)
    ctx.allgather(modl[:, :], modg[:, :])
    stage_tl(ctx, None, 0, xT, xa, modg, post=dict(w_in=w_in[0, 0], w_out=w_out[0, 0], h_out=h))

    def lat_out(col, part, off, wd):
        if col < 640:
            return lat[col:col + part, off:off + wd]
        return KK[D:D + 64, off:off + wd]
    rope = (cos, sin, psw)
    stage_proj(ctx, 8, h, Wlat, Glat, [(0, 3, 128, "full", False), (384, 2, 128, "full", False), (640, 1, 64, "full", True)],
               704, lat_out, rope, obf=lambda col: col >= 640)
    gq = []
    for hh in range(8):
        gq += [(hh * 192, 1, 128, "full", False), (hh * 192 + 128, 1, 64, "full", True)]
    stage_proj(ctx, 3, lat[0:384, :], w_uq, Gq, gq, 1536, lambda col, part, off, wd: Qd[col:col + part, off:off + wd], rope,
               obf=lambda col: True)
    stage_proj(ctx, 2, lat[384:640, :], w_uk, Gk, [(hh * 128, 1, 128, "full", False) for hh in range(8)], D,
               lambda col, part, off, wd: KK[col:col + part, off:off + wd], obf=lambda col: True)

    def v_out(tok0, nt, c0, cw, src):
        h0, nh = c0 // 128, cw // 128
        sv = src.rearrange("p (h d) -> p h d", d=128)
        if tok0 < TPC:
            t = tok0 // 128
            return VL[h0 * 128:(h0 + nh) * 128, t * 128:(t + 1) * 128].rearrange("(h p) d -> p h d", p=128), sv
        return VC[h0 * CPC:(h0 + nh) * CPC, :].rearrange("(h p) d -> p h d", p=CPC), sv
    stage_projT(ctx, 2, lat[384:640, :], w_uv, D, v_out, bf16=True)
    ctx.allgather(KK[:, :], KKg[:, :])
    ctx.allgather(VL[:, :], VLg[:, :])
    ctx.allgather(VC[:, :], VCg[:, :])
    stage_att(ctx, Qd, KKg, VLg, VCg, oT, float(192 ** -0.5))
    stage_tl(ctx, 0, 1, xa, xb, modg, pre=dict(mT=oT, wo=mla_w_o, ps=ones8, w_in=w_in[0, 1], w_out=w_out[0, 1]),
             post=dict(w_in=w_in[1, 0], w_out=w_out[1, 0], h_out=h))

    stage_edges(ctx, h, 8, [(0, 8, 0), (TPC - 8, 8, 8), (TPC, 8, 16), (NTL - 8, 8, 24)], edgeP, 32)
    ctx.allgather(edgeP[:, :], edgePg[:, :])
    stage_pool(ctx, h, edgePg, rc, ohL, ohR, mTp)
    stage_tl(ctx, 1, 2, xb, xa, modg, pre=dict(mT=mTp, wo=wbd, ps=psv, w_in=w_in[1, 1], w_out=w_out[1, 1]),
             post=dict(w_in=w_in[2, 0], w_out=w_out[2, 0], h_out=h))

    stage_proj(ctx, 8, h, w_qkv[:, 0:2 * D], Gna, [(cc * 128, 1, 128, "blk64", False) for cc in range(16)], 2 * D,
               lambda col, part, off, wd: qk[col:col + part, off:off + wd])
    stage_projT(ctx, 8, h, w_qkv[:, 2 * D:3 * D], D,
                lambda tok0, nt, c0, cw, src: (Vna[tok0:tok0 + nt, c0:c0 + cw], src))
    stage_edges(ctx, qk[D:2 * D, :], 8, [(0, 256, 0), (TPC - 256, 256, 256), (TPC, CPC, 512)], KH, NA_HW)
    stage_copy(ctx, [(VH[0:128, :], Vna[0:128, :], 128, D), (VH[128:256, :], Vna[128:256, :], 128, D),
                     (VH[256:384, :], Vna[TPC - 256:TPC - 128, :], 128, D), (VH[384:512, :], Vna[TPC - 128:TPC, :], 128, D),
                     (VH[512:NA_HW, :], Vna[TPC:NTL, :], CPC, D)],
               zero=[(mTn[k * 128:(k + 1) * 128, TPC:NTL], 128, CPC) for k in range(8)]
               + [(mTc[k * 128:(k + 1) * 128, TPC:NTL], 128, CPC) for k in range(8)])
    ctx.allgather(KH[:, :], KHg[:, :])
    ctx.allgather(VH[:, :], VHg[:, :])
    stage_natt(ctx, qk, Vna, KHg, VHg, bias, ident, ohL, ohR, mTn)
    stage_tl(ctx, 2, 3, xa, xb, modg, pre=dict(mT=mTn, wo=na_w_o, ps=ones8, w_in=w_in[2, 1], w_out=w_out[2, 1]),
             post=dict(w_in=w_in[3, 0], w_out=w_out[3, 0], h_out=h))

    stage_proj(ctx, 8, h, conv_w_in, Gc, [(0, 24, 128, None, False)], 3 * D,
               lambda col, part, off, wd: bcu[col:col + part, off:off + wd])
    stage_edges(ctx, bcu[D:3 * D, :], 16, [(0, 1, 0), (TPC - 1, 1, 1)], EC, 2)
    ctx.allgather(EC[:, :], ECg[:, :])
    stage_conve(ctx, bcu, ECg, cw, ohL, ohR, mTc)
    stage_tl(ctx, 3, None, xb, xo, modg, pre=dict(mT=mTc, wo=conv_w_out, ps=ones8, w_in=w_in[3, 1], w_out=w_out[3, 1]))
    return ctx.finish()


_cache = {}


def _f32(a):
    return np.ascontiguousarray(a, dtype=np.float32)


def _col(v):
    o = np.ones((128,), np.float32)
    o[:v.shape[0]] = v
    return o


def _rope_tables():
    freqs = (np.float32(10000.0) ** (-np.arange(16, dtype=np.float32) / np.float32(16))).astype(np.float32)
    t = np.arange(SEQ)
    row = (t // 64).astype(np.float32)
    col = (t % 64).astype(np.float32)
    ang = np.concatenate([row[:, None] * freqs, col[:, None] * freqs], axis=-1).astype(np.float32)
    cos = np.repeat(np.cos(ang), 2, axis=1).T.astype(np.float32)
    sin = np.repeat(np.sin(ang), 2, axis=1).T.astype(np.float32)
    coss, sins = [], []
    for r in range(NCORES):
        coss.append(np.concatenate([cos[:, r * TPC:(r + 1) * TPC], np.ones((64, CPC), np.float32)], axis=1))
        sins.append(np.concatenate([sin[:, r * TPC:(r + 1) * TPC], np.zeros((64, CPC), np.float32)], axis=1))
    psw = np.zeros((64, 64), np.float32)
    for i in range(32):
        psw[2 * i + 1, 2 * i] = -1.0
        psw[2 * i, 2 * i + 1] = 1.0
    return coss, sins, psw


def _na_bias(rpb, base):
    t = np.arange(6)[:, None, None]
    kk = np.arange(128)[None, :, None]
    qq = np.arange(256)[None, None, :]
    key_row = base - 4 + 2 * t + kk // 64
    kcol = kk % 64
    r = base + qq // 64
    qc = qq % 64
    rs = np.clip(r - 4, 0, 248)
    cs = np.clip(qc - 8, 0, 48)
    valid = (key_row >= rs) & (key_row < rs + 8) & (kcol >= cs) & (kcol < cs + 16) & (key_row >= 0) & (key_row < 256)
    dr = np.clip(key_row - r + 7, 0, 14)
    dc = np.clip(kcol - qc + 15, 0, 30)
    vals = rpb[:, dr, dc]
    return np.where(valid[None], vals, np.float32(-30000.0)).astype(np.float32)


def kernel(x, c, ctx, c_ctx, mod_w, mod_b, ffn_w_in, ffn_w_out,
           mla_w_dq, mla_g_dq, mla_w_uq, mla_w_dkv, mla_g_dkv, mla_w_uk, mla_w_uv,
           mla_g_qn, mla_g_qr, mla_g_kn, mla_g_kr, mla_w_o,
           pool_w, pool_scale,
           na_w_qkv, na_g_q, na_g_k, na_rpb, na_w_o,
           conv_w_in, conv_w, conv_w_out):
    from concourse.bass_utils import run_bass_kernel_spmd
    x = np.asarray(x, np.float32)[0]
    cx = np.asarray(ctx, np.float32)[0]
    R = range(NCORES)
    ones8 = np.ones((128, 8), np.float32)
    cc = np.stack([np.asarray(c, np.float32)[0], np.asarray(c_ctx, np.float32)], axis=-1)
    cT = _f32(cc.reshape(8, 128, 2).transpose(1, 0, 2))
    coss, sins, psw = _rope_tables()
    Wlat = _f32(np.concatenate([mla_w_dq[0], mla_w_dkv[0]], axis=1))
    Glat = _f32(np.stack([_col(mla_g_dq[0][0:128]), _col(mla_g_dq[0][128:256]), _col(mla_g_dq[0][256:384]),
                          _col(mla_g_dkv[0][0:128]), _col(mla_g_dkv[0][128:256]), _col(mla_g_kr[0])], axis=1))
    Gq = _f32(np.stack([_col(mla_g_qn[0]), _col(mla_g_qr[0])] * 8, axis=1))
    Gk = _f32(np.stack([_col(mla_g_kn[0])] * 8, axis=1))
    wbd = np.zeros((D, D), np.float32)
    for g in range(4):
        wbd[g * 256:(g + 1) * 256, g * 256:(g + 1) * 256] = pool_w[0, g]
    psv = _f32(np.asarray(pool_scale[0], np.float32).reshape(8, 128).T)
    half = np.repeat(np.array([1, 2, 4, 8]), 256)[:, None]

    def rcount(T, t):
        lo = np.clip(t[None, :] - half, 0, T)
        hi = np.clip(t[None, :] + half, 0, T)
        return (np.float32(1.0) / (hi - lo).astype(np.float32)).astype(np.float32)

    Gna = _f32(np.stack([np.tile(np.asarray(na_g_q[0], np.float32), 2)] * 8 + [np.tile(np.asarray(na_g_k[0], np.float32), 2)] * 8, axis=1))
    rpb = np.asarray(na_rpb[0], np.float32)
    b_int, b_top, b_bot = _na_bias(rpb, 8), _na_bias(rpb, 0), _na_bias(rpb, 252)
    ident = np.eye(128, dtype=np.float32)
    cw = _f32(np.asarray(conv_w[0], np.float32).T.reshape(8, 128, 3).transpose(1, 0, 2))
    shared = {
        "cT": cT, "ffn_w_in": _f32(ffn_w_in), "ffn_w_out": _f32(ffn_w_out), "Wlat": Wlat, "Glat": Glat,
        "w_uq": _f32(np.asarray(mla_w_uq[0]).reshape(384, 1536)), "Gq": Gq, "w_uk": _f32(np.asarray(mla_w_uk[0]).reshape(256, D)),
        "Gk": Gk, "w_uv": _f32(np.asarray(mla_w_uv[0]).reshape(256, D)), "psw": psw, "mla_w_o": _f32(mla_w_o[0]), "ones8": ones8,
        "wbd": wbd, "psv": psv, "w_qkv": _f32(na_w_qkv[0]), "Gna": Gna, "ident": ident, "na_w_o": _f32(na_w_o[0]),
        "conv_w_in": _f32(conv_w_in[0]), "Gc": np.ones((128, 24), np.float32), "cw": cw, "conv_w_out": _f32(conv_w_out[0]),
    }
    maps = []
    for r in R:
        d = dict(shared)
        d["xT"] = _f32(np.concatenate([x[r * TPC:(r + 1) * TPC].T, cx[r * CPC:(r + 1) * CPC].T], axis=1))
        d["mw"] = _f32(np.asarray(mod_w)[:, :, 1152 * r:1152 * (r + 1)])
        d["mb"] = _f32(np.asarray(mod_b)[:, 1152 * r:1152 * (r + 1)].reshape(4, 9, 128).transpose(2, 0, 1))
        d["cos"] = coss[r]
        d["sin"] = sins[r]
        d["rc"] = _f32(np.concatenate([rcount(SEQ, np.arange(r * TPC, (r + 1) * TPC)),
                                       rcount(CTX, np.arange(r * CPC, (r + 1) * CPC))], axis=1))
        ohL = np.zeros((128, NCORES), np.float32)
        ohR = np.zeros((128, NCORES), np.float32)
        if r > 0:
            ohL[:, r - 1] = 1.0
        if r < NCORES - 1:
            ohR[:, r + 1] = 1.0
        d["ohL"] = ohL
        d["ohR"] = ohR
        bv = np.stack([b_top if r == 0 else b_int, b_int, b_bot if r == NCORES - 1 else b_int], axis=1)
        d["bias"] = _f32(bv.transpose(0, 3, 1, 2, 4).reshape(NA_H, 128, 3 * 6 * 256))
        maps.append(d)
    if "nc" not in _cache:
        _cache["nc"] = build_fused()
    res = run_bass_kernel_spmd(_cache["nc"], maps, core_ids=list(R))
    out = np.concatenate([res.results[r]["xo"][:, :TPC].T for r in R], axis=0)
    return np.ascontiguousarray(out[None], dtype=np.float32)
```

```python
import numpy as np
from contextlib import ExitStack
import concourse.bass as bass
import concourse.mybir as mybir

F32 = mybir.dt.float32
BF16 = mybir.dt.bfloat16
AF = mybir.ActivationFunctionType
ALU = mybir.AluOpType

D = 1024
DFF = 2816
NJ = DFF // 128
EPS = 1e-6
NCORES = 8
SEQ = 16384
TPC = SEQ // NCORES
CTX = 256
CPC = CTX // NCORES
NTL = TPC + CPC
NKEY = NCORES * NTL
HALVES = [[(0, 512, 0), (512, 512, 0)], [(1024, 512, 0), (1536, 512, 0), (2048, CPC, 1)]]
PCH = [(0, 512), (512, 512), (1024, 512), (1536, 512), (2048, CPC)]


class Tile:
    __slots__ = ("t", "name", "acc")

    def __init__(self, t, name):
        self.t = t
        self.name = name
        self.acc = {}

    def __getitem__(self, idx):
        return self.t[idx]


class Op:
    __slots__ = ("eng", "fn", "waits", "sem", "val", "isdma")


class Ctx:
    ENGS = ("pe", "act", "dve", "pool", "sp")

    def __init__(self):
        self.nc = bass.Bass("TRN2", target_bir_lowering=False, num_devices=NCORES)
        self.es = ExitStack()
        self.esem = [{e: self.es.enter_context(self.nc.semaphore("se%d_%s" % (p, e))) for e in self.ENGS} for p in range(2)]
        self.dpool = [[], []]
        self.ccsem = self.es.enter_context(self.nc.semaphore("cc"))
        self.cccount = 0
        self.stage = 0
        self.names = 0

    def dram(self, name, shape, dt=F32, kind="Internal"):
        return self.nc.dram_tensor(name, list(shape), dt, kind=kind).ap()

    def dsem(self, parity, idx):
        pool = self.dpool[parity]
        while len(pool) <= idx:
            pool.append(self.es.enter_context(self.nc.semaphore("sd%d_%d" % (parity, len(pool)))))
        return pool[idx]

    def allgather(self, src, dst):
        nc = self.nc
        self.cccount += 1
        cnt = self.cccount
        with nc.Block() as block:
            @block.gpsimd
            def _(g):
                g.collective_compute("AllGather", ALU.bypass, replica_groups=[list(range(NCORES))],
                                     ins=[src], outs=[dst]).then_inc(self.ccsem)
                g.wait_ge(self.ccsem, cnt)

    def finish(self):
        self.es.close()
        return self.nc


class Prog:
    ENGS = Ctx.ENGS

    def __init__(self, ctx):
        self.ctx = ctx
        self.nc = ctx.nc
        self.par = ctx.stage % 2
        self.es = ExitStack()
        self.ops = {e: [] for e in self.ENGS}
        self.esem = ctx.esem[self.par]
        self.ecount = {e: 0 for e in self.ENGS}
        self.waited = {e: {} for e in self.ENGS}
        self.dkeys = {}
        self.dcount = {}
        self.final = []
        self.sid = ctx.stage

    def sb(self, name, shape, dt):
        nm = "s%d_%s" % (self.sid, name)
        return Tile(self.es.enter_context(self.nc.sbuf_tensor(nm, list(shape), dt)), nm)

    def ps(self, name, shape, dt=F32):
        nm = "s%d_%s" % (self.sid, name)
        return Tile(self.es.enter_context(self.nc.psum_tensor(nm, list(shape), dt)), nm)

    def _deps(self, reads, writes):
        deps = []
        for (tl, rg) in reads:
            for k, ent in tl.acc.items():
                if rg is None or k is None or k == rg:
                    if ent[0] is not None:
                        deps.append(ent[0])
        for (tl, rg) in writes:
            for k, ent in tl.acc.items():
                if rg is None or k is None or k == rg:
                    if ent[0] is not None:
                        deps.append(ent[0])
                    deps.extend(ent[1])
        return deps

    def _commit(self, op, reads, writes):
        for (tl, rg) in reads:
            ent = tl.acc.setdefault(rg, [None, []])
            ent[1].append(op)
        for (tl, rg) in writes:
            if rg is None:
                tl.acc.clear()
            tl.acc[rg] = [op, []]

    def add(self, eng, fn, r=(), w=(), dma_key=None):
        op = Op()
        op.eng = eng
        op.fn = fn
        op.isdma = dma_key is not None
        deps = self._deps(r, w)
        waits = {}
        wd = self.waited[eng]
        for d in deps:
            if d.eng == "pe" and eng == "pe" and not d.isdma:
                continue
            if wd.get(d.sem, 0) >= d.val:
                continue
            if waits.get(d.sem, (None, 0))[1] < d.val:
                waits[d.sem] = (d.sem, d.val)
        for s, v in waits.values():
            wd[s] = v
        op.waits = list(waits.values())
        if op.isdma:
            if dma_key not in self.dkeys:
                self.dkeys[dma_key] = self.ctx.dsem(self.par, len(self.dkeys))
                self.dcount[dma_key] = 0
            self.dcount[dma_key] += 16
            op.sem = self.dkeys[dma_key]
            op.val = self.dcount[dma_key]
        else:
            self.ecount[eng] += 1
            op.sem = self.esem[eng]
            op.val = self.ecount[eng]
        self._commit(op, r, w)
        self.ops[eng].append(op)
        return op

    def dma(self, eng, out, in_, r=(), w=(), key=None, final=False, **kw):
        op = self.add(eng, lambda e: e.dma_start(out=out, in_=in_, **kw), r=r, w=w, dma_key=key)
        if final:
            self.final.append(op)
        return op

    def emit(self):
        nc = self.nc
        ctx = self.ctx
        other = list(ctx.esem[1 - self.par].values()) + list(ctx.dpool[1 - self.par])
        with nc.Block() as block:
            def run(e, name):
                if name == "sp":
                    for s in other:
                        e.sem_clear(s)
                for op in self.ops[name]:
                    for s, v in op.waits:
                        e.wait_ge(s, v)
                    ins = op.fn(e)
                    ins.then_inc(op.sem, 16 if op.isdma else 1)
                if name == "sp":
                    for op in self.final:
                        e.wait_ge(op.sem, op.val)

            @block.tensor
            def _(e):
                run(e, "pe")

            @block.scalar
            def _(e):
                run(e, "act")

            @block.vector
            def _(e):
                run(e, "dve")

            @block.gpsimd
            def _(e):
                run(e, "pool")

            @block.sync
            def _(e):
                run(e, "sp")
        self.es.close()
        ctx.stage += 1


def mvec(T, l, v, c, ci):
    g = 8 * v + c
    r, j = divmod(g, 9)
    col = (l * 9 + j) * 2 + ci
    return T[:, r, col:col + 1]


def stage_mod(ctx, cT, mw, mb, modl):
    P = Prog(ctx)
    C = P.sb("C", [128, 8, 2], F32)
    SC = P.sb("SC", [128, 8, 2], F32)
    MB = P.sb("MB", [128, 4, 9], F32)
    W = [P.sb("W%d" % i, [128, 8, 1152], F32) for i in range(2)]
    O = P.sb("O", [128, 4, 9, 2], F32)
    PS = [P.ps("ps%d" % i, [128, 2]) for i in range(4)]
    P.dma("sp", C[:], cT, w=[(C, None)], key="c")
    P.dma("sp", MB[:], mb, w=[(MB, None)], key="mb")
    P.add("act", lambda e: e.activation(out=SC[:], in_=C[:], func=AF.Silu), r=[(C, None)], w=[(SC, None)])
    n = 0
    for l in range(4):
        Wl = W[l % 2]
        P.dma("sp", Wl[:], mw[l].rearrange("(k p) c -> p k c", p=128), w=[(Wl, None)], key=("w", l % 2))
        for j in range(9):
            pt = PS[n % 4]
            n += 1
            for k in range(8):
                P.add("pe", lambda e, pt=pt, Wl=Wl, k=k, j=j: e.matmul(
                    pt[:], lhsT=Wl[:, k, j * 128:(j + 1) * 128], rhs=SC[:, k, :], start=(k == 0), stop=(k == 7)),
                    r=[(Wl, None), (SC, None)], w=[(pt, None)])
            P.add("dve", lambda e, pt=pt, l=l, j=j: e.tensor_scalar(
                out=O[:, l, j, :], in0=pt[:], scalar1=MB[:, l, j:j + 1], scalar2=None, op0=ALU.add),
                r=[(pt, None), (MB, None)], w=[(O, (l, j))])
    P.dma("sp", modl, O[:].rearrange("p l j c -> p (l j c)"), r=[(O, None)], key="o", final=True)
    P.emit()


def stage_tl(ctx, l_pre, l_post, x_in, x_out, modg, pre=None, post=None, chunks=None):
    P = Prog(ctx)
    if chunks is None:
        chunks = [(0, 512, 0), (512, 512, 0), (1024, 512, 0), (1536, 512, 0), (TPC, CPC, 1)]
    has_pre, has_post = pre is not None, post is not None
    HW = NTL
    JG = 6
    onesD = P.sb("onesD", [128, 128], BF16)
    epsc = P.sb("epsc", [128, 1], F32)
    P.add("dve", lambda e: e.memset(onesD[:], 1.0 / D), w=[(onesD, None)])
    P.add("dve", lambda e: e.memset(epsc[:], EPS), w=[(epsc, None)])
    X = P.sb("X", [128, 8, HW], F32)
    H = P.sb("H", [128, 8, HW], BF16)
    A = P.sb("A", [128, JG, HW], BF16)
    WST = P.sb("WST", [128, 2, 8, 256], F32)
    WIN = P.sb("WIN", [128, 2, 8, 256], BF16)
    WOST = P.sb("WOST", [128, 2, 1024], F32)
    WOUT = P.sb("WOUT", [128, 8, 1024], BF16)
    SQ = P.sb("SQ", [128, 2, 512], BF16)
    RS = P.sb("RS", [128, 2, 512], F32)
    TMP = P.sb("TMP", [128, 2, 512], F32)
    SG = P.sb("SG", [128, 2, 512], F32)
    HOF = P.sb("HOF", [128, 2, 512], F32)
    MOD = P.sb("MOD", [128, 8, 72], F32)
    OPS = P.sb("OPS", [128, 8, 72], F32)
    HG = P.sb("HG", [128, 8, 72], F32)
    GEFF = P.sb("GEFF", [128, 2, 8], F32)
    PSV = P.sb("PSV", [128, 8], F32)
    PG = [P.ps("pg%d" % i, [128, 512]) for i in range(2)]
    PU = [P.ps("pu%d" % i, [128, 512]) for i in range(2)]
    PY = [P.ps("py%d" % i, [128, 512]) for i in range(2)]
    PM = [P.ps("pm%d" % i, [128, 512]) for i in range(2)]

    P.dma("sp", MOD[:], modg.rearrange("(r p) c -> p r c", p=128), w=[(MOD, None)], key="mod")
    P.add("dve", lambda e: e.tensor_scalar(out=OPS[:], in0=MOD[:], scalar1=1.0, scalar2=None, op0=ALU.add),
          r=[(MOD, None)], w=[(OPS, None)])
    P.add("dve", lambda e: e.tensor_scalar(out=HG[:], in0=MOD[:], scalar1=0.5, scalar2=None, op0=ALU.mult),
          r=[(MOD, None)], w=[(HG, None)])
    if has_pre:
        P.dma("sp", PSV[:], pre["ps"], w=[(PSV, None)], key="psv")
        for ci in range(2):
            for c in range(8):
                P.add("dve", lambda e, ci=ci, c=c: e.tensor_tensor(out=GEFF[:, ci, c:c + 1], in0=mvec(MOD, l_pre, 5, c, ci),
                                                                 in1=PSV[:, c:c + 1], op=ALU.mult),
                      r=[(MOD, None), (PSV, None)], w=[(GEFF, (ci, c))])
    cnt = {"n": 0, "pm": 0, "pg": 0, "py": 0, "sq": 0}
    for k in range(8):
        P.dma("sp", X[:, k, :], x_in[k * 128:(k + 1) * 128, :], w=[(X, (k, "all"))], key=("x", k))

    def xr(k, off):
        return (X, (k, off))

    def xreads(k, off):
        return [(X, (k, off)), (X, (k, "all"))]

    def modulate(l, v_shift, v_scale, h_out):
        for (off, wd, ci) in chunks:
            sl = slice(off, off + wd)
            pm = PM[cnt["pm"] % 2]
            rs = cnt["pm"] % 2
            cnt["pm"] += 1
            for k in range(8):
                sq = cnt["sq"] % 2
                cnt["sq"] += 1
                P.add("act", lambda e, k=k, sl=sl, wd=wd, sq=sq: e.activation(out=SQ[:, sq, 0:wd], in_=X[:, k, sl], func=AF.Square),
                      r=xreads(k, off), w=[(SQ, sq)])
                P.add("pe", lambda e, k=k, wd=wd, pm=pm, sq=sq: e.matmul(pm[:, 0:wd], lhsT=onesD[:], rhs=SQ[:, sq, 0:wd],
                                                                        start=(k == 0), stop=(k == 7)),
                      r=[(SQ, sq), (onesD, None)], w=[(pm, None)])
            P.add("act", lambda e, wd=wd, pm=pm, rs=rs: e.activation(out=RS[:, rs, 0:wd], in_=pm[:, 0:wd], func=AF.Sqrt,
                                                                     bias=epsc[:], scale=1.0),
                  r=[(pm, None), (epsc, None)], w=[(RS, rs)])
            P.add("dve", lambda e, wd=wd, rs=rs: e.reciprocal(out=RS[:, rs, 0:wd], in_=RS[:, rs, 0:wd]),
                  r=[(RS, rs)], w=[(RS, rs)])
            for k in range(8):
                t = cnt["n"] % 2
                cnt["n"] += 1
                P.add("dve", lambda e, k=k, sl=sl, wd=wd, t=t, ci=ci, rs=rs: e.scalar_tensor_tensor(
                    out=TMP[:, t, 0:wd], in0=X[:, k, sl], scalar=mvec(OPS, l, v_scale, k, ci), in1=RS[:, rs, 0:wd],
                    op0=ALU.mult, op1=ALU.mult),
                    r=xreads(k, off) + [(OPS, None), (RS, rs)], w=[(TMP, t)])
                P.add("act", lambda e, k=k, sl=sl, wd=wd, t=t, ci=ci: e.activation(
                    out=H[:, k, sl], in_=TMP[:, t, 0:wd], func=AF.Identity, bias=mvec(MOD, l, v_shift, k, ci), scale=1.0),
                    r=[(TMP, t), (MOD, None)], w=[(H, (k, off))])
                if h_out is not None:
                    P.add("dve", lambda e, k=k, wd=wd, t=t, ci=ci: e.tensor_scalar(
                        out=HOF[:, t, 0:wd], in0=TMP[:, t, 0:wd], scalar1=mvec(MOD, l, v_shift, k, ci), scalar2=None, op0=ALU.add),
                        r=[(TMP, t), (MOD, None)], w=[(HOF, t)])
                    P.dma("sp", h_out[k * 128:(k + 1) * 128, off:off + wd], HOF[:, t, 0:wd], r=[(HOF, t)],
                          key=("hof", t), final=True)

    def ffn(l, w_in, w_out, v_shift, v_scale, v_gate):
        modulate(l, v_shift, v_scale, None)

        def load_in(j, do_in=True, do_out=True):
            s = j % 2
            jl = j % JG
            if do_in:
                P.dma("sp", WST[:, s, :, 0:128], w_in[:, j * 128:(j + 1) * 128].rearrange("(k p) c -> p k c", p=128),
                      w=[(WST, s)], key=("wst", s))
                P.dma("sp", WST[:, s, :, 128:256],
                      w_in[:, DFF + j * 128:DFF + (j + 1) * 128].rearrange("(k p) c -> p k c", p=128),
                      w=[(WST, (s, 1))], key=("wst", s))
                P.add("pool", lambda e, s=s: e.tensor_copy(out=WIN[:, s], in_=WST[:, s]), r=[(WST, None)], w=[(WIN, s)])
            if do_out:
                P.dma("sp", WOST[:, s, :], w_out[j * 128:(j + 1) * 128, :], w=[(WOST, s)], key=("wost", s))
                P.add("pool", lambda e, s=s, jl=jl: e.tensor_copy(out=WOUT[:, jl, :], in_=WOST[:, s, :]),
                      r=[(WOST, s)], w=[(WOUT, jl)])

        load_in(0)
        for g0 in range(0, NJ, JG):
            gn = min(JG, NJ - g0)
            if g0 > 0:
                load_in(g0, do_in=False, do_out=True)
            for j in range(g0, g0 + gn):
                if j + 1 < NJ:
                    load_in(j + 1, do_in=True, do_out=(j + 1 < g0 + gn))
                s = j % 2
                jl = j % JG
                for (off, wd, ci) in chunks:
                    sl = slice(off, off + wd)
                    b = cnt["pg"] % 2
                    cnt["pg"] += 1
                    for k in range(8):
                        P.add("pe", lambda e, k=k, s=s, sl=sl, wd=wd, b=b: e.matmul(
                            PG[b][:, 0:wd], lhsT=WIN[:, s, k, 0:128], rhs=H[:, k, sl], start=(k == 0), stop=(k == 7)),
                            r=[(WIN, s), (H, (k, off))], w=[(PG[b], None)])
                    for k in range(8):
                        P.add("pe", lambda e, k=k, s=s, sl=sl, wd=wd, b=b: e.matmul(
                            PU[b][:, 0:wd], lhsT=WIN[:, s, k, 128:256], rhs=H[:, k, sl], start=(k == 0), stop=(k == 7)),
                            r=[(WIN, s), (H, (k, off))], w=[(PU[b], None)])
                    P.add("act", lambda e, wd=wd, b=b: e.activation(out=SG[:, b, 0:wd], in_=PG[b][:, 0:wd], func=AF.Silu),
                          r=[(PG[b], None)], w=[(SG, b)])
                    P.add("dve", lambda e, wd=wd, b=b, jl=jl, sl=sl: e.tensor_tensor(
                        out=A[:, jl, sl], in0=SG[:, b, 0:wd], in1=PU[b][:, 0:wd], op=ALU.mult),
                        r=[(SG, b), (PU[b], None)], w=[(A, (jl, off))])
            for mc in range(8):
                for (off, wd, ci) in chunks:
                    sl = slice(off, off + wd)
                    b = cnt["py"] % 2
                    cnt["py"] += 1
                    for jl in range(gn):
                        P.add("pe", lambda e, jl=jl, mc=mc, sl=sl, wd=wd, b=b, gn=gn: e.matmul(
                            PY[b][:, 0:wd], lhsT=WOUT[:, jl, mc * 128:(mc + 1) * 128], rhs=A[:, jl, sl],
                            start=(jl == 0), stop=(jl == gn - 1)),
                            r=[(WOUT, jl), (A, (jl, off))], w=[(PY[b], None)])
                    P.add("dve", lambda e, mc=mc, sl=sl, wd=wd, b=b, ci=ci: e.scalar_tensor_tensor(
                        out=X[:, mc, sl], in0=PY[b][:, 0:wd], scalar=mvec(HG, l, v_gate, mc, ci), in1=X[:, mc, sl],
                        op0=ALU.mult, op1=ALU.add),
                        r=[(PY[b], None), (HG, None)] + xreads(mc, off), w=[(X, (mc, off))])

    if has_pre:
        mT, wo = pre["mT"], pre["wo"]
        for (off, wd, ci) in chunks:
            for q in range(0, wd, 256):
                qw = min(256, wd - q)
                s = (q // 256) % 2
                P.dma("sp", WST[:, s, :, 0:qw], mT[:, off + q:off + q + qw].rearrange("(k p) n -> p k n", p=128),
                      w=[(WST, None)], key=("wst", s))
                P.add("pool", lambda e, s=s, qw=qw, a=off + q: e.tensor_copy(out=H[:, :, a:a + qw], in_=WST[:, s, :, 0:qw]),
                      r=[(WST, None)], w=[(H, None)])
        for k in range(8):
            s = k % 2
            P.dma("sp", WOST[:, s, :], wo[k * 128:(k + 1) * 128, :], w=[(WOST, s)], key=("wost", s))
            P.add("pool", lambda e, s=s, k=k: e.tensor_copy(out=WOUT[:, k, :], in_=WOST[:, s, :]),
                  r=[(WOST, s)], w=[(WOUT, k)])
        for mc in range(8):
            for (off, wd, ci) in chunks:
                sl = slice(off, off + wd)
                b = cnt["py"] % 2
                cnt["py"] += 1
                for k in range(8):
                    P.add("pe", lambda e, k=k, mc=mc, sl=sl, wd=wd, b=b: e.matmul(
                        PY[b][:, 0:wd], lhsT=WOUT[:, k, mc * 128:(mc + 1) * 128], rhs=H[:, k, sl],
                        start=(k == 0), stop=(k == 7)),
                        r=[(WOUT, k), (H, None)], w=[(PY[b], None)])
                P.add("dve", lambda e, mc=mc, sl=sl, wd=wd, b=b, ci=ci: e.scalar_tensor_tensor(
                    out=X[:, mc, sl], in0=PY[b][:, 0:wd], scalar=GEFF[:, ci, mc:mc + 1], in1=X[:, mc, sl],
                    op0=ALU.mult, op1=ALU.add),
                    r=[(PY[b], None), (GEFF, None)] + xreads(mc, off), w=[(X, (mc, off))])
        ffn(l_pre, pre["w_in"], pre["w_out"], 6, 7, 8)
    if has_post:
        ffn(l_post, post["w_in"], post["w_out"], 0, 1, 2)
        modulate(l_post, 3, 4, post["h_out"])
    for k in range(8):
        P.dma("sp", x_out[k * 128:(k + 1) * 128, :], X[:, k, :], r=[(X, None)], key=("xo", k % 2), final=True)
    P.emit()


def stage_proj(ctx, KC, hT, Wd, Gd, groups, M, out_fn, rope=None, chunks=PCH, obf=None):
    P = Prog(ctx)
    g2 = []
    for (c0, nck, part, norm, rp) in groups:
        if norm is None:
            g2 += [(c0 + i * part, 1, part, None, rp) for i in range(nck)]
        else:
            assert nck <= 3
            g2.append((c0, nck, part, norm, rp))
    groups = g2
    nch = sum(g[1] for g in groups)
    WB = P.sb("WB", [128, KC, M], BF16)
    WS = P.sb("WS", [128, 2, KC, 512], F32)
    HS = P.sb("HS", [128, KC, 512], F32)
    HB = P.sb("HB", [128, 2, KC, 512], BF16)
    G = P.sb("G", [128, nch], F32)
    Y = P.sb("Y", [128, 4, 512], F32)
    SQ = P.sb("SQ", [128, 2, 512], BF16)
    RS = P.sb("RS", [128, 2, 512], F32)
    OB = P.sb("OB", [128, 4, 512], F32)
    OBH = P.sb("OBH", [128, 4, 512], BF16)
    if obf is None:
        obf = lambda col: False
    eps = P.sb("eps", [128, 1], F32)
    P.add("dve", lambda e: e.memset(eps[:], EPS), w=[(eps, None)])
    ones = {}
    for (c0, nck, part, norm, rp) in groups:
        if norm == "full" and (nck * part) not in ones:
            t = P.sb("ones%d" % (nck * part), [128, 128], BF16)
            P.add("dve", lambda e, t=t, v=1.0 / (nck * part): e.memset(t[:], v), w=[(t, None)])
            ones[nck * part] = t
        if norm == "blk64" and "b" not in ones:
            t = P.sb("onesb", [128, 128], BF16)
            P.add("dve", lambda e, t=t: e.memset(t[:], 0.0), w=[(t, None)])
            P.add("dve", lambda e, t=t: e.memset(t[0:64, 0:64], 1.0 / 64), r=[(t, None)], w=[(t, None)])
            P.add("dve", lambda e, t=t: e.memset(t[64:128, 64:128], 1.0 / 64), r=[(t, None)], w=[(t, None)])
            ones["b"] = t
    if rope is not None:
        cosd, sind, pswd = rope
        COS = P.sb("COS", [64, 512], F32)
        SIN = P.sb("SIN", [64, 512], F32)
        PSWf = P.sb("PSWf", [64, 64], F32)
        PSW = P.sb("PSW", [64, 64], BF16)
        KRB = P.sb("KRB", [64, 512], BF16)
        T1 = P.sb("T1", [64, 512], F32)
        P.dma("sp", PSWf[:], pswd, w=[(PSWf, None)], key="psw")
        P.add("dve", lambda e: e.tensor_copy(out=PSW[:], in_=PSWf[:]), r=[(PSWf, None)], w=[(PSW, None)])
        PR = P.ps("pr", [64, 512])
    PSA = [P.ps("psa%d" % i, [128, 512]) for i in range(4)]
    PM = [P.ps("pm%d" % i, [128, 512]) for i in range(2)]
    P.dma("sp", G[:], Gd, w=[(G, None)], key="g")
    for si, c0 in enumerate(range(0, M, 512)):
        cw = min(512, M - c0)
        s = si % 2
        P.dma("sp", WS[:, s, :, 0:cw], Wd[:, c0:c0 + cw].rearrange("(k p) c -> p k c", p=128), w=[(WS, s)], key=("ws", s))
        P.add("pool", lambda e, s=s, cw=cw, c0=c0: e.tensor_copy(out=WB[:, :, c0:c0 + cw], in_=WS[:, s, :, 0:cw]),
              r=[(WS, s)], w=[(WB, c0)])
    cnt = {"a": 0, "m": 0, "o": 0}
    for ti, (off, wd) in enumerate(chunks):
        hs = ti % 2
        P.dma("sp", HS[:, :, 0:wd], hT[:, off:off + wd].rearrange("(k p) n -> p k n", p=128), w=[(HS, None)], key="hs")
        P.add("pool", lambda e, hs=hs, wd=wd: e.tensor_copy(out=HB[:, hs, :, 0:wd], in_=HS[:, :, 0:wd]),
              r=[(HS, None)], w=[(HB, hs)])
        if rope is not None:
            P.dma("sp", COS[:, 0:wd], cosd[:, off:off + wd], w=[(COS, None)], key="cos")
            P.dma("sp", SIN[:, 0:wd], sind[:, off:off + wd], w=[(SIN, None)], key="sin")
        gi = 0
        for (c0, nck, part, norm, rp) in groups:
            pm = PM[cnt["m"] % 2]
            rs = cnt["m"] % 2
            if norm is not None:
                cnt["m"] += 1
            ys = []
            for c in range(nck):
                col = c0 + c * part
                pa = PSA[cnt["a"] % 4]
                yi = cnt["a"] % 4
                cnt["a"] += 1
                ys.append(yi)
                for k in range(KC):
                    P.add("pe", lambda e, pa=pa, k=k, col=col, part=part, hs=hs, wd=wd: e.matmul(
                        pa[0:part, 0:wd], lhsT=WB[:, k, col:col + part], rhs=HB[:, hs, k, 0:wd], start=(k == 0), stop=(k == KC - 1)),
                        r=[(WB, None), (HB, hs)], w=[(pa, None)])
                P.add("act", lambda e, pa=pa, yi=yi, part=part, wd=wd: e.activation(
                    out=Y[0:part, yi, 0:wd], in_=pa[0:part, 0:wd], func=AF.Identity),
                    r=[(pa, None)], w=[(Y, yi)])
                if norm is not None:
                    sq = cnt["a"] % 2
                    P.add("act", lambda e, pa=pa, sq=sq, part=part, wd=wd: e.activation(
                        out=SQ[0:part, sq, 0:wd], in_=pa[0:part, 0:wd], func=AF.Square),
                        r=[(pa, None)], w=[(SQ, sq)])
                    om = ones["b"] if norm == "blk64" else ones[nck * part]
                    P.add("pe", lambda e, pm=pm, om=om, sq=sq, part=part, wd=wd, c=c, nck=nck: e.matmul(
                        pm[0:part, 0:wd], lhsT=om[0:part, 0:part], rhs=SQ[0:part, sq, 0:wd], start=(c == 0), stop=(c == nck - 1)),
                        r=[(om, None), (SQ, sq)], w=[(pm, None)])
            if norm is not None:
                P.add("act", lambda e, pm=pm, rs=rs, part=part, wd=wd: e.activation(
                    out=RS[0:part, rs, 0:wd], in_=pm[0:part, 0:wd], func=AF.Sqrt, bias=eps[0:part, :], scale=1.0),
                    r=[(pm, None), (eps, None)], w=[(RS, rs)])
                P.add("dve", lambda e, rs=rs, part=part, wd=wd: e.reciprocal(out=RS[0:part, rs, 0:wd], in_=RS[0:part, rs, 0:wd]),
                      r=[(RS, rs)], w=[(RS, rs)])
            for c in range(nck):
                col = c0 + c * part
                yi = ys[c]
                hb = obf(col)
                if norm is not None:
                    ob = cnt["o"] % 4
                    cnt["o"] += 1
                    ot = OBH if (hb and not rp) else OB
                    P.add("dve", lambda e, yi=yi, ob=ob, rs=rs, part=part, wd=wd, gc=gi + c, ot=ot: e.scalar_tensor_tensor(
                        out=ot[0:part, ob, 0:wd], in0=Y[0:part, yi, 0:wd], scalar=G[0:part, gc:gc + 1], in1=RS[0:part, rs, 0:wd],
                        op0=ALU.mult, op1=ALU.mult),
                        r=[(Y, yi), (G, None), (RS, rs)], w=[(ot, ob)])
                    src = (ot, ob)
                else:
                    assert not hb
                    src = (Y, yi)
                if rp:
                    st, sidx = src
                    P.add("act", lambda e, st=st, sidx=sidx, wd=wd: e.activation(out=KRB[:, 0:wd], in_=st[0:64, sidx, 0:wd], func=AF.Identity),
                          r=[src], w=[(KRB, None)])
                    P.add("pe", lambda e, wd=wd: e.matmul(PR[:, 0:wd], lhsT=PSW[:], rhs=KRB[:, 0:wd], start=True, stop=True),
                          r=[(PSW, None), (KRB, None)], w=[(PR, None)])
                    P.add("dve", lambda e, st=st, sidx=sidx, wd=wd: e.tensor_tensor(out=T1[:, 0:wd], in0=st[0:64, sidx, 0:wd], in1=COS[:, 0:wd], op=ALU.mult),
                          r=[src, (COS, None)], w=[(T1, None)])
                    ob2 = cnt["o"] % 4
                    cnt["o"] += 1
                    P.add("dve", lambda e, ob2=ob2, wd=wd: e.tensor_tensor(out=OB[0:64, ob2, 0:wd], in0=PR[:, 0:wd], in1=SIN[:, 0:wd], op=ALU.mult),
                          r=[(PR, None), (SIN, None)], w=[(OB, ob2)])
                    ot2 = OBH if hb else OB
                    P.add("dve", lambda e, ob2=ob2, wd=wd, ot2=ot2: e.tensor_tensor(out=ot2[0:64, ob2, 0:wd], in0=OB[0:64, ob2, 0:wd], in1=T1[:, 0:wd], op=ALU.add),
                          r=[(OB, ob2), (T1, None)], w=[(ot2, ob2)])
                    src = (ot2, ob2)
                st, sidx = src
                P.dma("sp", out_fn(col, part, off, wd), st[0:part, sidx, 0:wd], r=[src], key=("o", st.name, sidx), final=True)
            gi += nck
    P.emit()


def stage_projT(ctx, KC, hT, Wd, M, out_fn, chunks=PCH, bf16=False):
    P = Prog(ctx)
    WB = P.sb("WB", [128, KC, M], BF16)
    WS = P.sb("WS", [128, 2, KC, 512], F32)
    HS = P.sb("HS", [128, KC, 512], F32)
    HB = P.sb("HB", [128, 2, KC, 512], BF16)
    OB = P.sb("OB", [128, 4, 512], BF16 if bf16 else F32)
    PSA = [P.ps("psa%d" % i, [128, 512]) for i in range(4)]
    for si, c0 in enumerate(range(0, M, 512)):
        cw = min(512, M - c0)
        s = si % 2
        P.dma("sp", WS[:, s, :, 0:cw], Wd[:, c0:c0 + cw].rearrange("(k p) c -> p k c", p=128), w=[(WS, s)], key=("ws", s))
        P.add("pool", lambda e, s=s, cw=cw, c0=c0: e.tensor_copy(out=WB[:, :, c0:c0 + cw], in_=WS[:, s, :, 0:cw]),
              r=[(WS, s)], w=[(WB, c0)])
    n = 0
    for ti, (off, wd) in enumerate(chunks):
        hs = ti % 2
        P.dma("sp", HS[:, :, 0:wd], hT[:, off:off + wd].rearrange("(k p) n -> p k n", p=128), w=[(HS, None)], key="hs")
        P.add("pool", lambda e, hs=hs, wd=wd: e.tensor_copy(out=HB[:, hs, :, 0:wd], in_=HS[:, :, 0:wd]),
              r=[(HS, None)], w=[(HB, hs)])
        for t0 in range(0, wd, 128):
            nt = min(128, wd - t0)
            for c0 in range(0, M, 512):
                cw = min(512, M - c0)
                b = n % 4
                n += 1
                for k in range(KC):
                    P.add("pe", lambda e, b=b, k=k, hs=hs, t0=t0, nt=nt, c0=c0, cw=cw: e.matmul(
                        PSA[b][0:nt, 0:cw], lhsT=HB[:, hs, k, t0:t0 + nt], rhs=WB[:, k, c0:c0 + cw], start=(k == 0), stop=(k == KC - 1)),
                        r=[(HB, hs), (WB, None)], w=[(PSA[b], None)])
                P.add("act", lambda e, b=b, nt=nt, cw=cw: e.activation(out=OB[0:nt, b, 0:cw], in_=PSA[b][0:nt, 0:cw], func=AF.Identity),
                      r=[(PSA[b], None)], w=[(OB, b)])
                dst, src_view = out_fn(off + t0, nt, c0, cw, OB[0:nt, b, 0:cw])
                P.dma("sp", dst, src_view, r=[(OB, b)], key=("o", b), final=True)
    P.emit()


def stage_copy(ctx, moves, zero=()):
    P = Prog(ctx)
    B = [P.sb("B%d" % i, [128, 2048], F32) for i in range(2)]
    Z = P.sb("Z", [128, 2048], F32)
    P.add("dve", lambda e: e.memset(Z[:], 0.0), w=[(Z, None)])
    for i, (dst, src, pp, fr) in enumerate(moves):
        b = B[i % 2]
        P.dma("sp", b[0:pp, 0:fr], src, w=[(b, None)], key=("ld", i % 2))
        P.dma("sp", dst, b[0:pp, 0:fr], r=[(b, None)], key=("st", i % 2), final=True)
    for (dst, pp, fr) in zero:
        P.dma("sp", dst, Z[0:pp, 0:fr], r=[(Z, None)], key="z", final=True)
    P.emit()


KKR = D + 64


def stage_att(ctx, Qd, KKg, VLg, VCg, oT, scale):
    P = Prog(ctx)
    NTT = NCORES * 16
    KN = [P.sb("KN%d" % i, [128, NKEY], BF16) for i in range(2)]
    KR = P.sb("KR", [64, NKEY], BF16)
    V = [P.sb("V%d" % i, [128, NTT, 128], BF16) for i in range(2)]
    KNc = [P.sb("KNc%d" % i, [128, CTX], BF16) for i in range(2)]
    KRc = P.sb("KRc", [64, CTX], BF16)
    Vc = [P.sb("Vc%d" % i, [128, 2, 128], BF16) for i in range(2)]
    QN = P.sb("QN", [128, 2, 512], BF16)
    QR = P.sb("QR", [64, 2, 512], BF16)
    PT = P.sb("PT", [128, 4, 512], BF16)
    RC = P.sb("RC", [128, 2, 512], F32)
    OS = P.sb("OS", [128, 2, 512], F32)
    ONESb = P.sb("ONESb", [128, 128], BF16)
    P.add("dve", lambda e: e.memset(ONESb[:], 1.0), w=[(ONESb, None)])
    PS = [P.ps("ps%d" % i, [128, 512]) for i in range(4)]
    PO = [P.ps("po%d" % i, [128, 512]) for i in range(2)]
    PSM = [P.ps("psm%d" % i, [128, 512]) for i in range(2)]
    cn = {"pt": 0, "ps": 0, "q": 0}

    for r in range(NCORES):
        P.dma("sp", KR[:, r * NTL:(r + 1) * NTL], KKg[r * KKR + D:r * KKR + D + 64, :], w=[(KR, r)], key=("kr", r))
    for r in range(NCORES):
        P.add("pool", lambda e, r=r: e.tensor_copy(out=KRc[:, r * CPC:(r + 1) * CPC], in_=KR[:, r * NTL + TPC:(r + 1) * NTL]),
              r=[(KR, r)], w=[(KRc, r)])

    def load_head(h):
        b = h % 2
        for r in range(NCORES):
            P.dma("sp", KN[b][:, r * NTL:(r + 1) * NTL], KKg[r * KKR + h * 128:r * KKR + (h + 1) * 128, :],
                  w=[(KN[b], r)], key=("kn", b, r))
            P.dma("sp", V[b][:, r * 16:(r + 1) * 16, :],
                  VLg[r * D + h * 128:r * D + (h + 1) * 128, :].rearrange("p (t d) -> p t d", d=128),
                  w=[(V[b], r)], key=("v", b, r))
            P.dma("sp", Vc[b][(r % 4) * 32:(r % 4) * 32 + 32, r // 4, :], VCg[r * CTX + h * CPC:r * CTX + (h + 1) * CPC, :],
                  w=[(Vc[b], r)], key=("vc", b, r))
        for r in range(NCORES):
            P.add("pool", lambda e, r=r, b=b: e.tensor_copy(out=KNc[b][:, r * CPC:(r + 1) * CPC],
                                                             in_=KN[b][:, r * NTL + TPC:(r + 1) * NTL]),
                  r=[(KN[b], r)], w=[(KNc[b], r)])

    load_head(0)
    for h in range(8):
        hb = h % 2
        if h + 1 < 8:
            load_head(h + 1)
        lat_tiles = [(KN[hb], r * NTL + t * 128, KR, r * NTL + t * 128, V[hb], r * 16 + t) for r in range(NCORES) for t in range(16)]
        ctx_tiles = [(KNc[hb], t * 128, KRc, t * 128, Vc[hb], t) for t in range(2)]
        for (off, wd) in PCH:
            is_ctx = off >= TPC
            tiles = ctx_tiles if is_ctx else lat_tiles + ctx_tiles
            nkt = len(tiles)
            s = cn["q"] % 2
            cn["q"] += 1
            P.dma("sp", QN[:, s, 0:wd], Qd[h * 192:h * 192 + 128, off:off + wd], w=[(QN, s)], key=("qn", s))
            P.dma("sp", QR[:, s, 0:wd], Qd[h * 192 + 128:(h + 1) * 192, off:off + wd], w=[(QR, s)], key=("qr", s))
            po = PO[s]
            psm = PSM[s]

            def qk(ti, b, wd=wd, s=s, tiles=tiles):
                kn_t, kc0, kr_t, rc0, _, _ = tiles[ti]
                P.add("pe", lambda e, b=b, wd=wd, s=s, kn_t=kn_t, kc0=kc0: e.matmul(
                    PS[b][:, 0:wd], lhsT=kn_t[:, kc0:kc0 + 128], rhs=QN[:, s, 0:wd], start=True, stop=False),
                    r=[(kn_t, None), (QN, s)], w=[(PS[b], None)])
                P.add("pe", lambda e, b=b, wd=wd, s=s, kr_t=kr_t, rc0=rc0: e.matmul(
                    PS[b][:, 0:wd], lhsT=kr_t[:, rc0:rc0 + 128], rhs=QR[:, s, 0:wd], start=False, stop=True),
                    r=[(kr_t, None), (QR, s)], w=[(PS[b], None)])

            bs = []
            for t0_ in range(min(2, nkt)):
                b0 = cn["ps"] % 4
                cn["ps"] += 1
                qk(t0_, b0)
                bs.append(b0)
            for ti in range(nkt):
                if ti + 2 < nkt:
                    b1 = cn["ps"] % 4
                    cn["ps"] += 1
                    qk(ti + 2, b1)
                    bs.append(b1)
                b = bs[ti]
                p = cn["pt"] % 4
                cn["pt"] += 1
                v_t, vt = tiles[ti][4], tiles[ti][5]
                P.add("act", lambda e, b=b, p=p, wd=wd: e.activation(out=PT[:, p, 0:wd], in_=PS[b][:, 0:wd], func=AF.Exp, scale=scale),
                      r=[(PS[b], None)], w=[(PT, p)])
                P.add("pe", lambda e, ti=ti, p=p, wd=wd, po=po, nkt=nkt, v_t=v_t, vt=vt: e.matmul(
                    po[:, 0:wd], lhsT=v_t[:, vt, :], rhs=PT[:, p, 0:wd], start=(ti == 0), stop=(ti == nkt - 1)),
                    r=[(v_t, None), (PT, p)], w=[(po, None)])
                P.add("pe", lambda e, ti=ti, p=p, wd=wd, psm=psm, nkt=nkt: e.matmul(
                    psm[:, 0:wd], lhsT=ONESb[:], rhs=PT[:, p, 0:wd], start=(ti == 0), stop=(ti == nkt - 1)),
                    r=[(ONESb, None), (PT, p)], w=[(psm, None)])
            P.add("dve", lambda e, s=s, wd=wd, psm=psm: e.reciprocal(out=RC[:, s, 0:wd], in_=psm[:, 0:wd]), r=[(psm, None)], w=[(RC, s)])
            P.add("dve", lambda e, s=s, wd=wd, po=po: e.tensor_tensor(out=OS[:, s, 0:wd], in0=po[:, 0:wd], in1=RC[:, s, 0:wd], op=ALU.mult),
                  r=[(po, None), (RC, s)], w=[(OS, s)])
            P.dma("sp", oT[h * 128:(h + 1) * 128, off:off + wd], OS[:, s, 0:wd], r=[(OS, s)], key=("os", s), final=True)
    P.emit()


def _select(P, eng, out, srcs, oh, r_lists, w_list, first_is_write=True):
    for r, src in enumerate(srcs):
        if r == 0:
            P.add(eng, lambda e, src=src: e.tensor_scalar(out=out, in0=src, scalar1=oh[:, 0:1], scalar2=None, op0=ALU.mult),
                  r=r_lists, w=w_list)
        else:
            P.add(eng, lambda e, src=src, r=r: e.scalar_tensor_tensor(out=out, in0=src, scalar=oh[:, r:r + 1], in1=out,
                                                                      op0=ALU.mult, op1=ALU.add),
                  r=r_lists + w_list, w=w_list)


def stage_pool(ctx, hT, edgeg, rc, ohLd, ohRd, mT):
    P = Prog(ctx)
    L = TPC + 16 + CPC + 16
    segs = [(0, TPC, 0), (TPC + 16, CPC, TPC)]
    EG = P.sb("EG", [128, NCORES, 8, 32], F32)
    OHL = P.sb("OHL", [128, NCORES], F32)
    OHR = P.sb("OHR", [128, NCORES], F32)
    HAL = P.sb("HAL", [128, 4, 8, 8], F32)
    A = [P.sb("A%d" % i, [128, L], F32) for i in range(2)]
    B = [P.sb("Bw%d" % i, [128, L], F32) for i in range(2)]
    RCs = [P.sb("RC%d" % i, [128, NTL], F32) for i in range(2)]
    O = [P.sb("O%d" % i, [128, NTL], F32) for i in range(2)]
    P.dma("sp", EG[:], edgeg.rearrange("(r k p) c -> p r k c", r=NCORES, p=128), w=[(EG, None)], key="eg")
    P.dma("sp", OHL[:], ohLd, w=[(OHL, None)], key="ohl")
    P.dma("sp", OHR[:], ohRd, w=[(OHR, None)], key="ohr")
    for hi, (oh, c0) in enumerate(((OHL, 8), (OHR, 0), (OHL, 24), (OHR, 16))):
        _select(P, "dve", HAL[:, hi], [EG[:, r, :, c0:c0 + 8] for r in range(NCORES)], oh,
                [(EG, None), (oh, None)], [(HAL, hi)])
    for k in range(8):
        g = k // 2
        a = A[k % 2]
        r_ = RCs[k % 2]
        o_ = O[k % 2]
        P.dma("sp", a[:, 8:8 + TPC], hT[k * 128:(k + 1) * 128, 0:TPC], w=[(a, "m")], key=("a", k % 2))
        P.dma("sp", a[:, TPC + 24:TPC + 24 + CPC], hT[k * 128:(k + 1) * 128, TPC:NTL], w=[(a, "c")], key=("a", k % 2))
        P.dma("sp", r_[:], rc[k * 128:(k + 1) * 128, :], w=[(r_, None)], key=("r", k % 2))
        for hi, c0 in ((0, 0), (1, TPC + 8), (2, TPC + 16), (3, TPC + 24 + CPC)):
            P.add("pool", lambda e, a=a, hi=hi, c0=c0, k=k: e.tensor_copy(out=a[:, c0:c0 + 8], in_=HAL[:, hi, k, :]),
                  r=[(HAL, hi)], w=[(a, ("h", hi))])
        cur = a
        lo, hi_ = 0, L
        for lev in range(g + 1):
            sh = 1 if lev == 0 else (1 << (lev - 1))
            dst = B[lev % 2]
            if lev == 0:
                nlo, nhi = lo + 1, hi_
                P.add("dve", lambda e, cur=cur, dst=dst, nlo=nlo, nhi=nhi: e.tensor_tensor(
                    out=dst[:, nlo:nhi], in0=cur[:, nlo - 1:nhi - 1], in1=cur[:, nlo:nhi], op=ALU.add),
                    r=[(cur, None)], w=[(dst, None)])
            else:
                nlo, nhi = lo + sh, hi_ - sh
                P.add("dve", lambda e, cur=cur, dst=dst, nlo=nlo, nhi=nhi, sh=sh: e.tensor_tensor(
                    out=dst[:, nlo:nhi], in0=cur[:, nlo - sh:nhi - sh], in1=cur[:, nlo + sh:nhi + sh], op=ALU.add),
                    r=[(cur, None)], w=[(dst, None)])
            cur, lo, hi_ = dst, nlo, nhi
        for (b, n, oo) in segs:
            P.add("dve", lambda e, cur=cur, b=b, n=n, oo=oo, o_=o_, r_=r_: e.tensor_tensor(
                out=o_[:, oo:oo + n], in0=cur[:, b + 8:b + 8 + n], in1=r_[:, oo:oo + n], op=ALU.mult),
                r=[(cur, None), (r_, None)], w=[(o_, oo)])
            P.add("dve", lambda e, a=a, b=b, n=n, oo=oo, o_=o_: e.tensor_tensor(
                out=o_[:, oo:oo + n], in0=o_[:, oo:oo + n], in1=a[:, b + 8:b + 8 + n], op=ALU.subtract),
                r=[(o_, oo), (a, None)], w=[(o_, oo)])
        P.dma("sp", mT[k * 128:(k + 1) * 128, :], o_[:], r=[(o_, None)], key=("o", k % 2), final=True)
    P.emit()


def stage_conve(ctx, bcu, ecg, cw, ohLd, ohRd, mT):
    P = Prog(ctx)
    n = TPC
    ECS = P.sb("ECS", [128, NCORES, 16, 2], F32)
    OHL = P.sb("OHL", [128, NCORES], F32)
    OHR = P.sb("OHR", [128, NCORES], F32)
    HL = P.sb("HL", [128, 16, 1], F32)
    HR = P.sb("HR", [128, 16, 1], F32)
    CW = P.sb("CW", [128, 8, 3], F32)
    Bt = [P.sb("B%d" % i, [128, n], F32) for i in range(2)]
    Ct = [P.sb("C%d" % i, [128, n + 2], F32) for i in range(2)]
    Ut = [P.sb("U%d" % i, [128, n + 2], F32) for i in range(2)]
    Z = [P.sb("Z%d" % i, [128, n], F32) for i in range(2)]
    P.dma("sp", CW[:], cw, w=[(CW, None)], key="cw")
    P.dma("sp", ECS[:], ecg.rearrange("(r k p) c -> p r k c", r=NCORES, p=128), w=[(ECS, None)], key="ecs")
    P.dma("sp", OHL[:], ohLd, w=[(OHL, None)], key="ohl")
    P.dma("sp", OHR[:], ohRd, w=[(OHR, None)], key="ohr")
    _select(P, "dve", HL[:], [ECS[:, r, :, 1:2] for r in range(NCORES)], OHL, [(ECS, None), (OHL, None)], [(HL, None)])
    _select(P, "dve", HR[:], [ECS[:, r, :, 0:1] for r in range(NCORES)], OHR, [(ECS, None), (OHR, None)], [(HR, None)])
    for k in range(8):
        s = k % 2
        b_, c_, u_, z_ = Bt[s], Ct[s], Ut[s], Z[s]
        P.dma("sp", b_[:], bcu[k * 128:(k + 1) * 128, 0:n], w=[(b_, None)], key=("b", s))
        P.dma("sp", c_[:, 1:n + 1], bcu[D + k * 128:D + (k + 1) * 128, 0:n], w=[(c_, "m")], key=("c", s))
        P.dma("sp", u_[:, 1:n + 1], bcu[2 * D + k * 128:2 * D + (k + 1) * 128, 0:n], w=[(u_, "m")], key=("u", s))
        for (t_, kk) in ((c_, k), (u_, 8 + k)):
            P.add("pool", lambda e, t_=t_, kk=kk: e.tensor_copy(out=t_[:, 0:1], in_=HL[:, kk, :]), r=[(HL, None)], w=[(t_, "l")])
            P.add("pool", lambda e, t_=t_, kk=kk: e.tensor_copy(out=t_[:, n + 1:n + 2], in_=HR[:, kk, :]), r=[(HR, None)], w=[(t_, "r")])
        P.add("dve", lambda e, c_=c_, u_=u_: e.tensor_tensor(out=c_[:], in0=c_[:], in1=u_[:], op=ALU.mult),
              r=[(c_, None), (u_, None)], w=[(c_, None)])
        P.add("dve", lambda e, c_=c_, z_=z_, k=k: e.tensor_scalar(out=z_[:], in0=c_[:, 0:n], scalar1=CW[:, k, 0:1], scalar2=None, op0=ALU.mult),
              r=[(c_, None), (CW, None)], w=[(z_, None)])
        for j in (1, 2):
            P.add("dve", lambda e, c_=c_, z_=z_, k=k, j=j: e.scalar_tensor_tensor(
                out=z_[:], in0=c_[:, j:j + n], scalar=CW[:, k, j:j + 1], in1=z_[:], op0=ALU.mult, op1=ALU.add),
                r=[(c_, None), (CW, None), (z_, None)], w=[(z_, None)])
        P.add("dve", lambda e, b_=b_, z_=z_: e.tensor_tensor(out=z_[:], in0=z_[:], in1=b_[:], op=ALU.mult),
              r=[(z_, None), (b_, None)], w=[(z_, None)])
        P.dma("sp", mT[k * 128:(k + 1) * 128, 0:n], z_[:], r=[(z_, None)], key=("z", s), final=True)
    P.emit()


NA_H = 16
NA_EXT = 2560
NA_NK = NA_EXT + CTX
NA_NT = NA_NK // 128
NA_HW = 512 + CPC


def stage_natt(ctx, qk, Vna, KHg, VHg, bias, ident, ohLd, ohRd, mT):
    P = Prog(ctx)
    KT = P.sb("KT", [128, 8, NA_NK], BF16)
    QT = P.sb("QT", [128, 8, TPC], BF16)
    VB = P.sb("VB", [128, NA_NT, D], BF16)
    ST = P.sb("ST", [128, 2, 1024], F32)
    HS = P.sb("HS", [128, NCORES, NA_HW], F32)
    ACC = P.sb("ACC", [128, 2, 512], F32)
    VCS = P.sb("VCS", [128, 512], F32)
    BS = P.sb("BS", [128, 1536], F32)
    BB = P.sb("BB", [128, 2, 3 * 6 * 256], BF16)
    IDf = P.sb("IDf", [128, 128], F32)
    ID = P.sb("ID", [128, 128], BF16)
    ONES = P.sb("ONES", [128, 64], BF16)
    OHL = P.sb("OHL", [128, NCORES], F32)
    OHR = P.sb("OHR", [128, NCORES], F32)
    PT = P.sb("PT", [128, 3, 256], BF16)
    RC = P.sb("RC", [64, 2, 256], F32)
    OS = P.sb("OS", [64, 2, 256], F32)
    PS = [P.ps("ps%d" % i, [128, 512]) for i in range(3)]
    PO = [P.ps("po%d" % i, [128, 512]) for i in range(2)]
    PSM = [P.ps("psm%d" % i, [128, 512]) for i in range(2)]
    P.add("dve", lambda e: e.memset(ONES[:], 1.0), w=[(ONES, None)])
    P.dma("sp", IDf[:], ident, w=[(IDf, None)], key="id")
    P.add("dve", lambda e: e.tensor_copy(out=ID[:], in_=IDf[:]), r=[(IDf, None)], w=[(ID, None)])
    P.dma("sp", OHL[:], ohLd, w=[(OHL, None)], key="ohl")
    P.dma("sp", OHR[:], ohRd, w=[(OHR, None)], key="ohr")
    n = 0
    KHv = KHg.rearrange("(r k p) c -> p r k c", r=NCORES, p=128)
    for k in range(8):
        for c0 in range(0, TPC, 1024):
            s = n % 2
            n += 1
            P.dma("sp", ST[:, s, :], qk[D + k * 128:D + (k + 1) * 128, c0:c0 + 1024], w=[(ST, s)], key=("st", s))
            P.add("pool", lambda e, s=s, c0=c0, k=k: e.tensor_copy(out=KT[:, k, 256 + c0:256 + c0 + 1024], in_=ST[:, s, :]),
                  r=[(ST, s)], w=[(KT, (k, "o", c0))])
            s = n % 2
            n += 1
            P.dma("sp", ST[:, s, :], qk[k * 128:(k + 1) * 128, c0:c0 + 1024], w=[(ST, s)], key=("st", s))
            P.add("pool", lambda e, s=s, c0=c0, k=k: e.tensor_scalar(out=QT[:, k, c0:c0 + 1024], in0=ST[:, s, :], scalar1=0.125,
                                                                    scalar2=None, op0=ALU.mult),
                  r=[(ST, s)], w=[(QT, (k, c0))])
        P.dma("sp", HS[:], KHv[:, :, k, :], w=[(HS, None)], key="hs")
        _select(P, "dve", ACC[:, 0, 0:256], [HS[:, r, 256:512] for r in range(NCORES)], OHL, [(HS, None), (OHL, None)], [(ACC, 0)])
        _select(P, "dve", ACC[:, 1, 0:256], [HS[:, r, 0:256] for r in range(NCORES)], OHR, [(HS, None), (OHR, None)], [(ACC, 1)])
        P.add("act", lambda e, k=k: e.activation(out=KT[:, k, 0:256], in_=ACC[:, 0, 0:256], func=AF.Identity),
              r=[(ACC, 0)], w=[(KT, (k, "a"))])
        P.add("act", lambda e, k=k: e.activation(out=KT[:, k, 2304:2560], in_=ACC[:, 1, 0:256], func=AF.Identity),
              r=[(ACC, 1)], w=[(KT, (k, "b"))])
        P.add("act", lambda e, k=k: e.activation(out=KT[:, k, NA_EXT:NA_NK].rearrange("p (r c) -> p r c", c=CPC),
                                                 in_=HS[:, :, 512:NA_HW], func=AF.Identity),
              r=[(HS, None)], w=[(KT, (k, "c"))])
    for t in range(16):
        for hh in range(0, D, 512):
            s = n % 2
            n += 1
            P.dma("sp", ST[:, s, 0:512], Vna[t * 128:(t + 1) * 128, hh:hh + 512], w=[(ST, s)], key=("st", s))
            P.add("pool", lambda e, s=s, t=t, hh=hh: e.tensor_copy(out=VB[:, 2 + t, hh:hh + 512], in_=ST[:, s, 0:512]),
                  r=[(ST, s)], w=[(VB, (2 + t, hh))])
    for (tile0, row0, oh) in ((0, 256, OHL), (18, 0, OHR)):
        for tt in range(2):
            for hh in range(0, D, 512):
                for r in range(NCORES):
                    P.dma("sp", HS[:, r, 0:512], VHg[r * NA_HW + row0 + tt * 128:r * NA_HW + row0 + (tt + 1) * 128, hh:hh + 512],
                          w=[(HS, r)], key=("hsv", r))
                a = n % 2
                n += 1
                _select(P, "dve", ACC[:, a, :], [HS[:, r, 0:512] for r in range(NCORES)], oh, [(HS, None), (oh, None)], [(ACC, a)])
                P.add("act", lambda e, a=a, tile0=tile0, tt=tt, hh=hh: e.activation(
                    out=VB[:, tile0 + tt, hh:hh + 512], in_=ACC[:, a, :], func=AF.Identity),
                    r=[(ACC, a)], w=[(VB, (tile0 + tt, hh))])
    for hh in range(0, D, 512):
        for t2 in range(2):
            for r4 in range(4):
                r = t2 * 4 + r4
                P.dma("sp", VCS[r4 * 32:(r4 + 1) * 32, :], VHg[r * NA_HW + 512:r * NA_HW + 512 + CPC, hh:hh + 512],
                      w=[(VCS, r4)], key="vcs")
            P.add("pool", lambda e, t2=t2, hh=hh: e.tensor_copy(out=VB[:, 20 + t2, hh:hh + 512], in_=VCS[:]),
                  r=[(VCS, None)], w=[(VB, (20 + t2, hh))])
    jobs = []
    for h in range(NA_H):
        for b in range(8):
            for ti, (kind, t) in enumerate([("l", t) for t in range(6)] + [("c", 0), ("c", 1)]):
                jobs.append((h, b, ti, kind, t))
    PSn = PS + [P.ps("ps3", [128, 512])]
    psb = {}

    def qk(j):
        h, b, ti, kind, t = jobs[j]
        kc, p0, bb = h // 2, (h % 2) * 64, h % 2
        if b == 0 and ti == 0:
            for var in range(3):
                P.dma("sp", BS[:], bias[h, :, var * 1536:(var + 1) * 1536], w=[(BS, None)], key="bs")
                P.add("pool", lambda e, bb=bb, var=var: e.tensor_copy(out=BB[:, bb, var * 1536:(var + 1) * 1536], in_=BS[:]),
                      r=[(BS, None)], w=[(BB, (bb, var))])
        var = 0 if b == 0 else (2 if b == 7 else 1)
        q0 = b * 256
        ps = PSn[j % 4]
        psb[j] = ps
        k0 = (4 * b + 2 * t) * 64 if kind == "l" else NA_EXT + t * 128
        P.add("pe", lambda e, ps=ps, kc=kc, p0=p0, k0=k0, q0=q0, kind=kind: e.matmul(
            ps[:, 0:256], lhsT=KT[p0:p0 + 64, kc, k0:k0 + 128], rhs=QT[p0:p0 + 64, kc, q0:q0 + 256],
            start=True, stop=(kind == "c")),
            r=[(KT, None), (QT, None)], w=[(ps, None)])
        if kind == "l":
            bo = (var * 6 + t) * 256
            P.add("pe", lambda e, ps=ps, bb=bb, bo=bo: e.matmul(
                ps[:, 0:256], lhsT=ID[:], rhs=BB[:, bb, bo:bo + 256], start=False, stop=True),
                r=[(ID, None), (BB, (bb, var))], w=[(ps, None)])

    for j in range(min(2, len(jobs))):
        qk(j)
    for j, (h, b, ti, kind, t) in enumerate(jobs):
        if j + 2 < len(jobs):
            qk(j + 2)
        g = j // 8
        po = PO[g % 2]
        psm = PSM[g % 2]
        osl = g % 2
        q0 = b * 256
        k0 = (4 * b + 2 * t) * 64 if kind == "l" else NA_EXT + t * 128
        vt = k0 // 128
        ps = psb[j]
        pt = j % 3
        P.add("act", lambda e, ps=ps, pt=pt: e.activation(out=PT[:, pt, :], in_=ps[:, 0:256], func=AF.Exp),
              r=[(ps, None)], w=[(PT, pt)])
        P.add("pe", lambda e, po=po, vt=vt, h=h, pt=pt, ti=ti: e.matmul(
            po[0:64, 0:256], lhsT=VB[:, vt, h * 64:(h + 1) * 64], rhs=PT[:, pt, :], start=(ti == 0), stop=(ti == 7)),
            r=[(VB, None), (PT, pt)], w=[(po, None)])
        P.add("pe", lambda e, psm=psm, pt=pt, ti=ti: e.matmul(
            psm[0:64, 0:256], lhsT=ONES[:], rhs=PT[:, pt, :], start=(ti == 0), stop=(ti == 7)),
            r=[(ONES, None), (PT, pt)], w=[(psm, None)])
        if ti == 7:
            P.add("dve", lambda e, psm=psm, osl=osl: e.reciprocal(out=RC[:, osl, :], in_=psm[0:64, 0:256]),
                  r=[(psm, None)], w=[(RC, osl)])
            P.add("dve", lambda e, po=po, osl=osl: e.tensor_tensor(out=OS[:, osl, :], in0=po[0:64, 0:256], in1=RC[:, osl, :], op=ALU.mult),
                  r=[(po, None), (RC, osl)], w=[(OS, osl)])
            P.dma("sp", mT[h * 64:(h + 1) * 64, q0:q0 + 256], OS[:, osl, :], r=[(OS, osl)], key=("os", osl), final=True)
    P.emit()


def stage_edges(ctx, src, nchunk, specs, dst, width):
    P = Prog(ctx)
    E = P.sb("E", [128, nchunk, width], F32)
    sv = src.rearrange("(k p) n -> p k n", p=128)
    for i, (c0, n, d0) in enumerate(specs):
        P.dma("sp", E[:, :, d0:d0 + n], sv[:, :, c0:c0 + n], w=[(E, i)], key="ld", allow_slow_non_contiguous=True)
    P.dma("sp", dst.rearrange("(k p) c -> p k c", p=128), E[:], r=[(E, None)], key="st", final=True,
          allow_slow_non_contiguous=True)
    P.emit()


def build_fused():
    ctx = Ctx()
    ext = lambda name, shape: ctx.dram(name, shape, F32, "ExternalInput")
    xT = ext("xT", [D, NTL])
    cT = ext("cT", [128, 8, 2]); mw = ext("mw", [4, D, 1152]); mb = ext("mb", [128, 4, 9])
    w_in = ext("ffn_w_in", [4, 2, D, 2 * DFF]); w_out = ext("ffn_w_out", [4, 2, DFF, D])
    Wlat = ext("Wlat", [D, 704]); Glat = ext("Glat", [128, 6])
    w_uq = ext("w_uq", [384, 1536]); Gq = ext("Gq", [128, 16])
    w_uk = ext("w_uk", [256, D]); Gk = ext("Gk", [128, 8]); w_uv = ext("w_uv", [256, D])
    cos = ext("cos", [64, NTL]); sin = ext("sin", [64, NTL]); psw = ext("psw", [64, 64])
    mla_w_o = ext("mla_w_o", [D, D]); ones8 = ext("ones8", [128, 8])
    wbd = ext("wbd", [D, D]); psv = ext("psv", [128, 8]); rc = ext("rc", [D, NTL])
    ohL = ext("ohL", [128, NCORES]); ohR = ext("ohR", [128, NCORES])
    w_qkv = ext("w_qkv", [D, 3 * D]); Gna = ext("Gna", [128, 16])
    bias = ext("bias", [NA_H, 128, 3 * 6 * 256]); ident = ext("ident", [128, 128]); na_w_o = ext("na_w_o", [D, D])
    conv_w_in = ext("conv_w_in", [D, 3 * D]); Gc = ext("Gc", [128, 24]); cw = ext("cw", [128, 8, 3]); conv_w_out = ext("conv_w_out", [D, D])
    xo = ctx.dram("xo", [D, NTL], F32, "ExternalOutput")

    modl = ctx.dram("modl", [128, 72]); modg = ctx.dram("modg", [NCORES * 128, 72])
    xa = ctx.dram("xa", [D, NTL]); xb = ctx.dram("xb", [D, NTL]); h = ctx.dram("h", [D, NTL])
    lat = ctx.dram("lat", [640, NTL]); Qd = ctx.dram("Qd", [1536, NTL], BF16)
    KK = ctx.dram("KK", [KKR, NTL], BF16)
    VL = ctx.dram("VL", [D, TPC], BF16)
    VC = ctx.dram("VC", [8 * CPC, 128], BF16)
    KKg = ctx.dram("KKg", [NCORES * KKR, NTL], BF16)
    VLg = ctx.dram("VLg", [NCORES * D, TPC], BF16); VCg = ctx.dram("VCg", [NCORES * 8 * CPC, 128], BF16)
    oT = ctx.dram("oT", [D, NTL])
    edgeP = ctx.dram("edgeP", [D, 32]); edgePg = ctx.dram("edgePg", [NCORES * D, 32])
    mTp = ctx.dram("mTp", [D, NTL])
    qk = ctx.dram("qk", [2 * D, NTL]); Vna = ctx.dram("Vna", [NTL, D])
    KH = ctx.dram("KH", [D, NA_HW]); VH = ctx.dram("VH", [NA_HW, D])
    KHg = ctx.dram("KHg", [NCORES * D, NA_HW]); VHg = ctx.dram("VHg", [NCORES * NA_HW, D])
    mTn = ctx.dram("mTn", [D, NTL])
    bcu = ctx.dram("bcu", [3 * D, NTL])
    EC = ctx.dram("EC", [2 * D, 2]); ECg = ctx.dram("ECg", [NCORES * 2 * D, 2])
    mTc = ctx.dram("mTc", [D, NTL])

    stage_mod(ctx, cT, mw, mb, modl)
    ctx.allgather(modl[:, :], modg[:, :])
    stage_tl(ctx, None, 0, xT, xa, modg, post=dict(w_in=w_in[0, 0], w_out=w_out[0, 0], h_out=h))

    def lat_out(col, part, off, wd):
        if col < 640:
            return lat[col:col + part, off:off + wd]
        return KK[D:D + 64, off:off + wd]
    rope = (cos, sin, psw)
    stage_proj(ctx, 8, h, Wlat, Glat, [(0, 3, 128, "full", False), (384, 2, 128, "full", False), (640, 1, 64, "full", True)],
               704, lat_out, rope, obf=lambda col: col >= 640)
    gq = []
    for hh in range(8):
        gq += [(hh * 192, 1, 128, "full", False), (hh * 192 + 128, 1, 64, "full", True)]
    stage_proj(ctx, 3, lat[0:384, :], w_uq, Gq, gq, 1536, lambda col, part, off, wd: Qd[col:col + part, off:off + wd], rope,
               obf=lambda col: True)
    stage_proj(ctx, 2, lat[384:640, :], w_uk, Gk, [(hh * 128, 1, 128, "full", False) for hh in range(8)], D,
               lambda col, part, off, wd: KK[col:col + part, off:off + wd], obf=lambda col: True)

    def v_out(tok0, nt, c0, cw, src):
        h0, nh = c0 // 128, cw // 128
        sv = src.rearrange("p (h d) -> p h d", d=128)
        if tok0 < TPC:
            t = tok0 // 128
            return VL[h0 * 128:(h0 + nh) * 128, t * 128:(t + 1) * 128].rearrange("(h p) d -> p h d", p=128), sv
        return VC[h0 * CPC:(h0 + nh) * CPC, :].rearrange("(h p) d -> p h d", p=CPC), sv
    stage_projT(ctx, 2, lat[384:640, :], w_uv, D, v_out, bf16=True)
    ctx.allgather(KK[:, :], KKg[:, :])
    ctx.allgather(VL[:, :], VLg[:, :])
    ctx.allgather(VC[:, :], VCg[:, :])
    stage_att(ctx, Qd, KKg, VLg, VCg, oT, float(192 ** -0.5))
    stage_tl(ctx, 0, 1, xa, xb, modg, pre=dict(mT=oT, wo=mla_w_o, ps=ones8, w_in=w_in[0, 1], w_out=w_out[0, 1]),
             post=dict(w_in=w_in[1, 0], w_out=w_out[1, 0], h_out=h))

    stage_edges(ctx, h, 8, [(0, 8, 0), (TPC - 8, 8, 8), (TPC, 8, 16), (NTL - 8, 8, 24)], edgeP, 32)
    ctx.allgather(edgeP[:, :], edgePg[:, :])
    stage_pool(ctx, h, edgePg, rc, ohL, ohR, mTp)
    stage_tl(ctx, 1, 2, xb, xa, modg, pre=dict(mT=mTp, wo=wbd, ps=psv, w_in=w_in[1, 1], w_out=w_out[1, 1]),
             post=dict(w_in=w_in[2, 0], w_out=w_out[2, 0], h_out=h))

    stage_proj(ctx, 8, h, w_qkv[:, 0:2 * D], Gna, [(cc * 128, 1, 128, "blk64", False) for cc in range(16)], 2 * D,
               lambda col, part, off, wd: qk[col:col + part, off:off + wd])
    stage_projT(ctx, 8, h, w_qkv[:, 2 * D:3 * D], D,
                lambda tok0, nt, c0, cw, src: (Vna[tok0:tok0 + nt, c0:c0 + cw], src))
    stage_edges(ctx, qk[D:2 * D, :], 8, [(0, 256, 0), (TPC - 256, 256, 256), (TPC, CPC, 512)], KH, NA_HW)
    stage_copy(ctx, [(VH[0:128, :], Vna[0:128, :], 128, D), (VH[128:256, :], Vna[128:256, :], 128, D),
                     (VH[256:384, :], Vna[TPC - 256:TPC - 128, :], 128, D), (VH[384:512, :], Vna[TPC - 128:TPC, :], 128, D),
                     (VH[512:NA_HW, :], Vna[TPC:NTL, :], CPC, D)],
               zero=[(mTn[k * 128:(k + 1) * 128, TPC:NTL], 128, CPC) for k in range(8)]
               + [(mTc[k * 128:(k + 1) * 128, TPC:NTL], 128, CPC) for k in range(8)])
    ctx.allgather(KH[:, :], KHg[:, :])
    ctx.allgather(VH[:, :], VHg[:, :])
    stage_natt(ctx, qk, Vna, KHg, VHg, bias, ident, ohL, ohR, mTn)
    stage_tl(ctx, 2, 3, xa, xb, modg, pre=dict(mT=mTn, wo=na_w_o, ps=ones8, w_in=w_in[2, 1], w_out=w_out[2, 1]),
             post=dict(w_in=w_in[3, 0], w_out=w_out[3, 0], h_out=h))

    stage_proj(ctx, 8, h, conv_w_in, Gc, [(0, 24, 128, None, False)], 3 * D,
               lambda col, part, off, wd: bcu[col:col + part, off:off + wd])
    stage_edges(ctx, bcu[D:3 * D, :], 16, [(0, 1, 0), (TPC - 1, 1, 1)], EC, 2)
    ctx.allgather(EC[:, :], ECg[:, :])
    stage_conve(ctx, bcu, ECg, cw, ohL, ohR, mTc)
    stage_tl(ctx, 3, None, xb, xo, modg, pre=dict(mT=mTc, wo=conv_w_out, ps=ones8, w_in=w_in[3, 1], w_out=w_out[3, 1]))
    return ctx.finish()


_cache = {}


def _f32(a):
    return np.ascontiguousarray(a, dtype=np.float32)


def _col(v):
    o = np.ones((128,), np.float32)
    o[:v.shape[0]] = v
    return o


def _rope_tables():
    freqs = (np.float32(10000.0) ** (-np.arange(16, dtype=np.float32) / np.float32(16))).astype(np.float32)
    t = np.arange(SEQ)
    row = (t // 64).astype(np.float32)
    col = (t % 64).astype(np.float32)
    ang = np.concatenate([row[:, None] * freqs, col[:, None] * freqs], axis=-1).astype(np.float32)
    cos = np.repeat(np.cos(ang), 2, axis=1).T.astype(np.float32)
    sin = np.repeat(np.sin(ang), 2, axis=1).T.astype(np.float32)
    coss, sins = [], []
    for r in range(NCORES):
        coss.append(np.concatenate([cos[:, r * TPC:(r + 1) * TPC], np.ones((64, CPC), np.float32)], axis=1))
        sins.append(np.concatenate([sin[:, r * TPC:(r + 1) * TPC], np.zeros((64, CPC), np.float32)], axis=1))
    psw = np.zeros((64, 64), np.float32)
    for i in range(32):
        psw[2 * i + 1, 2 * i] = -1.0
        psw[2 * i, 2 * i + 1] = 1.0
    return coss, sins, psw


def _na_bias(rpb, base):
    t = np.arange(6)[:, None, None]
    kk = np.arange(128)[None, :, None]
    qq = np.arange(256)[None, None, :]
    key_row = base - 4 + 2 * t + kk // 64
    kcol = kk % 64
    r = base + qq // 64
    qc = qq % 64
    rs = np.clip(r - 4, 0, 248)
    cs = np.clip(qc - 8, 0, 48)
    valid = (key_row >= rs) & (key_row < rs + 8) & (kcol >= cs) & (kcol < cs + 16) & (key_row >= 0) & (key_row < 256)
    dr = np.clip(key_row - r + 7, 0, 14)
    dc = np.clip(kcol - qc + 15, 0, 30)
    vals = rpb[:, dr, dc]
    return np.where(valid[None], vals, np.float32(-30000.0)).astype(np.float32)


def kernel(x, c, ctx, c_ctx, mod_w, mod_b, ffn_w_in, ffn_w_out,
           mla_w_dq, mla_g_dq, mla_w_uq, mla_w_dkv, mla_g_dkv, mla_w_uk, mla_w_uv,
           mla_g_qn, mla_g_qr, mla_g_kn, mla_g_kr, mla_w_o,
           pool_w, pool_scale,
           na_w_qkv, na_g_q, na_g_k, na_rpb, na_w_o,
           conv_w_in, conv_w, conv_w_out):
    from concourse.bass_utils import run_bass_kernel_spmd
    x = np.asarray(x, np.float32)[0]
    cx = np.asarray(ctx, np.float32)[0]
    R = range(NCORES)
    ones8 = np.ones((128, 8), np.float32)
    cc = np.stack([np.asarray(c, np.float32)[0], np.asarray(c_ctx, np.float32)], axis=-1)
    cT = _f32(cc.reshape(8, 128, 2).transpose(1, 0, 2))
    coss, sins, psw = _rope_tables()
    Wlat = _f32(np.concatenate([mla_w_dq[0], mla_w_dkv[0]], axis=1))
    Glat = _f32(np.stack([_col(mla_g_dq[0][0:128]), _col(mla_g_dq[0][128:256]), _col(mla_g_dq[0][256:384]),
                          _col(mla_g_dkv[0][0:128]), _col(mla_g_dkv[0][128:256]), _col(mla_g_kr[0])], axis=1))
    Gq = _f32(np.stack([_col(mla_g_qn[0]), _col(mla_g_qr[0])] * 8, axis=1))
    Gk = _f32(np.stack([_col(mla_g_kn[0])] * 8, axis=1))
    wbd = np.zeros((D, D), np.float32)
    for g in range(4):
        wbd[g * 256:(g + 1) * 256, g * 256:(g + 1) * 256] = pool_w[0, g]
    psv = _f32(np.asarray(pool_scale[0], np.float32).reshape(8, 128).T)
    half = np.repeat(np.array([1, 2, 4, 8]), 256)[:, None]

    def rcount(T, t):
        lo = np.clip(t[None, :] - half, 0, T)
        hi = np.clip(t[None, :] + half, 0, T)
        return (np.float32(1.0) / (hi - lo).astype(np.float32)).astype(np.float32)

    Gna = _f32(np.stack([np.tile(np.asarray(na_g_q[0], np.float32), 2)] * 8 + [np.tile(np.asarray(na_g_k[0], np.float32), 2)] * 8, axis=1))
    rpb = np.asarray(na_rpb[0], np.float32)
    b_int, b_top, b_bot = _na_bias(rpb, 8), _na_bias(rpb, 0), _na_bias(rpb, 252)
    ident = np.eye(128, dtype=np.float32)
    cw = _f32(np.asarray(conv_w[0], np.float32).T.reshape(8, 128, 3).transpose(1, 0, 2))
    shared = {
        "cT": cT, "ffn_w_in": _f32(ffn_w_in), "ffn_w_out": _f32(ffn_w_out), "Wlat": Wlat, "Glat": Glat,
        "w_uq": _f32(np.asarray(mla_w_uq[0]).reshape(384, 1536)), "Gq": Gq, "w_uk": _f32(np.asarray(mla_w_uk[0]).reshape(256, D)),
        "Gk": Gk, "w_uv": _f32(np.asarray(mla_w_uv[0]).reshape(256, D)), "psw": psw, "mla_w_o": _f32(mla_w_o[0]), "ones8": ones8,
        "wbd": wbd, "psv": psv, "w_qkv": _f32(na_w_qkv[0]), "Gna": Gna, "ident": ident, "na_w_o": _f32(na_w_o[0]),
        "conv_w_in": _f32(conv_w_in[0]), "Gc": np.ones((128, 24), np.float32), "cw": cw, "conv_w_out": _f32(conv_w_out[0]),
    }
    maps = []
    for r in R:
        d = dict(shared)
        d["xT"] = _f32(np.concatenate([x[r * TPC:(r + 1) * TPC].T, cx[r * CPC:(r + 1) * CPC].T], axis=1))
        d["mw"] = _f32(np.asarray(mod_w)[:, :, 1152 * r:1152 * (r + 1)])
        d["mb"] = _f32(np.asarray(mod_b)[:, 1152 * r:1152 * (r + 1)].reshape(4, 9, 128).transpose(2, 0, 1))
        d["cos"] = coss[r]
        d["sin"] = sins[r]
        d["rc"] = _f32(np.concatenate([rcount(SEQ, np.arange(r * TPC, (r + 1) * TPC)),
                                       rcount(CTX, np.arange(r * CPC, (r + 1) * CPC))], axis=1))
        ohL = np.zeros((128, NCORES), np.float32)
        ohR = np.zeros((128, NCORES), np.float32)
        if r > 0:
            ohL[:, r - 1] = 1.0
        if r < NCORES - 1:
            ohR[:, r + 1] = 1.0
        d["ohL"] = ohL
        d["ohR"] = ohR
        bv = np.stack([b_top if r == 0 else b_int, b_int, b_bot if r == NCORES - 1 else b_int], axis=1)
        d["bias"] = _f32(bv.transpose(0, 3, 1, 2, 4).reshape(NA_H, 128, 3 * 6 * 256))
        maps.append(d)
    if "nc" not in _cache:
        _cache["nc"] = build_fused()
    res = run_bass_kernel_spmd(_cache["nc"], maps, core_ids=list(R))
    out = np.concatenate([res.results[r]["xo"][:, :TPC].T for r in R], axis=0)
    return np.ascontiguousarray(out[None], dtype=np.float32)
```
